# Optimizing a Trainium2 kernel written in Bass

```python
import math
import jax, jax.numpy as jnp
from jax import lax
import numpy as np

D_MODEL = 1024
BATCH = 16
SEQ = 4096
DEPTH = 2
DEC_BATCH = 32
DEC_SEQ = 16
PAST_LEN = 2048

CHUNK = 64
D_SSM = D_MODEL // 2
SSM_CH = 16
SSM_GROUPS = D_SSM // SSM_CH
SSM_STATE = 64
D_ATTN = D_MODEL - D_SSM
HEAD_DIM = 64
N_HEADS = D_ATTN // HEAD_DIM
PAST_CHUNKS = 8
BAND = PAST_CHUNKS + 1
REL_CLIP = 128
N_REL = 2 * REL_CLIP + 1
EPS = 1e-6
NEG_INF = -1e30
DT_MIN = 1e-3
DT_MAX = 1e-1
SPLITS = [D_SSM, 2 * D_SSM, 2 * D_SSM + D_ATTN, 2 * D_SSM + 2 * D_ATTN, 2 * D_SSM + 3 * D_ATTN]
D_IN = 2 * D_SSM + 4 * D_ATTN

kernel_name = "hymba_s5_chunkband_stream_step"


def _rms_norm(x, gain):
    xf = x.astype(jnp.float32)
    y = xf * lax.rsqrt(jnp.mean(xf * xf, axis=-1, keepdims=True) + EPS)
    return (y * gain.astype(jnp.float32)).astype(x.dtype)


def _cmul(xr, xi, yr, yi):
    return xr * yr - xi * yi, xr * yi + xi * yr


def _s5_discretize(a_re, a_im, b_re, b_im, log_dt):
    f32 = jnp.float32
    a_re = a_re.astype(f32)
    a_im = a_im.astype(f32)
    dt = jnp.exp(log_dt.astype(f32))[:, None]
    mag = jnp.exp(dt * a_re)
    ab_re = mag * jnp.cos(dt * a_im)
    ab_im = mag * jnp.sin(dt * a_im)
    den = a_re * a_re + a_im * a_im
    nr = ab_re - 1.0
    f_re = (nr * a_re + ab_im * a_im) / den
    f_im = (ab_im * a_re - nr * a_im) / den
    bb_re, bb_im = _cmul(f_re[..., None], f_im[..., None], b_re.astype(f32), b_im.astype(f32))
    return ab_re, ab_im, bb_re, bb_im


def _scan_combine(left, right):
    ar1, ai1, br1, bi1 = left
    ar2, ai2, br2, bi2 = right
    ar, ai = _cmul(ar2, ai2, ar1, ai1)
    pr, pi = _cmul(ar2, ai2, br1, bi1)
    return ar, ai, pr + br2, pi + bi2


def _s5_branch(u, a_re, a_im, b_re, b_im, c_re, c_im, d, log_dt, w_glu, h0_re, h0_im):
    f32 = jnp.float32
    bsz, length, _ = u.shape
    uf = u.astype(f32).reshape(bsz, length, SSM_GROUPS, SSM_CH)
    ab_re, ab_im, bb_re, bb_im = _s5_discretize(a_re, a_im, b_re, b_im, log_dt)
    bu_re = jnp.einsum('blgh,gph->lbgp', uf, bb_re)
    bu_im = jnp.einsum('blgh,gph->lbgp', uf, bb_im)
    if h0_re is not None:
        cr, ci = _cmul(ab_re, ab_im, h0_re.astype(f32), h0_im.astype(f32))
        bu_re = bu_re.at[0].add(cr)
        bu_im = bu_im.at[0].add(ci)
    a_re_t = jnp.broadcast_to(ab_re, (length, 1, SSM_GROUPS, SSM_STATE))
    a_im_t = jnp.broadcast_to(ab_im, (length, 1, SSM_GROUPS, SSM_STATE))
    _, _, h_re, h_im = lax.associative_scan(_scan_combine, (a_re_t, a_im_t, bu_re, bu_im), axis=0)
    y = (jnp.einsum('lbgp,ghp->blgh', h_re, c_re.astype(f32))
         - jnp.einsum('lbgp,ghp->blgh', h_im, c_im.astype(f32))
         + d.astype(f32) * uf)
    y = jax.nn.gelu(y.reshape(bsz, length, D_SSM)).astype(u.dtype)
    z_val, z_gate = jnp.split(jnp.einsum('ble,ef->blf', y, w_glu), 2, axis=-1)
    out = z_val * jax.nn.sigmoid(z_gate)
    return out, h_re[-1].astype(u.dtype), h_im[-1].astype(u.dtype)


def _rel_bias(rel_table, dist):
    idx = jnp.clip(dist, -REL_CLIP, REL_CLIP) + REL_CLIP
    return rel_table.astype(jnp.float32)[:, idx]


def _band_attention(q, k, v, rel_table):
    bsz, length = q.shape[:2]
    nc = length // CHUNK

    def chunks(t):
        return t.reshape(bsz, nc, CHUNK, N_HEADS, HEAD_DIM)

    pad = ((0, 0), (PAST_CHUNKS, 0), (0, 0), (0, 0), (0, 0))
    kp = jnp.pad(chunks(k), pad)
    vp = jnp.pad(chunks(v), pad)
    kb = jnp.concatenate([kp[:, j:j + nc] for j in range(BAND)], axis=2)
    vb = jnp.concatenate([vp[:, j:j + nc] for j in range(BAND)], axis=2)
    s = jnp.einsum('bnqhd,bnkhd->bnhqk', chunks(q), kb).astype(jnp.float32) * (HEAD_DIM ** -0.5)
    dist = jnp.arange(CHUNK)[:, None] + PAST_CHUNKS * CHUNK - jnp.arange(BAND * CHUNK)[None, :]
    bias = _rel_bias(rel_table, dist)
    valid = (jnp.arange(nc)[:, None] - PAST_CHUNKS + jnp.arange(BAND)[None, :]) >= 0
    valid = jnp.repeat(valid, CHUNK, axis=1)
    s = jnp.where(valid[None, :, None, None, :], s + bias[None, None], NEG_INF)
    p = jax.nn.softmax(s, axis=-1).astype(v.dtype)
    o = jnp.einsum('bnhqk,bnkhd->bnqhd', p, vb)
    return o.reshape(bsz, length, D_ATTN)


def _cached_attention(q, k, v, k_cache, v_cache, rel_table):
    bsz, length = q.shape[:2]
    rows = k_cache.shape[1]
    keys = jnp.concatenate([k_cache.astype(k.dtype), k], axis=1)
    vals = jnp.concatenate([v_cache.astype(v.dtype), v], axis=1)
    s = jnp.einsum('bqhd,bkhd->bhqk', q, keys).astype(jnp.float32) * (HEAD_DIM ** -0.5)
    dist = jnp.arange(length)[:, None] + rows - jnp.arange(rows + length)[None, :]
    s = s + _rel_bias(rel_table, dist)[None]
    p = jax.nn.softmax(s, axis=-1).astype(v.dtype)
    o = jnp.einsum('bhqk,bkhd->bqhd', p, vals)
    return o.reshape(bsz, length, D_ATTN)


def _layer(x, norm_gain, w_in, a_re, a_im, b_re, b_im, c_re, c_im, d, log_dt, w_glu,
           q_gain, k_gain, rel_table, w_out, k_cache, v_cache, h0_re, h0_im):
    bsz, length, _ = x.shape
    h = _rms_norm(x, norm_gain)
    z = jnp.einsum('bld,de->ble', h, w_in)
    u, g_s, q, k, v, g_a = jnp.split(z, SPLITS, axis=-1)
    y_s, hr, hi = _s5_branch(u, a_re, a_im, b_re, b_im, c_re, c_im, d, log_dt, w_glu, h0_re, h0_im)

    def heads(t):
        return t.reshape(bsz, length, N_HEADS, HEAD_DIM)

    q = _rms_norm(heads(q), q_gain)
    k = _rms_norm(heads(k), k_gain)
    v = heads(v)
    if k_cache is None:
        y_a = _band_attention(q, k, v, rel_table)
        rows = min(PAST_CHUNKS * CHUNK, length)
        new_k = k[:, length - rows:]
        new_v = v[:, length - rows:]
    else:
        y_a = _cached_attention(q, k, v, k_cache, v_cache, rel_table)
        new_k = k
        new_v = v
    mixed = jnp.concatenate([y_s * jax.nn.silu(g_s), y_a * jax.nn.silu(g_a)], axis=-1)
    y = x + jnp.einsum('ble,ed->bld', mixed, w_out)
    return y, new_k, new_v, hr, hi


def setup_inputs(seed: int = 0) -> dict:
    key = jax.random.key(seed)
    ks = jax.random.split(key, 24)
    f32 = jnp.float32
    kv_rows = min(PAST_CHUNKS * CHUNK, PAST_LEN)
    nrm = lambda k, shp: jax.random.normal(k, shp, f32)
    a_im_base = math.pi * jnp.arange(SSM_STATE, dtype=f32)
    return {
        "x_prompt": nrm(ks[0], (BATCH, SEQ, D_MODEL)),
        "x_sample": nrm(ks[1], (DEC_BATCH, DEC_SEQ, D_MODEL)),
        "cache_k": nrm(ks[2], (DEPTH, DEC_BATCH, kv_rows, N_HEADS, HEAD_DIM)),
        "cache_v": nrm(ks[3], (DEPTH, DEC_BATCH, kv_rows, N_HEADS, HEAD_DIM)),
        "state_ssm_re": 0.1 * nrm(ks[4], (DEPTH, DEC_BATCH, SSM_GROUPS, SSM_STATE)),
        "state_ssm_im": 0.1 * nrm(ks[5], (DEPTH, DEC_BATCH, SSM_GROUPS, SSM_STATE)),
        "norm_gain": 1.0 + 0.05 * nrm(ks[6], (DEPTH, D_MODEL)),
        "w_in": nrm(ks[7], (DEPTH, D_MODEL, D_IN)) * D_MODEL ** -0.5,
        "ssm_a_re": -0.5 + 0.01 * nrm(ks[8], (DEPTH, SSM_GROUPS, SSM_STATE)),
        "ssm_a_im": a_im_base + 0.01 * nrm(ks[9], (DEPTH, SSM_GROUPS, SSM_STATE)),
        "ssm_b_re": nrm(ks[10], (DEPTH, SSM_GROUPS, SSM_STATE, SSM_CH)) * (2 * SSM_CH) ** -0.5,
        "ssm_b_im": nrm(ks[11], (DEPTH, SSM_GROUPS, SSM_STATE, SSM_CH)) * (2 * SSM_CH) ** -0.5,
        "ssm_c_re": nrm(ks[12], (DEPTH, SSM_GROUPS, SSM_CH, SSM_STATE)) * SSM_STATE ** -0.5,
        "ssm_c_im": nrm(ks[13], (DEPTH, SSM_GROUPS, SSM_CH, SSM_STATE)) * SSM_STATE ** -0.5,
        "ssm_d": nrm(ks[14], (DEPTH, SSM_GROUPS, SSM_CH)),
        "ssm_log_dt": jax.random.uniform(ks[15], (DEPTH, SSM_GROUPS), f32, math.log(DT_MIN), math.log(DT_MAX)),
        "w_glu": nrm(ks[16], (DEPTH, D_SSM, 2 * D_SSM)) * D_SSM ** -0.5,
        "q_norm_gain": 1.0 + 0.05 * nrm(ks[17], (DEPTH, HEAD_DIM)),
        "k_norm_gain": 1.0 + 0.05 * nrm(ks[18], (DEPTH, HEAD_DIM)),
        "rel_bias": 0.1 * nrm(ks[19], (DEPTH, N_HEADS, N_REL)),
        "w_out": nrm(ks[20], (DEPTH, D_MODEL, D_MODEL)) * D_MODEL ** -0.5,
    }


def reference(x_prompt, x_sample, cache_k, cache_v, state_ssm_re, state_ssm_im, norm_gain, w_in,
              ssm_a_re, ssm_a_im, ssm_b_re, ssm_b_im, ssm_c_re, ssm_c_im, ssm_d, ssm_log_dt,
              w_glu, q_norm_gain, k_norm_gain, rel_bias, w_out):
    yp = x_prompt
    ys = x_sample
    pk, pv, pr, pi, sk, sv, sr, si = [], [], [], [], [], [], [], []
    for l in range(DEPTH):
        w = (norm_gain[l], w_in[l], ssm_a_re[l], ssm_a_im[l], ssm_b_re[l], ssm_b_im[l],
             ssm_c_re[l], ssm_c_im[l], ssm_d[l], ssm_log_dt[l], w_glu[l],
             q_norm_gain[l], k_norm_gain[l], rel_bias[l], w_out[l])
        yp, k_p, v_p, r_p, i_p = _layer(yp, *w, None, None, None, None)
        ys, k_s, v_s, r_s, i_s = _layer(ys, *w, cache_k[l], cache_v[l], state_ssm_re[l], state_ssm_im[l])
        pk.append(k_p); pv.append(v_p); pr.append(r_p); pi.append(i_p)
        sk.append(k_s); sv.append(v_s); sr.append(r_s); si.append(i_s)
    return (yp, ys, jnp.stack(pk), jnp.stack(pv), jnp.stack(pr), jnp.stack(pi),
            jnp.stack(sk), jnp.stack(sv), jnp.stack(sr), jnp.stack(si))
```

```python
import math
from contextlib import ExitStack
import numpy as np
import concourse.bass as bass
import concourse.mybir as mybir
from concourse.bass_utils import run_bass_kernel_spmd

F32 = mybir.dt.float32
BF16 = mybir.dt.bfloat16
I32 = mybir.dt.int32
ALU = mybir.AluOpType
AF = mybir.ActivationFunctionType
AX = mybir.AxisListType

L = 2
D = 1024
NCH = 8
DIN = 3072
G = 32
TS = 16
EPS = 1e-6
NEG = -30000.0
TWO_PI = 2.0 * math.pi
STRICT = True

C_J, C_SWN, C_ONES, C_BLK, C_ONE64, C_M4, C_M0, C_JROW, C_SH = 0, 128, 256, 384, 512, 576, 704, 832, 960
NCST = 961


class Buf:
    def __init__(self, name):
        self.name = name
        self.w = None
        self.r = {}
        self.excl = False


class TT:
    def __init__(self, h, shape, dt, name, nbuf=None):
        self.h = h
        self.shape = shape
        self.row = int(np.prod(shape[1:]))
        self.dt = dt
        self.b = Buf(name)

    def __getitem__(self, idx):
        return self.h[idx]

    def ap(self, p0, npart, off, dims):
        return bass.AP(self.h, p0 * self.row + off, [[self.row, npart]] + [list(d) for d in dims])


class Op:
    __slots__ = ("eng", "fn", "deps", "needs_inc", "sem", "target", "is_dma", "epoch", "gen")


class Chan:
    def __init__(self, sem):
        self.sem = sem
        self.count = 0


class Prog:
    ENG = ("pe", "act", "dve", "pool", "sp")

    def __init__(self, nc, stack):
        self.nc = nc
        self.stack = stack
        self.e = {"pe": nc.tensor, "act": nc.scalar, "dve": nc.vector, "pool": nc.gpsimd, "sp": nc.sync}
        self.ops = []
        self.epoch = 0
        self.gen = 0
        self.sems = {}
        self.rank = {}
        self.seen = {}
        self.last = {}
        self.chans = {}
        self.nsem = 0
        self.n_inst = 0

    def _sem(self, name):
        self.nsem += 1
        return self.stack.enter_context(self.nc.semaphore(name))

    def chan(self, name):
        if name not in self.chans:
            self.chans[name] = Chan(self._sem("c_" + name))
        return self.chans[name]

    def _deps(self, eng, reads, writes):
        deps = []
        for b in reads:
            if b.w is not None:
                deps.append(b.w)
            if b.excl:
                for en_, o_ in b.r.items():
                    if en_ != eng:
                        deps.append(o_)
        for b in writes:
            if b.w is not None:
                deps.append(b.w)
            deps.extend(b.r.values())
        out = []
        for d in deps:
            if d.gen != self.gen:
                continue
            if (not d.is_dma) and d.eng == eng and (eng == "pe" or not STRICT):
                continue
            d.needs_inc = True
            out.append(d)
        return out

    def op(self, eng, fn, reads=(), writes=()):
        reads = [x.b if hasattr(x, 'b') else x for x in reads]
        writes = [x.b if hasattr(x, 'b') else x for x in writes]
        o = Op()
        o.eng = eng
        o.fn = fn
        o.is_dma = False
        o.needs_inc = False
        o.sem = None
        o.target = 0
        o.epoch = self.epoch
        o.gen = self.gen
        o.deps = self._deps(eng, reads, writes)
        for b in writes:
            b.w = o
            b.r = {}
        for b in reads:
            b.r[eng] = o
        self.ops.append(o)
        return o

    def dma(self, chan, fn, reads=(), writes=()):
        reads = [x.b if hasattr(x, 'b') else x for x in reads]
        writes = [x.b if hasattr(x, 'b') else x for x in writes]
        ch = self.chan(chan)
        o = Op()
        o.eng = "sp"
        o.fn = fn
        o.is_dma = True
        o.needs_inc = True
        ch.count += 1
        o.sem = ch.sem
        o.target = 16 * ch.count
        o.epoch = self.epoch
        o.gen = self.gen
        o.deps = self._deps("sp", reads, writes)
        for b in writes:
            b.w = o
            b.r = {}
        for b in reads:
            b.r["dma_" + chan] = o
        self.ops.append(o)
        return o

    def emit(self):
        for o in self.ops:
            e = self.e[o.eng]
            need = {}
            for d in o.deps:
                k = id(d.sem)
                if k not in need or need[k][1] < d.target:
                    need[k] = (d.sem, d.target)
            for k, (sem, tgt) in need.items():
                sk = (o.eng, k)
                if self.seen.get(sk, 0) >= tgt:
                    continue
                self.seen[sk] = tgt
                e.wait_ge(sem, tgt)
                self.n_inst += 1
            inst = o.fn()
            self.n_inst += 1
            if o.is_dma:
                inst.then_inc(o.sem, 16)
            else:
                self.last[o.eng] = o
                if o.needs_inc:
                    key = (o.eng, o.epoch)
                    if key not in self.sems:
                        self.sems[key] = self._sem("e_%s_%d" % key)
                        self.rank[key] = 0
                    self.rank[key] += 1
                    o.sem = self.sems[key]
                    o.target = self.rank[key]
                    inst.then_inc(o.sem, 1)
        self.ops = []

    def phase_end(self):
        lastops = {}
        for o in self.ops:
            if not o.is_dma:
                lastops[o.eng] = o
        for o in lastops.values():
            o.needs_inc = True
        self.emit()
        for en in self.ENG:
            e = self.e[en]
            for fn_, o in self.last.items():
                if fn_ == en or o.sem is None:
                    continue
                sk = (en, id(o.sem))
                if self.seen.get(sk, 0) >= o.target:
                    continue
                self.seen[sk] = o.target
                e.wait_ge(o.sem, o.target)
            for ch in self.chans.values():
                if ch.count == 0:
                    continue
                sk = (en, id(ch.sem))
                if self.seen.get(sk, 0) >= 16 * ch.count:
                    continue
                self.seen[sk] = 16 * ch.count
                e.wait_ge(ch.sem, 16 * ch.count)
        self.gen += 1
        self.epoch += 1


class PS:
    def __init__(self, bank, c0, name):
        self.bank = bank
        self.c0 = c0
        self.b = Buf(name)

    def ap(self, p0, n, off, dims):
        return self.bank.ap(p0, n, self.c0 + off, dims)

    def v(self, n, T, off=0, p0=0):
        return self.bank.ap(p0, n, self.c0 + off, [[1, T]])


class Ring:
    def __init__(self, items):
        self.items = items
        self.i = 0

    def next(self):
        x = self.items[self.i % len(self.items)]
        self.i += 1
        return x


def build_program(NT, NSP, NSS):
    nc = bass.Bass("TRN2", target_bir_lowering=False)

    def din(name, shape):
        return nc.dram_tensor(name, list(shape), F32, kind="ExternalInput").ap()

    def dout(name, shape):
        return nc.dram_tensor(name, list(shape), F32, kind="ExternalOutput").ap()

    xp = din("xp", [NSP, NT, 128, NCH, 128])
    xs = din("xs", [NSS, 128, NCH, TS])
    ck = din("ck", [L, NSS, 128, 4, 512])
    cv = din("cv", [L, NSS, 512, 512])
    st0 = din("st0", [L, NSS, 128, G])
    w_in = din("w_in", [L, D, DIN])
    w_glu = din("w_glu", [L, 512, 1024])
    w_out = din("w_out", [L, D, D])
    ng = din("ng", [128, L, NCH])
    s5a = din("s5a", [128, L, 3, G])
    b1h = din("b1h", [L, 128, 1024])
    b2h = din("b2h", [L, 128, 1024])
    l1h = din("l1h", [L, 128, 1024])
    l2h = din("l2h", [L, 128, 1024])
    dcol = din("dcol", [128, L, 4])
    qkg = din("qkg", [128, L, 2])
    qkrow = din("qkrow", [128, L, 128])
    rb = din("rb", [L, 8, 257])
    rbrep = din("rbrep", [128, L, 8 * 257])
    cst = din("cst", [128, NCST])

    yp = dout("yp", [NSP, NT, 128, NCH, 128])
    ys = dout("ys", [NSS, 128, NCH, TS])
    nkp = dout("nkp", [L, NSP, 4, 128, 4, 128])
    nvp = dout("nvp", [L, NSP, 512, 512])
    stp = dout("stp", [L, NSP, 128, G])
    nks = dout("nks", [L, NSS, 128, 4, TS])
    nvs = dout("nvs", [L, NSS, TS, 512])
    sts = dout("sts", [L, NSS, 128, G])
    y1p = nc.dram_tensor("y1p", [NSP, NT, 128, NCH, 128], F32, kind="Internal").ap()
    y1s = nc.dram_tensor("y1s", [NSS, 128, NCH, TS], F32, kind="Internal").ap()
    ext = nc.dram_tensor("ext", [L, 8, 384], F32, kind="Internal").ap()

    with ExitStack() as glob:
        P = Prog(nc, glob)

        uniq = [0]

        def sb(stack, name, shape, dt=F32):
            uniq[0] += 1
            name = "%s_%d" % (name, uniq[0])
            h = stack.enter_context(nc.sbuf_tensor(name, list(shape), dt))
            return TT(h, list(shape), dt, name)

        def ps(stack, name):
            h = stack.enter_context(nc.psum_tensor(name, [128, 512], F32))
            t_ = TT(h, [128, 512], F32, name)
            t_.b.excl = True
            return t_

        Win = sb(glob, "Win", [128, NCH, DIN], BF16)
        Wglu = sb(glob, "Wglu", [128, 4, 1024], BF16)
        Wout = sb(glob, "Wout", [128, NCH, D], BF16)
        PRE1 = sb(glob, "PRE1", [128, G, 128], BF16)
        PRE2 = sb(glob, "PRE2", [128, G, 128], BF16)
        POST1 = sb(glob, "POST1", [128, G, 128], BF16)
        POST2 = sb(glob, "POST2", [128, G, 128], BF16)
        B1 = sb(glob, "B1", [128, 8, 128], BF16)
        B2 = sb(glob, "B2", [128, 8, 128], BF16)
        L1 = sb(glob, "L1", [128, G, 32], BF16)
        L2 = sb(glob, "L2", [128, G, 32], BF16)
        CST = sb(glob, "CST", [128, NCST], F32)
        ONESB = sb(glob, "ONESB", [128, 128], BF16)
        BLKB = sb(glob, "BLKB", [128, 128], BF16)
        ONE64B = sb(glob, "ONE64B", [128, 64], BF16)
        BT = sb(glob, "BT", [128, 16, 128], F32)
        kwin = sb(glob, "kwin", [128, 4, 5, 128], BF16)
        vwin = sb(glob, "vwin", [128, 5, 512], BF16)
        kslot = [Buf("ks%d" % i) for i in range(5)]
        vslot = [Buf("vs%d" % i) for i in range(5)]
        SM = sb(glob, "SM", [128, 40, G], F32)
        SMB = [Buf("sm%d" % i) for i in range(40)]
        COL = sb(glob, "COL", [128, 64], F32)
        COLB = Buf("col")
        NGs = sb(glob, "NGs", [128, L, NCH], F32)
        S5A = sb(glob, "S5A", [128, L, 3, G], F32)
        DCOL = sb(glob, "DCOL", [128, L, 4], F32)
        QKG = sb(glob, "QKG", [128, L, 2], F32)
        banks = [ps(glob, "pb%d" % i) for i in range(8)]
        ybank = banks[0]
        pqr = Ring(banks[1:3])
        scr = Ring(banks[3:5])
        OS = [banks[5], banks[6]]
        GEN = banks[7]
        prot = Ring(banks[1:8])

        (I_ARE, I_AIM, I_LDT, I_DT, I_R, I_TH, I_S1, I_C1, I_ABRE, I_ABIM, I_NR, I_DEN, I_FRE, I_FIM,
         I_CA, I_SA, I_CLP, I_SLP, I_CLS, I_SLS, I_INIT, I_GLAST, I_T0, I_T1, I_T2, I_T3, I_T4, I_T5, I_T6,
         I_T7, I_HL, I_ST0, I_NFIM) = range(33)

        def sm(i, n=128):
            return SM.ap(0, n, i * G, [[1, G]])

        def col(i, n=128, p0=0):
            return COL.ap(p0, n, i, [[1, 1]])

        cJ = CST.ap(0, 128, C_J, [[1, 128]])
        cSWN = CST.ap(0, 128, C_SWN, [[1, 128]])
        cM4 = lambda nk, T: CST.ap(0, nk, C_M4, [[1, T]])
        cM0 = lambda nk, T: CST.ap(0, nk, C_M0, [[1, T]])
        cSH = CST.ap(0, 128, C_SH, [[1, 1]])

        V, A, Pl, PE = nc.vector, nc.scalar, nc.gpsimd, nc.tensor

        P.dma("cst", lambda: nc.sync.dma_start(out=CST[:], in_=cst[:]), writes=[CST])
        P.dma("cst2", lambda: nc.sync.dma_start(out=NGs[:], in_=ng[:]), writes=[NGs])
        P.dma("cst3", lambda: nc.sync.dma_start(out=S5A[:], in_=s5a[:]), writes=[S5A])
        P.dma("cst4", lambda: nc.sync.dma_start(out=DCOL[:], in_=dcol[:]), writes=[DCOL])
        P.dma("cst5", lambda: nc.sync.dma_start(out=QKG[:], in_=qkg[:]), writes=[QKG])
        P.op("dve", lambda: V.tensor_copy(ONESB[:], CST.ap(0, 128, C_ONES, [[1, 128]])), reads=[CST], writes=[ONESB])
        P.op("dve", lambda: V.tensor_copy(BLKB[:], CST.ap(0, 128, C_BLK, [[1, 128]])), reads=[CST], writes=[BLKB])
        P.op("dve", lambda: V.tensor_copy(ONE64B[:], CST.ap(0, 128, C_ONE64, [[1, 64]])), reads=[CST], writes=[ONE64B])
        P.phase_end()

        for l in range(L):
            with ExitStack() as s1:
                stg = [sb(s1, "stg%d" % i, [128, 1536], F32) for i in range(2)]
                tmpN = 256
                tb = [sb(s1, "tb%d" % i, [128, tmpN], F32) for i in range(8)]
                tbi = sb(s1, "tbi", [128, tmpN], I32)
                HK = sb(s1, "HK", [128, 16, 128], F32)
                RBR = sb(s1, "RBR", [128, 8 * 257], F32)
                QKR = sb(s1, "QKR", [128, 128], F32)
                EXS = sb(s1, "EXS", [8, 384], F32)

                wi = 0
                for kc2 in range(2 * NCH):
                    kc, hf = kc2 // 2, kc2 % 2
                    st = stg[wi % 2]
                    P.dma("w%d" % (wi % 2), (lambda st=st, kc=kc, hf=hf: nc.sync.dma_start(out=st[:], in_=w_in[l, kc * 128:(kc + 1) * 128, hf * 1536:(hf + 1) * 1536])), writes=[st])
                    gcol = NGs.ap(0, 128, l * NCH + kc, [[1, 1]])
                    if kc2 % 2 == 0:
                        P.op("act", (lambda st=st, kc=kc, hf=hf, gcol=gcol: A.activation(out=Win[:, kc, hf * 1536:(hf + 1) * 1536], in_=st[:], func=AF.Identity, scale=gcol)), reads=[st, NGs], writes=[Win])
                    else:
                        P.op("dve", (lambda st=st, kc=kc, hf=hf, gcol=gcol: V.tensor_scalar(Win[:, kc, hf * 1536:(hf + 1) * 1536], st[:], gcol, None, ALU.mult)), reads=[st, NGs], writes=[Win])
                    wi += 1
                for kc in range(4):
                    st = stg[wi % 2]
                    P.dma("w%d" % (wi % 2), (lambda st=st, kc=kc: nc.sync.dma_start(out=st[:, 0:1024], in_=w_glu[l, kc * 128:(kc + 1) * 128, :])), writes=[st])
                    P.op("pool", (lambda st=st, kc=kc: Pl.tensor_copy(Wglu[:, kc, :], st[:, 0:1024])), reads=[st], writes=[Wglu])
                    wi += 1
                for kc in range(NCH):
                    st = stg[wi % 2]
                    P.dma("w%d" % (wi % 2), (lambda st=st, kc=kc: nc.sync.dma_start(out=st[:, 0:1024], in_=w_out[l, kc * 128:(kc + 1) * 128, :])), writes=[st])
                    eng = ("act", "dve", "pool")[kc % 3]
                    if eng == "act":
                        P.op("act", (lambda st=st, kc=kc: A.activation(func=AF.Copy, out=Wout[:, kc, :], in_=st[:, 0:1024])), reads=[st], writes=[Wout])
                    elif eng == "dve":
                        P.op("dve", (lambda st=st, kc=kc: V.tensor_copy(Wout[:, kc, :], st[:, 0:1024])), reads=[st], writes=[Wout])
                    else:
                        P.op("pool", (lambda st=st, kc=kc: Pl.tensor_copy(Wout[:, kc, :], st[:, 0:1024])), reads=[st], writes=[Wout])
                    wi += 1
                for src, dst in ((b1h, B1), (b2h, B2), (l1h, L1), (l2h, L2)):
                    st = stg[wi % 2]
                    P.dma("w%d" % (wi % 2), (lambda st=st, src=src: nc.sync.dma_start(out=st[:, 0:1024], in_=src[l])), writes=[st])
                    P.op("dve", (lambda st=st, dst=dst: V.tensor_copy(dst.ap(0, 128, 0, [[1, 1024]]), st[:, 0:1024])), reads=[st], writes=[dst])
                    wi += 1

                def sa(i):
                    return S5A.ap(0, 128, (l * 3 + i) * G, [[1, G]])

                def sincos(ang, n, out_sin, out_cos, rd, wr):
                    t, kf, d, m = (tb[4].ap(0, 128, 0, [[1, n]]), tb[5].ap(0, 128, 0, [[1, n]]),
                                   tb[6].ap(0, 128, 0, [[1, n]]), tb[7].ap(0, 128, 0, [[1, n]]))
                    ki = tbi.ap(0, 128, 0, [[1, n]])
                    for off, outp in ((0.0, out_sin), (0.25, out_cos)):
                        P.op("dve", (lambda off=off: V.tensor_scalar(t, ang, 1.0 / TWO_PI, off, ALU.mult, ALU.add)), reads=rd + [tb[4]], writes=[tb[4]])
                        P.op("dve", lambda: V.tensor_copy(ki, t), reads=[tb[4]], writes=[tbi])
                        P.op("dve", lambda: V.tensor_copy(kf, ki), reads=[tbi], writes=[tb[5]])
                        P.op("dve", lambda: V.tensor_tensor(d, t, kf, ALU.subtract), reads=[tb[4], tb[5]], writes=[tb[6]])
                        P.op("dve", lambda: V.tensor_scalar(m, d, 0.5, None, ALU.is_gt), reads=[tb[6]], writes=[tb[7]])
                        P.op("dve", lambda: V.tensor_tensor(d, d, m, ALU.subtract), reads=[tb[6], tb[7]], writes=[tb[6]])
                        P.op("dve", lambda: V.tensor_scalar(m, d, -0.5, None, ALU.is_lt), reads=[tb[6]], writes=[tb[7]])
                        P.op("dve", lambda: V.tensor_tensor(d, d, m, ALU.add), reads=[tb[6], tb[7]], writes=[tb[6]])
                        P.op("act", (lambda outp=outp: A.activation(out=outp, in_=d, func=AF.Sin, scale=TWO_PI * (1.0 - 2e-6))), reads=[tb[6]], writes=wr)

                smb = lambda *ids: [SMB[i] for i in ids]
                P.op("act", lambda: A.activation(out=sm(I_DT), in_=sa(2), func=AF.Exp), reads=[S5A], writes=smb(I_DT))
                P.op("dve", lambda: V.tensor_tensor(sm(I_T0), sm(I_DT), sa(0), ALU.mult), reads=[S5A] + smb(I_DT), writes=smb(I_T0))
                P.op("act", lambda: A.activation(out=sm(I_R), in_=sm(I_T0), func=AF.Exp), reads=smb(I_T0), writes=smb(I_R))
                P.op("dve", lambda: V.tensor_tensor(sm(I_TH), sm(I_DT), sa(1), ALU.mult), reads=[S5A] + smb(I_DT), writes=smb(I_TH))
                sincos(sm(I_TH), G, sm(I_S1), sm(I_C1), smb(I_TH), smb(I_S1, I_C1))
                for mult_, (ci, si) in ((128.0, (I_CA, I_SA)), (127.0, (I_CLP, I_SLP)), (float(TS - 1), (I_CLS, I_SLS))):
                    P.op("dve", (lambda mult_=mult_: V.tensor_scalar(sm(I_T1), sm(I_TH), mult_, None, ALU.mult)), reads=smb(I_TH), writes=smb(I_T1))
                    sincos(sm(I_T1), G, sm(si), sm(ci), smb(I_T1), smb(si, ci))
                P.op("dve", lambda: V.tensor_tensor(sm(I_ABRE), sm(I_R), sm(I_C1), ALU.mult), reads=smb(I_R, I_C1), writes=smb(I_ABRE))
                P.op("dve", lambda: V.tensor_tensor(sm(I_ABIM), sm(I_R), sm(I_S1), ALU.mult), reads=smb(I_R, I_S1), writes=smb(I_ABIM))
                P.op("dve", lambda: V.tensor_scalar(sm(I_NR), sm(I_ABRE), -1.0, None, ALU.add), reads=smb(I_ABRE), writes=smb(I_NR))
                P.op("dve", lambda: V.tensor_tensor(sm(I_T0), sa(0), sa(0), ALU.mult), reads=[S5A], writes=smb(I_T0))
                P.op("dve", lambda: V.tensor_tensor(sm(I_T1), sa(1), sa(1), ALU.mult), reads=[S5A], writes=smb(I_T1))
                P.op("dve", lambda: V.tensor_tensor(sm(I_DEN), sm(I_T0), sm(I_T1), ALU.add), reads=smb(I_T0, I_T1), writes=smb(I_DEN))
                P.op("dve", lambda: V.reciprocal(sm(I_DEN), sm(I_DEN)), reads=smb(I_DEN), writes=smb(I_DEN))
                P.op("dve", lambda: V.tensor_tensor(sm(I_T0), sm(I_NR), sa(0), ALU.mult), reads=[S5A] + smb(I_NR), writes=smb(I_T0))
                P.op("dve", lambda: V.tensor_tensor(sm(I_T1), sm(I_ABIM), sa(1), ALU.mult), reads=[S5A] + smb(I_ABIM), writes=smb(I_T1))
                P.op("dve", lambda: V.tensor_tensor(sm(I_T0), sm(I_T0), sm(I_T1), ALU.add), reads=smb(I_T0, I_T1), writes=smb(I_T0))
                P.op("dve", lambda: V.tensor_tensor(sm(I_FRE), sm(I_T0), sm(I_DEN), ALU.mult), reads=smb(I_T0, I_DEN), writes=smb(I_FRE))
                P.op("dve", lambda: V.tensor_tensor(sm(I_T2), sm(I_ABIM), sa(0), ALU.mult), reads=[S5A] + smb(I_ABIM), writes=smb(I_T2))
                P.op("dve", lambda: V.tensor_tensor(sm(I_T3), sm(I_NR), sa(1), ALU.mult), reads=[S5A] + smb(I_NR), writes=smb(I_T3))
                P.op("dve", lambda: V.tensor_tensor(sm(I_T2), sm(I_T2), sm(I_T3), ALU.subtract), reads=smb(I_T2, I_T3), writes=smb(I_T2))
                P.op("dve", lambda: V.tensor_tensor(sm(I_FIM), sm(I_T2), sm(I_DEN), ALU.mult), reads=smb(I_T2, I_DEN), writes=smb(I_FIM))
                P.op("dve", lambda: V.tensor_scalar(sm(I_NFIM), sm(I_FIM), -1.0, None, ALU.mult), reads=smb(I_FIM), writes=smb(I_NFIM))

                NB = 2
                for gb in range(G // NB):
                    ang = tb[0].ap(0, 128, 0, [[1, NB * 128]])
                    sinb = tb[1].ap(0, 128, 0, [[1, NB * 128]])
                    cosb = tb[2].ap(0, 128, 0, [[1, NB * 128]])
                    a3 = tb[3].ap(0, 128, 0, [[128, NB], [1, 128]])
                    ang3 = tb[0].ap(0, 128, 0, [[128, NB], [1, 128]])
                    sin3 = tb[1].ap(0, 128, 0, [[128, NB], [1, 128]])
                    cos3 = tb[2].ap(0, 128, 0, [[128, NB], [1, 128]])
                    thb = SM.ap(0, 128, I_TH * G + gb * NB, [[1, NB], [0, 128]])
                    freb = SM.ap(0, 128, I_FRE * G + gb * NB, [[1, NB], [0, 128]])
                    fimb = SM.ap(0, 128, I_FIM * G + gb * NB, [[1, NB], [0, 128]])
                    nfimb = SM.ap(0, 128, I_NFIM * G + gb * NB, [[1, NB], [0, 128]])
                    jb = CST.ap(0, 128, C_JROW, [[0, NB], [1, 128]])
                    dst = lambda Tn: Tn.ap(0, 128, gb * NB * 128, [[128, NB], [1, 128]])
                    P.op("dve", (lambda ang3=ang3, thb=thb, jb=jb: V.tensor_tensor(ang3, thb, jb, ALU.mult)), reads=[CST] + smb(I_TH), writes=[tb[0]])
                    sincos(ang, NB * 128, sinb, cosb, [tb[0]], [tb[1], tb[2]])
                    P.op("dve", (lambda a3=a3, cos3=cos3, freb=freb: V.tensor_tensor(a3, cos3, freb, ALU.mult)), reads=[tb[2]] + smb(I_FRE), writes=[tb[3]])
                    P.op("dve", (lambda ang3=ang3, sin3=sin3, fimb=fimb: V.tensor_tensor(ang3, sin3, fimb, ALU.mult)), reads=[tb[1]] + smb(I_FIM), writes=[tb[0]])
                    P.op("dve", (lambda a3=a3, ang3=ang3, d_=dst(PRE1): V.tensor_tensor(d_, a3, ang3, ALU.add)), reads=[tb[3], tb[0]], writes=[PRE1])
                    P.op("dve", (lambda a3=a3, sin3=sin3, freb=freb: V.tensor_tensor(a3, sin3, freb, ALU.mult)), reads=[tb[1]] + smb(I_FRE), writes=[tb[3]])
                    P.op("dve", (lambda ang3=ang3, cos3=cos3, nfimb=nfimb: V.tensor_tensor(ang3, cos3, nfimb, ALU.mult)), reads=[tb[2]] + smb(I_NFIM), writes=[tb[0]])
                    P.op("dve", (lambda a3=a3, ang3=ang3: V.tensor_tensor(a3, a3, ang3, ALU.add)), reads=[tb[3], tb[0]], writes=[tb[3]])
                    P.op("dve", (lambda a3=a3, d_=dst(PRE2): V.tensor_scalar(d_, a3, cSH, None, ALU.mult)), reads=[tb[3], CST], writes=[PRE2])
                    P.op("dve", (lambda cos3=cos3, d_=dst(POST1): V.tensor_scalar(d_, cos3, cSH, None, ALU.mult)), reads=[tb[2], CST], writes=[POST1])
                    P.op("dve", (lambda sin3=sin3, d_=dst(POST2): V.tensor_scalar(d_, sin3, -1.0, None, ALU.mult)), reads=[tb[1]], writes=[POST2])

                P.dma("rbr", lambda: nc.sync.dma_start(out=RBR[:], in_=rbrep[:, l, :]), writes=[RBR])
                P.dma("qkr", lambda: nc.sync.dma_start(out=QKR[:], in_=qkrow[:, l, :]), writes=[QKR])
                P.dma("exs", lambda: nc.sync.dma_start(out=EXS[0:8, 0:257], in_=rb[l]), writes=[EXS])
                P.op("dve", lambda: V.tensor_copy(EXS[0:8, 257:384], EXS[0:8, 256:257].to_broadcast([8, 127])), reads=[EXS], writes=[EXS])
                P.dma("exd", lambda: nc.sync.dma_start(out=ext[l], in_=EXS[0:8, :]), reads=[EXS], writes=[HK])
                for i in range(16):
                    h = i % 8
                    off = 1 if i < 8 else 129
                    src = bass.AP(ext.tensor, (l * 8 + h) * 384 + off, [[1, 128], [1, 128]])
                    P.dma("hk", (lambda i=i, src=src: nc.sync.dma_start(out=HK[:, i, :], in_=src)), reads=[HK], writes=[HK])
                P.op("act", lambda: A.activation(out=RBR[:], in_=RBR[:], func=AF.Abs), reads=[RBR], writes=[RBR])
                P.op("dve", lambda: V.reduce_max(col(4), RBR[:], AX.X), reads=[RBR], writes=[COLB])
                P.op("act", lambda: A.activation(out=QKR[:], in_=QKR[:], func=AF.Abs), reads=[QKR], writes=[QKR])
                P.op("dve", lambda: V.reduce_max(col(2), QKR[:, 0:64], AX.X), reads=[QKR], writes=[COLB])
                P.op("dve", lambda: V.reduce_max(col(3), QKR[:, 64:128], AX.X), reads=[QKR], writes=[COLB])
                P.op("dve", lambda: V.tensor_tensor(col(5), col(2), col(3), ALU.mult), reads=[COLB], writes=[COLB])
                P.op("dve", lambda: V.tensor_scalar(col(5), col(5), 8.0, col(4), ALU.mult, ALU.add), reads=[COLB], writes=[COLB])
                P.op("dve", lambda: V.tensor_scalar(col(6), col(5), -1.0, None, ALU.mult), reads=[COLB], writes=[COLB])
                P.dma("rbr", lambda: nc.sync.dma_start(out=RBR[:], in_=rbrep[:, l, :]), reads=[RBR], writes=[RBR])
                P.op("dve", lambda: V.tensor_scalar(COL.ap(0, 128, 8, [[1, 8]]), RBR.ap(0, 128, 256, [[257, 8]]), col(5), None, ALU.subtract), reads=[RBR, COLB], writes=[COLB])
                P.op("dve", lambda: V.tensor_scalar(col(0), QKG.ap(0, 128, l * 2, [[1, 1]]), 0.125, None, ALU.mult), reads=[QKG], writes=[COLB])
                P.op("dve", lambda: V.tensor_copy(col(1), QKG.ap(0, 128, l * 2 + 1, [[1, 1]])), reads=[QKG], writes=[COLB])
                for i in range(16):
                    bk = prot.next()
                    P.op("pe", (lambda i=i, bk=bk: PE.matmul(bk[:, 0:128], cJ, HK[:, i, :], start=True, stop=True)), reads=[CST, HK], writes=[bk])
                    if i < 8:
                        P.op("dve", (lambda i=i, bk=bk: V.tensor_tensor(BT[:, i, :], bk[:, 0:128], cM4(128, 128), ALU.add)), reads=[bk, CST], writes=[BT])
                    else:
                        P.op("dve", (lambda i=i, bk=bk: V.tensor_copy(BT[:, i, :], bk[:, 0:128])), reads=[bk], writes=[BT])
                P.phase_end()

            with ExitStack() as s2:
                xT = [sb(s2, "xT%d" % i, [128, NCH, 128], F32) for i in range(2)]
                sq = sb(s2, "sq", [128, NCH, 128], BF16)
                hT = sb(s2, "hT", [128, NCH, 128], BF16)
                uT = sb(s2, "uT", [128, 4, 128], BF16)
                sgs = sb(s2, "sgs", [128, 4, 128], F32)
                sga = sb(s2, "sga", [128, 4, 128], F32)
                qT = sb(s2, "qT", [128, 4, 128], BF16)
                qsq = sb(s2, "qsq", [128, 4, 128], BF16)
                rstd = sb(s2, "rstd", [128, 128], F32)
                rq = sb(s2, "rq", [128, 4, 128], F32)
                kn32 = sb(s2, "kn32", [128, 4, 128], F32)
                v32 = sb(s2, "v32", [128, 512], F32)
                t1r = Ring([sb(s2, "t1_%d" % i, [128, 128], F32) for i in range(2)])
                t2r = Ring([sb(s2, "t2_%d" % i, [128, 128], F32) for i in range(2)])
                btr = Ring([sb(s2, "bt_%d" % i, [128, 128], F32) for i in range(2)])
                Gr = Ring([sb(s2, "G_%d" % i, [128, 128], F32) for i in range(2)])
                W1r = Ring([sb(s2, "W1_%d" % i, [128, 128], BF16) for i in range(2)])
                W2r = Ring([sb(s2, "W2_%d" % i, [128, 128], BF16) for i in range(2)])
                ypre = sb(s2, "ypre", [128, 4, 128], F32)
                x2 = sb(s2, "x2", [128, 4, 128], F32)
                tt = sb(s2, "tt", [128, 4, 128], F32)
                yg = sb(s2, "yg", [128, 4, 128], BF16)
                sg = sb(s2, "sg", [128, 4, 128], F32)
                mixT = sb(s2, "mixT", [128, NCH, 128], BF16)
                mixb = [Buf("mix%d" % i) for i in range(NCH)]
                stmpr = Ring([sb(s2, "stmp%d" % i, [128, 128], F32) for i in range(3)])
                pTr = Ring([sb(s2, "pT%d" % i, [128, 128], BF16) for i in range(4)])
                rsr = Ring([sb(s2, "rs%d" % i, [128, 128], F32) for i in range(2)])
                cstg = sb(s2, "cstg", [128, 2, 512], F32)

                def v3(t, T, nchunk, c0=0, n=128, p0=0):
                    return t.ap(p0, n, c0 * 128, [[128, nchunk], [1, T]])

                def pv3(bk, T, nchunk=4):
                    return bk.ap(0, 128, 0, [[128, nchunk], [1, T]])

                def rotate(dst_i, src_ap, src_bufs, ci, si, bk=None):
                    bk = bk if bk is not None else prot.next()
                    P.op("pe", lambda: PE.matmul(bk[:, 0:G], cSWN, src_ap, start=True, stop=True), reads=[CST] + src_bufs, writes=[bk])
                    P.op("dve", lambda: V.tensor_tensor(sm(I_T6), sm(ci), src_ap, ALU.mult), reads=smb(ci) + src_bufs, writes=smb(I_T6))
                    P.op("dve", lambda: V.tensor_tensor(sm(I_T7), sm(si), bk[:, 0:G], ALU.mult), reads=smb(si) + [bk], writes=smb(I_T7))
                    P.op("dve", lambda: V.tensor_tensor(sm(dst_i), sm(I_T6), sm(I_T7), ALU.add), reads=smb(I_T6, I_T7), writes=smb(dst_i))

                def load_x(src_ap, xb, T, bi):
                    P.dma("x%d" % bi, lambda: nc.sync.dma_start(out=v3(xb, T, NCH), in_=src_ap), writes=[xb])

                def do_tile(xb, bi, T, ti, dst_ap, outs):
                    slot = ti % 5
                    jl = [j for j in range(5) if ti - 4 + j >= 0]
                    P.op("act", lambda: A.activation(out=v3(sq, T, NCH), in_=v3(xb, T, NCH), func=AF.Square), reads=[xb], writes=[sq])
                    bk = prot.next()
                    for c in range(NCH):
                        P.op("pe", (lambda c=c, bk=bk: PE.matmul(bk[:, 0:T], ONESB[:], sq[:, c, 0:T], start=(c == 0), stop=(c == NCH - 1))), reads=[ONESB, sq], writes=[bk])
                    P.op("act", (lambda bk=bk: A.activation(out=rstd[:, 0:T], in_=bk[:, 0:T], func=AF.Sqrt, bias=col(7), scale=1.0)), reads=[bk, COLB], writes=[rstd])
                    P.op("dve", lambda: V.reciprocal(rstd[:, 0:T], rstd[:, 0:T]), reads=[rstd], writes=[rstd])
                    P.op("dve", lambda: V.tensor_tensor(v3(hT, T, NCH), v3(xb, T, NCH), rstd.ap(0, 128, 0, [[0, NCH], [1, T]]), ALU.mult), reads=[xb, rstd], writes=[hT])

                    def win_group(cb):
                        bk = prot.next()
                        for c in range(4):
                            for kc in range(NCH):
                                P.op("pe", (lambda c=c, kc=kc, bk=bk: PE.matmul(bk[:, c * 128:c * 128 + T], Win[:, kc, cb + c * 128:cb + (c + 1) * 128], hT[:, kc, 0:T], start=(kc == 0), stop=(kc == NCH - 1))), reads=[Win, hT], writes=[bk])
                        return bk

                    bk = win_group(0)
                    P.op("act", (lambda bk=bk: A.activation(func=AF.Copy, out=v3(uT, T, 4), in_=pv3(bk, T))), reads=[bk], writes=[uT])
                    bk = win_group(512)
                    P.op("act", (lambda bk=bk: A.activation(out=v3(sgs, T, 4), in_=pv3(bk, T), func=AF.Silu)), reads=[bk], writes=[sgs])
                    bq = win_group(1024)
                    P.op("act", (lambda bq=bq: A.activation(out=v3(qsq, T, 4), in_=pv3(bq, T), func=AF.Square)), reads=[bq], writes=[qsq])
                    bk2 = prot.next()
                    for c in range(4):
                        P.op("pe", (lambda c=c, bk2=bk2: PE.matmul(bk2[:, c * 128:c * 128 + T], BLKB[:], qsq[:, c, 0:T], start=True, stop=True)), reads=[BLKB, qsq], writes=[bk2])
                    P.op("act", (lambda bk2=bk2: A.activation(out=v3(rq, T, 4), in_=pv3(bk2, T), func=AF.Sqrt, bias=col(7), scale=1.0)), reads=[bk2, COLB], writes=[rq])
                    P.op("dve", lambda: V.reciprocal(v3(rq, T, 4), v3(rq, T, 4)), reads=[rq], writes=[rq])
                    P.op("dve", (lambda bq=bq: V.scalar_tensor_tensor(v3(qT, T, 4), pv3(bq, T), col(0), v3(rq, T, 4), ALU.mult, ALU.mult)), reads=[bq, rq, COLB], writes=[qT])
                    bkk = win_group(1536)
                    P.op("act", (lambda bkk=bkk: A.activation(out=v3(qsq, T, 4), in_=pv3(bkk, T), func=AF.Square)), reads=[bkk], writes=[qsq])
                    bk3 = prot.next()
                    for c in range(4):
                        P.op("pe", (lambda c=c, bk3=bk3: PE.matmul(bk3[:, c * 128:c * 128 + T], BLKB[:], qsq[:, c, 0:T], start=True, stop=True)), reads=[BLKB, qsq], writes=[bk3])
                    P.op("act", (lambda bk3=bk3: A.activation(out=v3(rq, T, 4), in_=pv3(bk3, T), func=AF.Sqrt, bias=col(7), scale=1.0)), reads=[bk3, COLB], writes=[rq])
                    P.op("dve", lambda: V.reciprocal(v3(rq, T, 4), v3(rq, T, 4)), reads=[rq], writes=[rq])
                    P.op("dve", (lambda bkk=bkk: V.scalar_tensor_tensor(v3(kn32, T, 4), pv3(bkk, T), col(1), v3(rq, T, 4), ALU.mult, ALU.mult)), reads=[bkk, rq, COLB], writes=[kn32])
                    P.op("pool", lambda: Pl.tensor_copy(kwin.ap(0, 128, slot * 128, [[640, 4], [1, T]]), v3(kn32, T, 4)), reads=[kn32], writes=[kslot[slot]])
                    if outs.get("nk") is not None:
                        P.dma("nk", lambda: nc.sync.dma_start(out=outs["nk"], in_=v3(kn32, T, 4)), reads=[kn32])
                    bv = prot.next()
                    for kc in range(NCH):
                        P.op("pe", (lambda kc=kc, bv=bv: PE.matmul(bv[0:T, :], hT[:, kc, 0:T], Win[:, kc, 2048:2560], start=(kc == 0), stop=(kc == NCH - 1))), reads=[Win, hT], writes=[bv])
                    P.op("act", (lambda bv=bv: A.activation(func=AF.Copy, out=vwin[0:T, slot, :], in_=bv[0:T, :])), reads=[bv], writes=[vslot[slot]])
                    if outs.get("nv") is not None:
                        P.op("dve", (lambda bv=bv: V.tensor_copy(v32[0:T, :], bv[0:T, :])), reads=[bv], writes=[v32])
                        P.dma("nv", lambda: nc.sync.dma_start(out=outs["nv"], in_=v32[0:T, :]), reads=[v32])
                    bk = win_group(2560)
                    P.op("act", (lambda bk=bk: A.activation(out=v3(sga, T, 4), in_=pv3(bk, T), func=AF.Silu)), reads=[bk], writes=[sga])

                    def s5_stream():
                        pend = []

                        def front(g):
                            c, band, e = g // 8, (g % 8) // 2, g % 2
                            r0 = 32 * band
                            bkp = pqr.next()
                            P.op("pe", lambda: PE.matmul(bkp[:, 0:T], B1.ap(r0, 32, (c * 2 + e) * 128, [[1, 128]]), uT.ap(r0, 32, c * 128, [[1, T]]), start=True, stop=True, tile_position=(r0, 0)), reads=[B1, uT], writes=[bkp])
                            P.op("pe", lambda: PE.matmul(bkp[:, 128:128 + T], B2.ap(r0, 32, (c * 2 + e) * 128, [[1, 128]]), uT.ap(r0, 32, c * 128, [[1, T]]), start=True, stop=True, tile_position=(r0, 0)), reads=[B2, uT], writes=[bkp])
                            return bkp

                        def mid(g, bkp):
                            t1, t2, bt_, Gt, W1, W2 = t1r.next(), t2r.next(), btr.next(), Gr.next(), W1r.next(), W2r.next()
                            P.op("dve", lambda: V.tensor_tensor(t1[:, 0:T], bkp[:, 0:T], PRE1[:, g, 0:T], ALU.mult), reads=[bkp, PRE1], writes=[t1])
                            P.op("dve", lambda: V.tensor_tensor(t2[:, 0:T], bkp[:, 128:128 + T], PRE2[:, g, 0:T], ALU.mult), reads=[bkp, PRE2], writes=[t2])
                            P.op("pool", lambda: Pl.tensor_tensor(bt_[:, 0:T], t1[:, 0:T], t2[:, 0:T], ALU.add), reads=[t1, t2], writes=[bt_])
                            P.op("dve", lambda: V.tensor_tensor_scan(Gt[:, 0:T], SM.ap(0, 128, I_R * G + g, [[0, T]]), bt_[:, 0:T], SM.ap(0, 128, I_INIT * G + g, [[1, 1]]), ALU.mult, ALU.add), reads=[bt_] + smb(I_R, I_INIT), writes=[Gt])
                            P.op("pool", lambda: Pl.tensor_tensor(W1[:, 0:T], Gt[:, 0:T], POST1[:, g, 0:T], ALU.mult), reads=[Gt, POST1], writes=[W1])
                            P.op("pool", lambda: Pl.tensor_tensor(W2[:, 0:T], Gt[:, 0:T], POST2[:, g, 0:T], ALU.mult), reads=[Gt, POST2], writes=[W2])
                            P.op("act", lambda: A.activation(func=AF.Copy, out=SM.ap(0, 128, I_GLAST * G + g, [[1, 1]]), in_=Gt[:, T - 1:T]), reads=[Gt], writes=smb(I_GLAST))
                            return W1, W2

                        def back(g, W1, W2):
                            c, band, e = g // 8, (g % 8) // 2, g % 2
                            r0 = 32 * band
                            o_ = ybank.ap(r0, 32, c * 128, [[1, T]])
                            P.op("pe", lambda: PE.matmul(o_, L1[:, g, :], W1[:, 0:T], start=(e == 0), stop=False, tile_position=(0, r0)), reads=[L1, W1], writes=[ybank])
                            P.op("pe", lambda: PE.matmul(o_, L2[:, g, :], W2[:, 0:T], start=False, stop=(e == 1), tile_position=(0, r0)), reads=[L2, W2], writes=[ybank])

                        fr = {}
                        fr[0] = front(0)
                        for g in range(G):
                            if g + 1 < G:
                                fr[g + 1] = front(g + 1)
                            W1, W2 = mid(g, fr.pop(g))
                            pend.append((g, W1, W2))
                            if len(pend) > 1:
                                back(*pend.pop(0))
                            yield
                        while pend:
                            back(*pend.pop(0))
                        for c in range(4):
                            P.op("dve", (lambda c=c: V.scalar_tensor_tensor(ypre[:, c, 0:T], uT[:, c, 0:T], DCOL.ap(0, 128, l * 4 + c, [[1, 1]]), ybank[:, c * 128:c * 128 + T], ALU.mult, ALU.add)), reads=[uT, DCOL, ybank], writes=[ypre])
                        P.op("act", lambda: A.activation(out=v3(x2, T, 4), in_=v3(ypre, T, 4), func=AF.Square), reads=[ypre], writes=[x2])
                        P.op("dve", lambda: V.tensor_scalar(v3(tt, T, 4), v3(x2, T, 4), 0.044715, 1.0, ALU.mult, ALU.add), reads=[x2], writes=[tt])
                        P.op("dve", lambda: V.tensor_tensor(v3(tt, T, 4), v3(tt, T, 4), v3(ypre, T, 4), ALU.mult), reads=[tt, ypre], writes=[tt])
                        P.op("act", lambda: A.activation(out=v3(x2, T, 4), in_=v3(tt, T, 4), func=AF.Sigmoid, scale=2.0 * math.sqrt(2.0 / math.pi)), reads=[tt], writes=[x2])
                        P.op("dve", lambda: V.tensor_tensor(v3(yg, T, 4), v3(x2, T, 4), v3(ypre, T, 4), ALU.mult), reads=[x2, ypre], writes=[yg])
                        yield
                        if outs.get("st") is not None:
                            ci, si = outs["st_cs"]
                            rotate(I_HL, sm(I_GLAST), smb(I_GLAST), ci, si, GEN)
                            P.dma("st", lambda: nc.sync.dma_start(out=outs["st"], in_=sm(I_HL)), reads=smb(I_HL))
                        else:
                            rotate(I_INIT, sm(I_GLAST), smb(I_GLAST), I_CA, I_SA, GEN)
                        bva, bga = GEN, ybank
                        for oc in range(8):
                            bk_ = bva if oc < 4 else bga
                            for kc in range(4):
                                P.op("pe", (lambda oc=oc, kc=kc, bk_=bk_: PE.matmul(bk_[:, (oc % 4) * 128:(oc % 4) * 128 + T], Wglu[:, kc, oc * 128:(oc + 1) * 128], yg[:, kc, 0:T], start=(kc == 0), stop=(kc == 3))), reads=[Wglu, yg], writes=[bk_])
                            if oc % 4 == 3:
                                yield
                        P.op("act", lambda: A.activation(out=v3(sg, T, 4), in_=pv3(bga, T), func=AF.Sigmoid), reads=[bga], writes=[sg])
                        P.op("pool", lambda: Pl.tensor_tensor(v3(sg, T, 4), v3(sg, T, 4), v3(sgs, T, 4), ALU.mult), reads=[sg, sgs], writes=[sg])
                        P.op("dve", lambda: V.tensor_tensor(v3(mixT, T, 4), pv3(bva, T), v3(sg, T, 4), ALU.mult), reads=[bva, sg], writes=mixb[0:4])

                    def attn_stream():
                        items = [(h, j) for h in range(8) for j in jl]

                        def qk(h, j):
                            hp, hh = h // 2, h % 2
                            sl = (ti - 4 + j) % 5
                            nk = T if j == 4 else 128
                            bk_ = scr.next()
                            P.op("pe", lambda: PE.matmul(bk_[0:nk, 0:T], kwin.ap(64 * hh, 64, (hp * 5 + sl) * 128, [[1, nk]]), qT.ap(64 * hh, 64, hp * 128, [[1, T]]), start=True, stop=True), reads=[kslot[sl], qT], writes=[bk_])
                            return bk_, nk, sl

                        def soft(h, j, bk_, nk):
                            pT = pTr.next()
                            if j in (1, 2):
                                P.op("act", lambda: A.activation(out=pT[0:nk, 0:T], in_=bk_[0:nk, 0:T], func=AF.Exp, bias=col(8 + h, nk), scale=1.0), reads=[bk_, COLB], writes=[pT])
                            else:
                                stmp = stmpr.next()
                                if j == 0:
                                    badd, bias_ = cM0(nk, T), col(8 + h, nk)
                                    rd = [CST]
                                elif j == 3:
                                    badd, bias_ = BT[0:nk, 8 + h, 0:T], col(6, nk)
                                    rd = [BT]
                                else:
                                    badd, bias_ = BT[0:nk, h, 0:T], col(6, nk)
                                    rd = [BT]
                                P.op("dve", lambda: V.tensor_tensor(stmp[0:nk, 0:T], bk_[0:nk, 0:T], badd, ALU.add), reads=[bk_] + rd, writes=[stmp])
                                P.op("act", lambda: A.activation(out=pT[0:nk, 0:T], in_=stmp[0:nk, 0:T], func=AF.Exp, bias=bias_, scale=1.0), reads=[stmp, COLB], writes=[pT])
                            return pT

                        def pvmm(h, j, pT, nk, sl):
                            hp, hh = h // 2, h % 2
                            first, last = (j == jl[0]), (j == jl[-1])
                            osb = OS[hp % 2]
                            o_ = osb.ap(64 * hh, 64, 0, [[1, T]])
                            s_ = osb.ap(64 * hh, 64, 128, [[1, T]])
                            P.op("pe", lambda: PE.matmul(o_, vwin[0:nk, sl, h * 64:(h + 1) * 64], pT[0:nk, 0:T], start=first, stop=last, tile_position=(0, 64 * hh), skip_group_check=True), reads=[vslot[sl], pT], writes=[osb])
                            P.op("pe", lambda: PE.matmul(s_, ONE64B[0:nk, :], pT[0:nk, 0:T], start=False, stop=last, tile_position=(0, 64 * hh), skip_group_check=True), reads=[ONE64B, pT], writes=[osb])
                            if last and hh == 1:
                                rs = rsr.next()
                                P.op("dve", lambda: V.reciprocal(rs[:, 0:T], osb[:, 128:128 + T]), reads=[osb], writes=[rs])
                                P.op("pool", lambda: Pl.tensor_tensor(rs[:, 0:T], rs[:, 0:T], sga[:, hp, 0:T], ALU.mult), reads=[rs, sga], writes=[rs])
                                P.op("dve", lambda: V.tensor_tensor(mixT[:, 4 + hp, 0:T], osb[:, 0:T], rs[:, 0:T], ALU.mult), reads=[osb, rs], writes=[mixb[4 + hp]])

                        q = []
                        pre = 1
                        for idx in range(min(pre, len(items))):
                            h, j = items[idx]
                            q.append((h, j) + qk(h, j))
                        for idx in range(len(items)):
                            if idx + pre < len(items):
                                h2, j2 = items[idx + pre]
                                q.append((h2, j2) + qk(h2, j2))
                            h, j, bk_, nk, sl = q.pop(0)
                            pT = soft(h, j, bk_, nk)
                            pvmm(h, j, pT, nk, sl)
                            yield

                    streams = [s5_stream(), attn_stream()]
                    while streams:
                        for s_ in list(streams):
                            try:
                                next(s_)
                            except StopIteration:
                                streams.remove(s_)

                    bwa, bwb = prot.next(), prot.next()
                    for oc in range(NCH):
                        bk_ = bwa if oc < 4 else bwb
                        for kc in range(NCH):
                            P.op("pe", (lambda oc=oc, kc=kc, bk_=bk_: PE.matmul(bk_[:, (oc % 4) * 128:(oc % 4) * 128 + T], Wout[:, kc, oc * 128:(oc + 1) * 128], mixT[:, kc, 0:T], start=(kc == 0), stop=(kc == NCH - 1))), reads=[Wout, mixb[kc]], writes=[bk_])
                    P.op("dve", lambda: V.tensor_tensor(v3(xb, T, 4), v3(xb, T, 4), pv3(bwa, T), ALU.add), reads=[xb, bwa], writes=[xb])
                    P.op("dve", lambda: V.tensor_tensor(v3(xb, T, 4, c0=4), v3(xb, T, 4, c0=4), pv3(bwb, T), ALU.add), reads=[xb, bwb], writes=[xb])
                    P.dma("y%d" % bi, lambda: nc.sync.dma_start(out=dst_ap, in_=v3(xb, T, NCH)), reads=[xb])

                P.op("dve", lambda: V.memset(col(7), EPS), writes=[COLB])
                srcp, dstp = (xp, y1p) if l == 0 else (y1p, yp)
                srcs, dsts = (xs, y1s) if l == 0 else (y1s, ys)
                if l == 1:
                    pass
                tix = 0
                for s in range(NSP):
                    P.op("dve", lambda: V.memset(sm(I_INIT), 0.0), writes=smb(I_INIT))
                    load_x(srcp[s, 0], xT[tix % 2], 128, tix % 2)
                    for i in range(NT):
                        bi = tix % 2
                        if i + 1 < NT:
                            load_x(srcp[s, i + 1], xT[(tix + 1) % 2], 128, (tix + 1) % 2)
                        outs = {}
                        if i >= NT - 4:
                            outs["nk"] = nkp[l, s, i - (NT - 4)]
                            outs["nv"] = nvp[l, s, (i - (NT - 4)) * 128:(i - (NT - 4) + 1) * 128, :]
                        if i == NT - 1:
                            outs["st"] = stp[l, s]
                            outs["st_cs"] = (I_CLP, I_SLP)
                        do_tile(xT[bi], bi, 128, i, dstp[s, i], outs)
                        tix += 1
                for s in range(NSS):
                    bi = tix % 2
                    load_x(srcs[s], xT[bi], TS, bi)
                    for hf in range(2):
                        P.dma("cstg", (lambda s=s, hf=hf: nc.sync.dma_start(out=cstg[:], in_=ck[l, s, :, 2 * hf:2 * hf + 2, :])), writes=[cstg])
                        P.op("pool", (lambda hf=hf: Pl.tensor_copy(kwin.ap(0, 128, 2 * hf * 640, [[640, 2], [128, 4], [1, 128]]), cstg.ap(0, 128, 0, [[512, 2], [128, 4], [1, 128]]))), reads=[cstg], writes=kslot[0:4])
                    for hf in range(2):
                        P.dma("cstg", (lambda s=s, hf=hf: nc.sync.dma_start(out=cstg[:], in_=cv[l, s, hf * 256:(hf + 1) * 256, :].rearrange("(j k) e -> k j e", k=128))), writes=[cstg])
                        P.op("act", (lambda hf=hf: A.activation(func=AF.Copy, out=vwin[:, 2 * hf:2 * hf + 2, :], in_=cstg[:])), reads=[cstg], writes=vslot[0:4])
                    P.dma("st0", (lambda s=s: nc.sync.dma_start(out=sm(I_ST0), in_=st0[l, s])), writes=smb(I_ST0))
                    rotate(I_INIT, sm(I_ST0), smb(I_ST0), I_C1, I_S1)
                    outs = {"nk": nks[l, s], "nv": nvs[l, s], "st": sts[l, s], "st_cs": (I_CLS, I_SLS)}
                    do_tile(xT[bi], bi, TS, 4, dsts[s], outs)
                    tix += 1
                P.phase_end()
        print("[kernel] instructions:", P.n_inst, "semaphores:", P.nsem)
    return nc


def _consts():
    c = np.zeros((128, NCST), np.float32)
    idx = np.arange(128)
    c[idx, C_J + 127 - idx] = 1.0
    p = np.arange(64)
    c[64 + p, C_SWN + p] = -1.0
    c[p, C_SWN + 64 + p] = 1.0
    c[:, C_ONES:C_ONES + 128] = 1.0 / 1024.0
    c[0:64, C_BLK:C_BLK + 64] = 1.0 / 64.0
    c[64:128, C_BLK + 64:C_BLK + 128] = 1.0 / 64.0
    c[:, C_ONE64:C_ONE64 + 64] = 1.0
    c[64:128, C_M4:C_M4 + 64] = NEG
    c[0:64, C_M0 + 64:C_M0 + 128] = NEG
    c[:, C_JROW:C_JROW + 128] = np.arange(128, dtype=np.float32)[None]
    c[0:64, C_SH] = 1.0
    c[64:128, C_SH] = -1.0
    return c


_PROG_CACHE = {}


def kernel(x_prompt, x_sample, cache_k, cache_v, state_ssm_re, state_ssm_im, norm_gain, w_in,
           ssm_a_re, ssm_a_im, ssm_b_re, ssm_b_im, ssm_c_re, ssm_c_im, ssm_d, ssm_log_dt,
           w_glu, q_norm_gain, k_norm_gain, rel_bias, w_out, n_cores=8):
    f = np.float32
    x_prompt = np.asarray(x_prompt, f)
    x_sample = np.asarray(x_sample, f)
    B, SEQ, _ = x_prompt.shape
    BS = x_sample.shape[0]
    NSP, NSS = B // n_cores, BS // n_cores
    NT = SEQ // 128
    R = cache_k.shape[2]
    assert R == 512 and SEQ >= 512 and SEQ % 128 == 0 and x_sample.shape[1] == TS

    xpb = x_prompt.reshape(B, NT, 128, NCH, 128).transpose(0, 1, 4, 3, 2)
    xsb = x_sample.reshape(BS, TS, NCH, 128).transpose(0, 3, 2, 1)
    ckb = np.asarray(cache_k, f).reshape(L, BS, R, 4, 2, 64).transpose(0, 1, 4, 5, 3, 2).reshape(L, BS, 128, 4, R)
    cvb = np.asarray(cache_v, f).reshape(L, BS, R, 512)
    st = np.concatenate([np.asarray(state_ssm_re, f).transpose(0, 1, 3, 2), np.asarray(state_ssm_im, f).transpose(0, 1, 3, 2)], axis=2)
    ngh = np.asarray(norm_gain, f).reshape(L, NCH, 128).transpose(2, 0, 1)
    are = np.asarray(ssm_a_re, f).transpose(2, 0, 1)
    aim = np.asarray(ssm_a_im, f).transpose(2, 0, 1)
    ldt = np.broadcast_to(np.asarray(ssm_log_dt, f)[None], (64, L, G))
    s5 = np.stack([are, aim, ldt], axis=2)
    s5 = np.concatenate([s5, s5], axis=0)
    bre = np.asarray(ssm_b_re, f)
    bim = np.asarray(ssm_b_im, f)
    b1 = np.zeros((L, 128, 8, 128), f)
    b2 = np.zeros((L, 128, 8, 128), f)
    cre = np.asarray(ssm_c_re, f)
    cim = np.asarray(ssm_c_im, f)
    l1 = np.zeros((L, 128, G, 32), f)
    l2 = np.zeros((L, 128, G, 32), f)
    for g in range(G):
        c, band, e = g // 8, (g % 8) // 2, g % 2
        r0 = 32 * band + 16 * e
        b1[:, r0:r0 + 16, c * 2 + e, 0:64] = bre[:, g].transpose(0, 2, 1)
        b1[:, r0:r0 + 16, c * 2 + e, 64:128] = bim[:, g].transpose(0, 2, 1)
        b2[:, r0:r0 + 16, c * 2 + e, 0:64] = bim[:, g].transpose(0, 2, 1)
        b2[:, r0:r0 + 16, c * 2 + e, 64:128] = bre[:, g].transpose(0, 2, 1)
        l1[:, 0:64, g, 16 * e:16 * e + 16] = cre[:, g].transpose(0, 2, 1)
        l1[:, 64:128, g, 16 * e:16 * e + 16] = cim[:, g].transpose(0, 2, 1)
        l2[:, 0:64, g, 16 * e:16 * e + 16] = cim[:, g].transpose(0, 2, 1)
        l2[:, 64:128, g, 16 * e:16 * e + 16] = cre[:, g].transpose(0, 2, 1)
    dch = np.asarray(ssm_d, f).reshape(L, 4, 128).transpose(2, 0, 1)
    qg = np.asarray(q_norm_gain, f)
    kg = np.asarray(k_norm_gain, f)
    qkgh = np.stack([np.concatenate([qg, qg], 1), np.concatenate([kg, kg], 1)], axis=2).transpose(1, 0, 2)
    qkr = np.broadcast_to(np.concatenate([qg, kg], 1)[None], (128, L, 128))
    rbh = np.asarray(rel_bias, f)
    rbr = np.broadcast_to(rbh.reshape(1, L, 8 * 257), (128, L, 8 * 257))
    cst = _consts()

    key = (NT, NSP, NSS)
    nc = build_program(NT, NSP, NSS)
    shared = dict(w_in=np.ascontiguousarray(w_in, f), w_glu=np.ascontiguousarray(w_glu, f), w_out=np.ascontiguousarray(w_out, f),
                  ng=np.ascontiguousarray(ngh), s5a=np.ascontiguousarray(s5), b1h=b1.reshape(L, 128, 1024), b2h=b2.reshape(L, 128, 1024),
                  l1h=l1.reshape(L, 128, 1024), l2h=l2.reshape(L, 128, 1024), dcol=np.ascontiguousarray(dch),
                  qkg=np.ascontiguousarray(qkgh), qkrow=np.ascontiguousarray(qkr), rb=np.ascontiguousarray(rbh),
                  rbrep=np.ascontiguousarray(rbr), cst=cst)
    in_maps = []
    for c in range(n_cores):
        m = dict(shared)
        m["xp"] = np.ascontiguousarray(xpb[c * NSP:(c + 1) * NSP])
        m["xs"] = np.ascontiguousarray(xsb[c * NSS:(c + 1) * NSS])
        m["ck"] = np.ascontiguousarray(ckb[:, c * NSS:(c + 1) * NSS])
        m["cv"] = np.ascontiguousarray(cvb[:, c * NSS:(c + 1) * NSS])
        m["st0"] = np.ascontiguousarray(st[:, c * NSS:(c + 1) * NSS])
        in_maps.append(m)
    res = run_bass_kernel_spmd(nc, in_maps, core_ids=list(range(n_cores)))
    rs = res.results

    def cat(name, axis):
        return np.concatenate([np.asarray(r[name]) for r in rs], axis=axis)

    ypo = cat("yp", 0).transpose(0, 1, 4, 3, 2).reshape(B, SEQ, D)
    yso = cat("ys", 0).transpose(0, 3, 2, 1).reshape(BS, TS, D)
    nkpo = cat("nkp", 1)
    nkpo = nkpo.reshape(L, B, 4, 2, 64, 4, 128).transpose(0, 1, 2, 6, 5, 3, 4).reshape(L, B, 512, 8, 64)
    nvpo = cat("nvp", 1).reshape(L, B, 512, 8, 64)
    stpo = cat("stp", 1)
    pr = stpo[:, :, 0:64].transpose(0, 1, 3, 2)
    pi = stpo[:, :, 64:128].transpose(0, 1, 3, 2)
    nkso = cat("nks", 1).reshape(L, BS, 2, 64, 4, TS).transpose(0, 1, 5, 4, 2, 3).reshape(L, BS, TS, 8, 64)
    nvso = cat("nvs", 1).reshape(L, BS, TS, 8, 64)
    stso = cat("sts", 1)
    sr = stso[:, :, 0:64].transpose(0, 1, 3, 2)
    si = stso[:, :, 64:128].transpose(0, 1, 3, 2)
    c_ = lambda a: np.ascontiguousarray(a, dtype=np.float32)
    return (c_(ypo), c_(yso), c_(nkpo), c_(nvpo), c_(pr), c_(pi), c_(nkso), c_(nvso), c_(sr), c_(si))
```

```python
import math
from contextlib import ExitStack
import numpy as np
import concourse.bass as bass
import concourse.mybir as mybir
from concourse.bass_utils import run_bass_kernel_spmd

F32 = mybir.dt.float32
BF16 = mybir.dt.bfloat16
I32 = mybir.dt.int32
ALU = mybir.AluOpType
AF = mybir.ActivationFunctionType
AX = mybir.AxisListType

L = 2
D = 1024
NCH = 8
DIN = 3072
G = 32
TS = 16
EPS = 1e-6
NEG = -30000.0
TWO_PI = 2.0 * math.pi
STRICT = True

C_J, C_SWN, C_ONES, C_BLK, C_ONE64, C_M4, C_M0, C_JROW, C_SH = 0, 128, 256, 384, 512, 576, 704, 832, 960
NCST = 961


class Buf:
    def __init__(self, name):
        self.name = name
        self.w = None
        self.r = {}
        self.excl = False


class TT:
    def __init__(self, h, shape, dt, name, nbuf=None):
        self.h = h
        self.shape = shape
        self.row = int(np.prod(shape[1:]))
        self.dt = dt
        self.b = Buf(name)

    def __getitem__(self, idx):
        return self.h[idx]

    def ap(self, p0, npart, off, dims):
        return bass.AP(self.h, p0 * self.row + off, [[self.row, npart]] + [list(d) for d in dims])


class Op:
    __slots__ = ("eng", "fn", "deps", "needs_inc", "sem", "target", "is_dma", "epoch", "gen")


class Chan:
    def __init__(self, sem):
        self.sem = sem
        self.count = 0


class Prog:
    ENG = ("pe", "act", "dve", "pool", "sp")

    def __init__(self, nc, stack):
        self.nc = nc
        self.stack = stack
        self.e = {"pe": nc.tensor, "act": nc.scalar, "dve": nc.vector, "pool": nc.gpsimd, "sp": nc.sync}
        self.ops = []
        self.epoch = 0
        self.gen = 0
        self.sems = {}
        self.rank = {}
        self.seen = {}
        self.last = {}
        self.chans = {}
        self.nsem = 0
        self.n_inst = 0

    def _sem(self, name):
        self.nsem += 1
        return self.stack.enter_context(self.nc.semaphore(name))

    def chan(self, name):
        if name not in self.chans:
            self.chans[name] = Chan(self._sem("c_" + name))
        return self.chans[name]

    def _deps(self, eng, reads, writes):
        deps = []
        for b in reads:
            if b.w is not None:
                deps.append(b.w)
            if b.excl:
                for en_, o_ in b.r.items():
                    if en_ != eng:
                        deps.append(o_)
        for b in writes:
            if b.w is not None:
                deps.append(b.w)
            deps.extend(b.r.values())
        out = []
        for d in deps:
            if d.gen != self.gen:
                continue
            if (not d.is_dma) and d.eng == eng and (eng == "pe" or not STRICT):
                continue
            d.needs_inc = True
            out.append(d)
        return out

    def op(self, eng, fn, reads=(), writes=()):
        reads = [x.b if hasattr(x, 'b') else x for x in reads]
        writes = [x.b if hasattr(x, 'b') else x for x in writes]
        o = Op()
        o.eng = eng
        o.fn = fn
        o.is_dma = False
        o.needs_inc = False
        o.sem = None
        o.target = 0
        o.epoch = self.epoch
        o.gen = self.gen
        o.deps = self._deps(eng, reads, writes)
        for b in writes:
            b.w = o
            b.r = {}
        for b in reads:
            b.r[eng] = o
        self.ops.append(o)
        return o

    def dma(self, chan, fn, reads=(), writes=()):
        reads = [x.b if hasattr(x, 'b') else x for x in reads]
        writes = [x.b if hasattr(x, 'b') else x for x in writes]
        ch = self.chan(chan)
        o = Op()
        o.eng = "sp"
        o.fn = fn
        o.is_dma = True
        o.needs_inc = True
        ch.count += 1
        o.sem = ch.sem
        o.target = 16 * ch.count
        o.epoch = self.epoch
        o.gen = self.gen
        o.deps = self._deps("sp", reads, writes)
        for b in writes:
            b.w = o
            b.r = {}
        for b in reads:
            b.r["dma_" + chan] = o
        self.ops.append(o)
        return o

    def emit(self):
        for o in self.ops:
            e = self.e[o.eng]
            need = {}
            for d in o.deps:
                k = id(d.sem)
                if k not in need or need[k][1] < d.target:
                    need[k] = (d.sem, d.target)
            for k, (sem, tgt) in need.items():
                sk = (o.eng, k)
                if self.seen.get(sk, 0) >= tgt:
                    continue
                self.seen[sk] = tgt
                e.wait_ge(sem, tgt)
                self.n_inst += 1
            inst = o.fn()
            self.n_inst += 1
            if o.is_dma:
                inst.then_inc(o.sem, 16)
            else:
                self.last[o.eng] = o
                if o.needs_inc:
                    key = (o.eng, o.epoch)
                    if key not in self.sems:
                        self.sems[key] = self._sem("e_%s_%d" % key)
                        self.rank[key] = 0
                    self.rank[key] += 1
                    o.sem = self.sems[key]
                    o.target = self.rank[key]
                    inst.then_inc(o.sem, 1)
        self.ops = []

    def phase_end(self):
        lastops = {}
        for o in self.ops:
            if not o.is_dma:
                lastops[o.eng] = o
        for o in lastops.values():
            o.needs_inc = True
        self.emit()
        for en in self.ENG:
            e = self.e[en]
            for fn_, o in self.last.items():
                if fn_ == en or o.sem is None:
                    continue
                sk = (en, id(o.sem))
                if self.seen.get(sk, 0) >= o.target:
                    continue
                self.seen[sk] = o.target
                e.wait_ge(o.sem, o.target)
            for ch in self.chans.values():
                if ch.count == 0:
                    continue
                sk = (en, id(ch.sem))
                if self.seen.get(sk, 0) >= 16 * ch.count:
                    continue
                self.seen[sk] = 16 * ch.count
                e.wait_ge(ch.sem, 16 * ch.count)
        self.gen += 1
        self.epoch += 1


class PS:
    def __init__(self, bank, c0, name):
        self.bank = bank
        self.c0 = c0
        self.b = Buf(name)

    def ap(self, p0, n, off, dims):
        return self.bank.ap(p0, n, self.c0 + off, dims)

    def v(self, n, T, off=0, p0=0):
        return self.bank.ap(p0, n, self.c0 + off, [[1, T]])


class Ring:
    def __init__(self, items):
        self.items = items
        self.i = 0

    def next(self):
        x = self.items[self.i % len(self.items)]
        self.i += 1
        return x


def build_program(NT, NSP, NSS):
    nc = bass.Bass("TRN2", target_bir_lowering=False)

    def din(name, shape):
        return nc.dram_tensor(name, list(shape), F32, kind="ExternalInput").ap()

    def dout(name, shape):
        return nc.dram_tensor(name, list(shape), F32, kind="ExternalOutput").ap()

    xp = din("xp", [NSP, NT, 128, NCH, 128])
    xs = din("xs", [NSS, 128, NCH, TS])
    ck = din("ck", [L, NSS, 128, 4, 512])
    cv = din("cv", [L, NSS, 512, 512])
    st0 = din("st0", [L, NSS, 128, G])
    w_in = din("w_in", [L, D, DIN])
    w_glu = din("w_glu", [L, 512, 1024])
    w_out = din("w_out", [L, D, D])
    ng = din("ng", [128, L, NCH])
    s5a = din("s5a", [128, L, 3, G])
    b1h = din("b1h", [L, 128, 1024])
    b2h = din("b2h", [L, 128, 1024])
    l1h = din("l1h", [L, 128, 1024])
    l2h = din("l2h", [L, 128, 1024])
    dcol = din("dcol", [128, L, 4])
    qkg = din("qkg", [128, L, 2])
    qkrow = din("qkrow", [128, L, 128])
    rb = din("rb", [L, 8, 257])
    rbrep = din("rbrep", [128, L, 8 * 257])
    cst = din("cst", [128, NCST])

    yp = dout("yp", [NSP, NT, 128, NCH, 128])
    ys = dout("ys", [NSS, 128, NCH, TS])
    nkp = dout("nkp", [L, NSP, 4, 128, 4, 128])
    nvp = dout("nvp", [L, NSP, 512, 512])
    stp = dout("stp", [L, NSP, 128, G])
    nks = dout("nks", [L, NSS, 128, 4, TS])
    nvs = dout("nvs", [L, NSS, TS, 512])
    sts = dout("sts", [L, NSS, 128, G])
    y1p = nc.dram_tensor("y1p", [NSP, NT, 128, NCH, 128], F32, kind="Internal").ap()
    y1s = nc.dram_tensor("y1s", [NSS, 128, NCH, TS], F32, kind="Internal").ap()
    ext = nc.dram_tensor("ext", [L, 8, 384], F32, kind="Internal").ap()

    with ExitStack() as glob:
        P = Prog(nc, glob)

        uniq = [0]

        def sb(stack, name, shape, dt=F32):
            uniq[0] += 1
            name = "%s_%d" % (name, uniq[0])
            h = stack.enter_context(nc.sbuf_tensor(name, list(shape), dt))
            return TT(h, list(shape), dt, name)

        def ps(stack, name):
            h = stack.enter_context(nc.psum_tensor(name, [128, 512], F32))
            t_ = TT(h, [128, 512], F32, name)
            t_.b.excl = True
            return t_

        Win = sb(glob, "Win", [128, NCH, DIN], BF16)
        Wglu = sb(glob, "Wglu", [128, 4, 1024], BF16)
        Wout = sb(glob, "Wout", [128, NCH, D], BF16)
        PRE1 = sb(glob, "PRE1", [128, G, 128], BF16)
        PRE2 = sb(glob, "PRE2", [128, G, 128], BF16)
        POST1 = sb(glob, "POST1", [128, G, 128], BF16)
        POST2 = sb(glob, "POST2", [128, G, 128], BF16)
        B1 = sb(glob, "B1", [128, 8, 128], BF16)
        B2 = sb(glob, "B2", [128, 8, 128], BF16)
        L1 = sb(glob, "L1", [128, G, 32], BF16)
        L2 = sb(glob, "L2", [128, G, 32], BF16)
        CST = sb(glob, "CST", [128, NCST], F32)
        ONESB = sb(glob, "ONESB", [128, 128], BF16)
        BLKB = sb(glob, "BLKB", [128, 128], BF16)
        ONE64B = sb(glob, "ONE64B", [128, 64], BF16)
        BT = sb(glob, "BT", [128, 16, 128], F32)
        kwin = sb(glob, "kwin", [128, 4, 6, 128], BF16)
        vwin = sb(glob, "vwin", [128, 6, 512], BF16)
        kslot = [Buf("ks%d" % i) for i in range(6)]
        vslot = [Buf("vs%d" % i) for i in range(6)]
        SM = sb(glob, "SM", [128, 40, G], F32)
        SMB = [Buf("sm%d" % i) for i in range(40)]
        COL = sb(glob, "COL", [128, 64], F32)
        COLB = Buf("col")
        NGs = sb(glob, "NGs", [128, L, NCH], F32)
        S5A = sb(glob, "S5A", [128, L, 3, G], F32)
        DCOL = sb(glob, "DCOL", [128, L, 4], F32)
        QKG = sb(glob, "QKG", [128, L, 2], F32)
        banks = [ps(glob, "pb%d" % i) for i in range(8)]
        ybank = banks[0]
        pqr = Ring(banks[1:3])
        scr = Ring(banks[3:5])
        OS = [banks[5], banks[6]]
        GEN = banks[1]
        GEN2 = banks[2]
        prot = Ring(banks[1:8])

        (I_ARE, I_AIM, I_LDT, I_DT, I_R, I_TH, I_S1, I_C1, I_ABRE, I_ABIM, I_NR, I_DEN, I_FRE, I_FIM,
         I_CA, I_SA, I_CLP, I_SLP, I_CLS, I_SLS, I_INIT, I_GLAST, I_T0, I_T1, I_T2, I_T3, I_T4, I_T5, I_T6,
         I_T7, I_HL, I_ST0, I_NFIM) = range(33)

        def sm(i, n=128):
            return SM.ap(0, n, i * G, [[1, G]])

        def col(i, n=128, p0=0):
            return COL.ap(p0, n, i, [[1, 1]])

        cJ = CST.ap(0, 128, C_J, [[1, 128]])
        cSWN = CST.ap(0, 128, C_SWN, [[1, 128]])
        cM4 = lambda nk, T: CST.ap(0, nk, C_M4, [[1, T]])
        cM0 = lambda nk, T: CST.ap(0, nk, C_M0, [[1, T]])
        cSH = CST.ap(0, 128, C_SH, [[1, 1]])

        V, A, Pl, PE = nc.vector, nc.scalar, nc.gpsimd, nc.tensor

        P.dma("cst", lambda: nc.sync.dma_start(out=CST[:], in_=cst[:]), writes=[CST])
        P.dma("cst2", lambda: nc.sync.dma_start(out=NGs[:], in_=ng[:]), writes=[NGs])
        P.dma("cst3", lambda: nc.sync.dma_start(out=S5A[:], in_=s5a[:]), writes=[S5A])
        P.dma("cst4", lambda: nc.sync.dma_start(out=DCOL[:], in_=dcol[:]), writes=[DCOL])
        P.dma("cst5", lambda: nc.sync.dma_start(out=QKG[:], in_=qkg[:]), writes=[QKG])
        P.op("dve", lambda: V.tensor_copy(ONESB[:], CST.ap(0, 128, C_ONES, [[1, 128]])), reads=[CST], writes=[ONESB])
        P.op("dve", lambda: V.tensor_copy(BLKB[:], CST.ap(0, 128, C_BLK, [[1, 128]])), reads=[CST], writes=[BLKB])
        P.op("dve", lambda: V.tensor_copy(ONE64B[:], CST.ap(0, 128, C_ONE64, [[1, 64]])), reads=[CST], writes=[ONE64B])
        P.phase_end()

        for l in range(L):
            with ExitStack() as s1:
                stg = [sb(s1, "stg%d" % i, [128, 1536], F32) for i in range(2)]
                tmpN = 256
                tb = [sb(s1, "tb%d" % i, [128, tmpN], F32) for i in range(8)]
                tbi = sb(s1, "tbi", [128, tmpN], I32)
                HK = sb(s1, "HK", [128, 16, 128], F32)
                RBR = sb(s1, "RBR", [128, 8 * 257], F32)
                QKR = sb(s1, "QKR", [128, 128], F32)
                EXS = sb(s1, "EXS", [8, 384], F32)

                wi = 0
                for kc2 in range(2 * NCH):
                    kc, hf = kc2 // 2, kc2 % 2
                    st = stg[wi % 2]
                    P.dma("w%d" % (wi % 2), (lambda st=st, kc=kc, hf=hf: nc.sync.dma_start(out=st[:], in_=w_in[l, kc * 128:(kc + 1) * 128, hf * 1536:(hf + 1) * 1536])), writes=[st])
                    gcol = NGs.ap(0, 128, l * NCH + kc, [[1, 1]])
                    if kc2 % 2 == 0:
                        P.op("act", (lambda st=st, kc=kc, hf=hf, gcol=gcol: A.activation(out=Win[:, kc, hf * 1536:(hf + 1) * 1536], in_=st[:], func=AF.Identity, scale=gcol)), reads=[st, NGs], writes=[Win])
                    else:
                        P.op("dve", (lambda st=st, kc=kc, hf=hf, gcol=gcol: V.tensor_scalar(Win[:, kc, hf * 1536:(hf + 1) * 1536], st[:], gcol, None, ALU.mult)), reads=[st, NGs], writes=[Win])
                    wi += 1
                for kc in range(4):
                    st = stg[wi % 2]
                    P.dma("w%d" % (wi % 2), (lambda st=st, kc=kc: nc.sync.dma_start(out=st[:, 0:1024], in_=w_glu[l, kc * 128:(kc + 1) * 128, :])), writes=[st])
                    P.op("pool", (lambda st=st, kc=kc: Pl.tensor_scalar(Wglu[:, kc, :], st[:, 0:1024], 0.5, None, ALU.mult)), reads=[st], writes=[Wglu])
                    wi += 1
                for kc in range(NCH):
                    st = stg[wi % 2]
                    P.dma("w%d" % (wi % 2), (lambda st=st, kc=kc: nc.sync.dma_start(out=st[:, 0:1024], in_=w_out[l, kc * 128:(kc + 1) * 128, :])), writes=[st])
                    wsc = 0.25 if kc < 4 else 0.5
                    if kc % 2 == 0:
                        P.op("dve", (lambda st=st, kc=kc, wsc=wsc: V.tensor_scalar(Wout[:, kc, :], st[:, 0:1024], wsc, None, ALU.mult)), reads=[st], writes=[Wout])
                    else:
                        P.op("pool", (lambda st=st, kc=kc, wsc=wsc: Pl.tensor_scalar(Wout[:, kc, :], st[:, 0:1024], wsc, None, ALU.mult)), reads=[st], writes=[Wout])
                    wi += 1
                for src, dst in ((b1h, B1), (b2h, B2), (l1h, L1), (l2h, L2)):
                    st = stg[wi % 2]
                    P.dma("w%d" % (wi % 2), (lambda st=st, src=src: nc.sync.dma_start(out=st[:, 0:1024], in_=src[l])), writes=[st])
                    P.op("dve", (lambda st=st, dst=dst: V.tensor_copy(dst.ap(0, 128, 0, [[1, 1024]]), st[:, 0:1024])), reads=[st], writes=[dst])
                    wi += 1

                def sa(i):
                    return S5A.ap(0, 128, (l * 3 + i) * G, [[1, G]])

                def sincos(ang, n, out_sin, out_cos, rd, wr):
                    t, kf, d, m = (tb[4].ap(0, 128, 0, [[1, n]]), tb[5].ap(0, 128, 0, [[1, n]]),
                                   tb[6].ap(0, 128, 0, [[1, n]]), tb[7].ap(0, 128, 0, [[1, n]]))
                    ki = tbi.ap(0, 128, 0, [[1, n]])
                    for off, outp in ((0.0, out_sin), (0.25, out_cos)):
                        P.op("dve", (lambda off=off: V.tensor_scalar(t, ang, 1.0 / TWO_PI, off, ALU.mult, ALU.add)), reads=rd + [tb[4]], writes=[tb[4]])
                        P.op("dve", lambda: V.tensor_copy(ki, t), reads=[tb[4]], writes=[tbi])
                        P.op("dve", lambda: V.tensor_copy(kf, ki), reads=[tbi], writes=[tb[5]])
                        P.op("dve", lambda: V.tensor_tensor(d, t, kf, ALU.subtract), reads=[tb[4], tb[5]], writes=[tb[6]])
                        P.op("dve", lambda: V.tensor_scalar(m, d, 0.5, None, ALU.is_gt), reads=[tb[6]], writes=[tb[7]])
                        P.op("dve", lambda: V.tensor_tensor(d, d, m, ALU.subtract), reads=[tb[6], tb[7]], writes=[tb[6]])
                        P.op("dve", lambda: V.tensor_scalar(m, d, -0.5, None, ALU.is_lt), reads=[tb[6]], writes=[tb[7]])
                        P.op("dve", lambda: V.tensor_tensor(d, d, m, ALU.add), reads=[tb[6], tb[7]], writes=[tb[6]])
                        P.op("act", (lambda outp=outp: A.activation(out=outp, in_=d, func=AF.Sin, scale=TWO_PI * (1.0 - 2e-6))), reads=[tb[6]], writes=wr)

                smb = lambda *ids: [SMB[i] for i in ids]
                P.op("act", lambda: A.activation(out=sm(I_DT), in_=sa(2), func=AF.Exp), reads=[S5A], writes=smb(I_DT))
                P.op("dve", lambda: V.tensor_tensor(sm(I_T0), sm(I_DT), sa(0), ALU.mult), reads=[S5A] + smb(I_DT), writes=smb(I_T0))
                P.op("act", lambda: A.activation(out=sm(I_R), in_=sm(I_T0), func=AF.Exp), reads=smb(I_T0), writes=smb(I_R))
                P.op("dve", lambda: V.tensor_tensor(sm(I_TH), sm(I_DT), sa(1), ALU.mult), reads=[S5A] + smb(I_DT), writes=smb(I_TH))
                sincos(sm(I_TH), G, sm(I_S1), sm(I_C1), smb(I_TH), smb(I_S1, I_C1))
                for mult_, (ci, si) in ((128.0, (I_CA, I_SA)), (127.0, (I_CLP, I_SLP)), (float(TS - 1), (I_CLS, I_SLS))):
                    P.op("dve", (lambda mult_=mult_: V.tensor_scalar(sm(I_T1), sm(I_TH), mult_, None, ALU.mult)), reads=smb(I_TH), writes=smb(I_T1))
                    sincos(sm(I_T1), G, sm(si), sm(ci), smb(I_T1), smb(si, ci))
                P.op("dve", lambda: V.tensor_tensor(sm(I_ABRE), sm(I_R), sm(I_C1), ALU.mult), reads=smb(I_R, I_C1), writes=smb(I_ABRE))
                P.op("dve", lambda: V.tensor_tensor(sm(I_ABIM), sm(I_R), sm(I_S1), ALU.mult), reads=smb(I_R, I_S1), writes=smb(I_ABIM))
                P.op("dve", lambda: V.tensor_scalar(sm(I_NR), sm(I_ABRE), -1.0, None, ALU.add), reads=smb(I_ABRE), writes=smb(I_NR))
                P.op("dve", lambda: V.tensor_tensor(sm(I_T0), sa(0), sa(0), ALU.mult), reads=[S5A], writes=smb(I_T0))
                P.op("dve", lambda: V.tensor_tensor(sm(I_T1), sa(1), sa(1), ALU.mult), reads=[S5A], writes=smb(I_T1))
                P.op("dve", lambda: V.tensor_tensor(sm(I_DEN), sm(I_T0), sm(I_T1), ALU.add), reads=smb(I_T0, I_T1), writes=smb(I_DEN))
                P.op("dve", lambda: V.reciprocal(sm(I_DEN), sm(I_DEN)), reads=smb(I_DEN), writes=smb(I_DEN))
                P.op("dve", lambda: V.tensor_tensor(sm(I_T0), sm(I_NR), sa(0), ALU.mult), reads=[S5A] + smb(I_NR), writes=smb(I_T0))
                P.op("dve", lambda: V.tensor_tensor(sm(I_T1), sm(I_ABIM), sa(1), ALU.mult), reads=[S5A] + smb(I_ABIM), writes=smb(I_T1))
                P.op("dve", lambda: V.tensor_tensor(sm(I_T0), sm(I_T0), sm(I_T1), ALU.add), reads=smb(I_T0, I_T1), writes=smb(I_T0))
                P.op("dve", lambda: V.tensor_tensor(sm(I_FRE), sm(I_T0), sm(I_DEN), ALU.mult), reads=smb(I_T0, I_DEN), writes=smb(I_FRE))
                P.op("dve", lambda: V.tensor_tensor(sm(I_T2), sm(I_ABIM), sa(0), ALU.mult), reads=[S5A] + smb(I_ABIM), writes=smb(I_T2))
                P.op("dve", lambda: V.tensor_tensor(sm(I_T3), sm(I_NR), sa(1), ALU.mult), reads=[S5A] + smb(I_NR), writes=smb(I_T3))
                P.op("dve", lambda: V.tensor_tensor(sm(I_T2), sm(I_T2), sm(I_T3), ALU.subtract), reads=smb(I_T2, I_T3), writes=smb(I_T2))
                P.op("dve", lambda: V.tensor_tensor(sm(I_FIM), sm(I_T2), sm(I_DEN), ALU.mult), reads=smb(I_T2, I_DEN), writes=smb(I_FIM))
                P.op("dve", lambda: V.tensor_scalar(sm(I_NFIM), sm(I_FIM), -1.0, None, ALU.mult), reads=smb(I_FIM), writes=smb(I_NFIM))

                NB = 2
                for gb in range(G // NB):
                    ang = tb[0].ap(0, 128, 0, [[1, NB * 128]])
                    sinb = tb[1].ap(0, 128, 0, [[1, NB * 128]])
                    cosb = tb[2].ap(0, 128, 0, [[1, NB * 128]])
                    a3 = tb[3].ap(0, 128, 0, [[128, NB], [1, 128]])
                    ang3 = tb[0].ap(0, 128, 0, [[128, NB], [1, 128]])
                    sin3 = tb[1].ap(0, 128, 0, [[128, NB], [1, 128]])
                    cos3 = tb[2].ap(0, 128, 0, [[128, NB], [1, 128]])
                    thb = SM.ap(0, 128, I_TH * G + gb * NB, [[1, NB], [0, 128]])
                    freb = SM.ap(0, 128, I_FRE * G + gb * NB, [[1, NB], [0, 128]])
                    fimb = SM.ap(0, 128, I_FIM * G + gb * NB, [[1, NB], [0, 128]])
                    nfimb = SM.ap(0, 128, I_NFIM * G + gb * NB, [[1, NB], [0, 128]])
                    jb = CST.ap(0, 128, C_JROW, [[0, NB], [1, 128]])
                    dst = lambda Tn: Tn.ap(0, 128, gb * NB * 128, [[128, NB], [1, 128]])
                    P.op("dve", (lambda ang3=ang3, thb=thb, jb=jb: V.tensor_tensor(ang3, thb, jb, ALU.mult)), reads=[CST] + smb(I_TH), writes=[tb[0]])
                    sincos(ang, NB * 128, sinb, cosb, [tb[0]], [tb[1], tb[2]])
                    P.op("dve", (lambda a3=a3, cos3=cos3, freb=freb: V.tensor_tensor(a3, cos3, freb, ALU.mult)), reads=[tb[2]] + smb(I_FRE), writes=[tb[3]])
                    P.op("dve", (lambda ang3=ang3, sin3=sin3, fimb=fimb: V.tensor_tensor(ang3, sin3, fimb, ALU.mult)), reads=[tb[1]] + smb(I_FIM), writes=[tb[0]])
                    P.op("dve", (lambda a3=a3, ang3=ang3, d_=dst(PRE1): V.tensor_tensor(d_, a3, ang3, ALU.add)), reads=[tb[3], tb[0]], writes=[PRE1])
                    P.op("dve", (lambda a3=a3, sin3=sin3, freb=freb: V.tensor_tensor(a3, sin3, freb, ALU.mult)), reads=[tb[1]] + smb(I_FRE), writes=[tb[3]])
                    P.op("dve", (lambda ang3=ang3, cos3=cos3, nfimb=nfimb: V.tensor_tensor(ang3, cos3, nfimb, ALU.mult)), reads=[tb[2]] + smb(I_NFIM), writes=[tb[0]])
                    P.op("dve", (lambda a3=a3, ang3=ang3: V.tensor_tensor(a3, a3, ang3, ALU.add)), reads=[tb[3], tb[0]], writes=[tb[3]])
                    P.op("dve", (lambda a3=a3, d_=dst(PRE2): V.tensor_scalar(d_, a3, cSH, None, ALU.mult)), reads=[tb[3], CST], writes=[PRE2])
                    P.op("dve", (lambda cos3=cos3, d_=dst(POST1): V.tensor_scalar(d_, cos3, cSH, None, ALU.mult)), reads=[tb[2], CST], writes=[POST1])
                    P.op("dve", (lambda sin3=sin3, d_=dst(POST2): V.tensor_scalar(d_, sin3, -1.0, None, ALU.mult)), reads=[tb[1]], writes=[POST2])

                P.dma("rbr", lambda: nc.sync.dma_start(out=RBR[:], in_=rbrep[:, l, :]), writes=[RBR])
                P.dma("qkr", lambda: nc.sync.dma_start(out=QKR[:], in_=qkrow[:, l, :]), writes=[QKR])
                P.dma("exs", lambda: nc.sync.dma_start(out=EXS[0:8, 0:257], in_=rb[l]), writes=[EXS])
                P.op("dve", lambda: V.tensor_copy(EXS[0:8, 257:384], EXS[0:8, 256:257].to_broadcast([8, 127])), reads=[EXS], writes=[EXS])
                P.dma("exd", lambda: nc.sync.dma_start(out=ext[l], in_=EXS[0:8, :]), reads=[EXS], writes=[HK])
                for i in range(16):
                    h = i % 8
                    off = 1 if i < 8 else 129
                    src = bass.AP(ext.tensor, (l * 8 + h) * 384 + off, [[1, 128], [1, 128]])
                    P.dma("hk", (lambda i=i, src=src: nc.sync.dma_start(out=HK[:, i, :], in_=src)), reads=[HK], writes=[HK])
                P.op("act", lambda: A.activation(out=RBR[:], in_=RBR[:], func=AF.Abs), reads=[RBR], writes=[RBR])
                P.op("dve", lambda: V.reduce_max(col(4), RBR[:], AX.X), reads=[RBR], writes=[COLB])
                P.op("act", lambda: A.activation(out=QKR[:], in_=QKR[:], func=AF.Abs), reads=[QKR], writes=[QKR])
                P.op("dve", lambda: V.reduce_max(col(2), QKR[:, 0:64], AX.X), reads=[QKR], writes=[COLB])
                P.op("dve", lambda: V.reduce_max(col(3), QKR[:, 64:128], AX.X), reads=[QKR], writes=[COLB])
                P.op("dve", lambda: V.tensor_tensor(col(5), col(2), col(3), ALU.mult), reads=[COLB], writes=[COLB])
                P.op("dve", lambda: V.tensor_scalar(col(5), col(5), 8.0, col(4), ALU.mult, ALU.add), reads=[COLB], writes=[COLB])
                P.op("dve", lambda: V.tensor_scalar(col(6), col(5), -1.0, None, ALU.mult), reads=[COLB], writes=[COLB])
                P.dma("rbr", lambda: nc.sync.dma_start(out=RBR[:], in_=rbrep[:, l, :]), reads=[RBR], writes=[RBR])
                P.op("dve", lambda: V.tensor_scalar(COL.ap(0, 128, 8, [[1, 8]]), RBR.ap(0, 128, 256, [[257, 8]]), col(5), None, ALU.subtract), reads=[RBR, COLB], writes=[COLB])
                P.op("dve", lambda: V.tensor_scalar(col(0), QKG.ap(0, 128, l * 2, [[1, 1]]), 0.125, None, ALU.mult), reads=[QKG], writes=[COLB])
                P.op("dve", lambda: V.tensor_copy(col(1), QKG.ap(0, 128, l * 2 + 1, [[1, 1]])), reads=[QKG], writes=[COLB])
                for i in range(16):
                    bk = prot.next()
                    P.op("pe", (lambda i=i, bk=bk: PE.matmul(bk[:, 0:128], cJ, HK[:, i, :], start=True, stop=True)), reads=[CST, HK], writes=[bk])
                    if i < 8:
                        P.op("dve", (lambda i=i, bk=bk: V.tensor_tensor(BT[:, i, :], bk[:, 0:128], cM4(128, 128), ALU.add)), reads=[bk, CST], writes=[BT])
                    else:
                        P.op("dve", (lambda i=i, bk=bk: V.tensor_copy(BT[:, i, :], bk[:, 0:128])), reads=[bk], writes=[BT])
                P.phase_end()

            with ExitStack() as s2:
                xT = [sb(s2, "xT%d" % i, [128, NCH, 128], F32) for i in range(2)]
                sq = sb(s2, "sq", [128, NCH, 128], BF16)
                hT = sb(s2, "hT", [128, NCH, 128], BF16)
                uT = [sb(s2, "uT%d" % i, [128, 4, 128], BF16) for i in range(2)]
                sgs = [sb(s2, "sgs%d" % i, [128, 4, 128], F32) for i in range(2)]
                sga = [sb(s2, "sga%d" % i, [128, 4, 128], F32) for i in range(2)]
                qT = [sb(s2, "qT%d" % i, [128, 4, 128], BF16) for i in range(2)]
                qsq = sb(s2, "qsq", [128, 2, 128], BF16)
                rstd = sb(s2, "rstd", [128, 128], F32)
                rq = sb(s2, "rq", [128, 2, 128], F32)
                kn32 = sb(s2, "kn32", [128, 4, 128], F32)
                v32 = sb(s2, "v32", [128, 512], F32)
                t1r = Ring([sb(s2, "t1_%d" % i, [128, 128], F32) for i in range(2)])
                t2r = Ring([sb(s2, "t2_%d" % i, [128, 128], F32) for i in range(2)])
                btr = Ring([sb(s2, "bt_%d" % i, [128, 128], F32) for i in range(2)])
                Gr = Ring([sb(s2, "G_%d" % i, [128, 128], F32) for i in range(2)])
                W1r = Ring([sb(s2, "W1_%d" % i, [128, 128], BF16) for i in range(2)])
                W2r = Ring([sb(s2, "W2_%d" % i, [128, 128], BF16) for i in range(2)])
                ypre = sb(s2, "ypre", [128, 4, 128], F32)
                tt = sb(s2, "tt", [128, 4, 128], F32)
                yg = sb(s2, "yg", [128, 4, 128], BF16)
                sg = sb(s2, "sg", [128, 4, 128], F32)
                mixT = sb(s2, "mixT", [128, NCH, 128], BF16)
                mixb = [Buf("mix%d" % i) for i in range(NCH)]
                stmpr = Ring([sb(s2, "stmp%d" % i, [128, 128], F32) for i in range(3)])
                pTr = Ring([sb(s2, "pT%d" % i, [128, 128], BF16) for i in range(4)])
                rsr = Ring([sb(s2, "rs%d" % i, [128, 128], F32) for i in range(2)])
                hring = Ring([banks[5], banks[6], banks[7]])
                wring = Ring(banks[3:5])
                C_GELU = math.sqrt(2.0 / math.pi)

                def v3(t, T, nchunk, c0=0, n=128, p0=0):
                    return t.ap(p0, n, c0 * 128, [[128, nchunk], [1, T]])

                def pv3(bk, T, nchunk=4, c0=0):
                    return bk.ap(0, 128, c0 * 128, [[128, nchunk], [1, T]])

                def rotate(dst_i, src_ap, src_bufs, ci, si, bk=None):
                    bk = bk if bk is not None else prot.next()
                    P.op("pe", lambda: PE.matmul(bk[:, 0:G], cSWN, src_ap, start=True, stop=True), reads=[CST] + src_bufs, writes=[bk])
                    P.op("dve", lambda: V.tensor_tensor(sm(I_T6), sm(ci), src_ap, ALU.mult), reads=smb(ci) + src_bufs, writes=smb(I_T6))
                    P.op("dve", lambda: V.tensor_tensor(sm(I_T7), sm(si), bk[:, 0:G], ALU.mult), reads=smb(si) + [bk], writes=smb(I_T7))
                    P.op("dve", lambda: V.tensor_tensor(sm(dst_i), sm(I_T6), sm(I_T7), ALU.add), reads=smb(I_T6, I_T7), writes=smb(dst_i))

                def load_x(src_ap, xb, T, bi):
                    P.dma("x%d" % bi, lambda: nc.sync.dma_start(out=v3(xb, T, NCH), in_=src_ap), writes=[xb])

                def rsqrt_act(dst_ap, src_ap, rd, wr):
                    P.op("act", lambda: A.activation(out=dst_ap, in_=src_ap, func=AF.Ln, bias=col(7), scale=1.0), reads=rd + [COLB], writes=wr)
                    P.op("act", lambda: A.activation(out=dst_ap, in_=dst_ap, func=AF.Exp, scale=-0.5), reads=wr, writes=wr)

                def head_stream(xb, T, ti, par, outs, gk):
                    slot = gk % 6
                    uTp, qTp, sgsp, sgap = uT[par], qT[par], sgs[par], sga[par]
                    P.op("act", lambda: A.activation(out=v3(sq, T, NCH), in_=v3(xb, T, NCH), func=AF.Square), reads=[xb], writes=[sq])
                    HB = hring.next()
                    for c in range(NCH):
                        P.op("pe", (lambda c=c, HB=HB: PE.matmul(HB[:, 0:T], ONESB[:], sq[:, c, 0:T], start=(c == 0), stop=(c == NCH - 1))), reads=[ONESB, sq], writes=[HB])
                    rsqrt_act(rstd[:, 0:T], HB[:, 0:T], [HB], [rstd])
                    P.op("dve", lambda: V.tensor_tensor(v3(hT, T, NCH), v3(xb, T, NCH), rstd.ap(0, 128, 0, [[0, NCH], [1, T]]), ALU.mult), reads=[xb, rstd], writes=[hT])
                    yield

                    def win_group(HB, cb, nch=4, c0=0):
                        for c in range(nch):
                            for kc in range(NCH):
                                P.op("pe", (lambda c=c, kc=kc, HB=HB: PE.matmul(HB[:, (c0 + c) * 128:(c0 + c) * 128 + T], Win[:, kc, cb + c * 128:cb + (c + 1) * 128], hT[:, kc, 0:T], start=(kc == 0), stop=(kc == NCH - 1))), reads=[Win, hT], writes=[HB])
                            if c % 2 == 1:
                                yield

                    for hf in range(2):
                        HB = hring.next()
                        yield from win_group(HB, 1024 + hf * 256, nch=2)
                        P.op("act", lambda HB=HB: A.activation(out=v3(qsq, T, 2), in_=pv3(HB, T, 2), func=AF.Square), reads=[HB], writes=[qsq])
                        for c in range(2):
                            P.op("pe", (lambda c=c, HB=HB: PE.matmul(HB[:, (2 + c) * 128:(2 + c) * 128 + T], BLKB[:], qsq[:, c, 0:T], start=True, stop=True)), reads=[BLKB, qsq], writes=[HB])
                        rsqrt_act(v3(rq, T, 2), pv3(HB, T, 2, c0=2), [HB], [rq])
                        P.op("dve", (lambda hf=hf, HB=HB: V.scalar_tensor_tensor(v3(qTp, T, 2, c0=2 * hf), pv3(HB, T, 2), col(0), v3(rq, T, 2), ALU.mult, ALU.mult)), reads=[HB, rq, COLB], writes=[qTp])
                        yield
                    for hf in range(2):
                        HB = hring.next()
                        yield from win_group(HB, 1536 + hf * 256, nch=2)
                        P.op("act", lambda HB=HB: A.activation(out=v3(qsq, T, 2), in_=pv3(HB, T, 2), func=AF.Square), reads=[HB], writes=[qsq])
                        for c in range(2):
                            P.op("pe", (lambda c=c, HB=HB: PE.matmul(HB[:, (2 + c) * 128:(2 + c) * 128 + T], BLKB[:], qsq[:, c, 0:T], start=True, stop=True)), reads=[BLKB, qsq], writes=[HB])
                        rsqrt_act(v3(rq, T, 2), pv3(HB, T, 2, c0=2), [HB], [rq])
                        P.op("dve", (lambda hf=hf, HB=HB: V.scalar_tensor_tensor(v3(kn32, T, 2, c0=2 * hf), pv3(HB, T, 2), col(1), v3(rq, T, 2), ALU.mult, ALU.mult)), reads=[HB, rq, COLB], writes=[kn32])
                        yield
                    P.op("pool", lambda: Pl.tensor_copy(kwin.ap(0, 128, slot * 128, [[768, 4], [1, T]]), v3(kn32, T, 4)), reads=[kn32], writes=[kslot[slot]])
                    if outs.get("nk") is not None:
                        P.dma("nk", lambda: nc.sync.dma_start(out=outs["nk"], in_=v3(kn32, T, 4)), reads=[kn32])
                    HB = hring.next()
                    yield from win_group(HB, 0)
                    P.op("act", lambda HB=HB: A.activation(func=AF.Copy, out=v3(uTp, T, 4), in_=pv3(HB, T)), reads=[HB], writes=[uTp])
                    HB = hring.next()
                    for kc in range(NCH):
                        P.op("pe", (lambda kc=kc, HB=HB: PE.matmul(HB[0:T, :], hT[:, kc, 0:T], Win[:, kc, 2048:2560], start=(kc == 0), stop=(kc == NCH - 1))), reads=[Win, hT], writes=[HB])
                    P.op("act", lambda HB=HB: A.activation(func=AF.Copy, out=vwin[0:T, slot, :], in_=HB[0:T, :]), reads=[HB], writes=[vslot[slot]])
                    if outs.get("nv") is not None:
                        P.op("dve", lambda HB=HB: V.tensor_copy(v32[0:T, :], HB[0:T, :]), reads=[HB], writes=[v32])
                        P.dma("nv", lambda: nc.sync.dma_start(out=outs["nv"], in_=v32[0:T, :]), reads=[v32])
                    yield
                    for cb, dstp in ((512, sgsp), (2560, sgap)):
                        HB = hring.next()
                        yield from win_group(HB, cb)
                        P.op("act", (lambda dstp=dstp, HB=HB: A.activation(out=v3(dstp, T, 4), in_=pv3(HB, T), func=AF.Tanh, scale=0.5)), reads=[HB], writes=[dstp])
                        P.op("dve", (lambda dstp=dstp, HB=HB: V.scalar_tensor_tensor(v3(dstp, T, 4), v3(dstp, T, 4), 1.0, pv3(HB, T), ALU.add, ALU.mult)), reads=[HB, dstp], writes=[dstp])
                        yield

                def s5_stream(T, par, outs):
                    uTp, sgsp = uT[par], sgs[par]

                    def front(g):
                        c, band, e = g // 8, (g % 8) // 2, g % 2
                        r0 = 32 * band
                        bkp = pqr.next()
                        P.op("pe", lambda: PE.matmul(bkp[:, 0:T], B1.ap(r0, 32, (c * 2 + e) * 128, [[1, 128]]), uTp.ap(r0, 32, c * 128, [[1, T]]), start=True, stop=True, tile_position=(r0, 0)), reads=[B1, uTp], writes=[bkp])
                        P.op("pe", lambda: PE.matmul(bkp[:, 128:128 + T], B2.ap(r0, 32, (c * 2 + e) * 128, [[1, 128]]), uTp.ap(r0, 32, c * 128, [[1, T]]), start=True, stop=True, tile_position=(r0, 0)), reads=[B2, uTp], writes=[bkp])
                        return bkp

                    def stageB(g, bkp):
                        t1, t2, bt_ = t1r.next(), t2r.next(), btr.next()
                        P.op("dve", lambda: V.tensor_tensor(t1[:, 0:T], bkp[:, 0:T], PRE1[:, g, 0:T], ALU.mult), reads=[bkp, PRE1], writes=[t1])
                        P.op("dve", lambda: V.tensor_tensor(t2[:, 0:T], bkp[:, 128:128 + T], PRE2[:, g, 0:T], ALU.mult), reads=[bkp, PRE2], writes=[t2])
                        P.op("pool", lambda: Pl.tensor_tensor(bt_[:, 0:T], t1[:, 0:T], t2[:, 0:T], ALU.add), reads=[t1, t2], writes=[bt_])
                        return bt_

                    def stageC(g, bt_):
                        Gt, W1, W2 = Gr.next(), W1r.next(), W2r.next()
                        P.op("dve", lambda: V.tensor_tensor_scan(Gt[:, 0:T], SM.ap(0, 128, I_R * G + g, [[0, T]]), bt_[:, 0:T], SM.ap(0, 128, I_INIT * G + g, [[1, 1]]), ALU.mult, ALU.add), reads=[bt_] + smb(I_R, I_INIT), writes=[Gt])
                        P.op("pool", lambda: Pl.tensor_tensor(W1[:, 0:T], Gt[:, 0:T], POST1[:, g, 0:T], ALU.mult), reads=[Gt, POST1], writes=[W1])
                        P.op("pool", lambda: Pl.tensor_tensor(W2[:, 0:T], Gt[:, 0:T], POST2[:, g, 0:T], ALU.mult), reads=[Gt, POST2], writes=[W2])
                        P.op("act", lambda: A.activation(func=AF.Copy, out=SM.ap(0, 128, I_GLAST * G + g, [[1, 1]]), in_=Gt[:, T - 1:T]), reads=[Gt], writes=smb(I_GLAST))
                        return W1, W2

                    def back(g, W1, W2):
                        c, band, e = g // 8, (g % 8) // 2, g % 2
                        r0 = 32 * band
                        o_ = ybank.ap(r0, 32, c * 128, [[1, T]])
                        P.op("pe", lambda: PE.matmul(o_, L1[:, g, :], W1[:, 0:T], start=(e == 0), stop=False, tile_position=(0, r0)), reads=[L1, W1], writes=[ybank])
                        P.op("pe", lambda: PE.matmul(o_, L2[:, g, :], W2[:, 0:T], start=False, stop=(e == 1), tile_position=(0, r0)), reads=[L2, W2], writes=[ybank])

                    sA, sB, sC = {}, {}, {}
                    for i in range(G + 3):
                        if i < G:
                            sA[i] = front(i)
                        if 0 <= i - 1 < G:
                            sB[i - 1] = stageB(i - 1, sA.pop(i - 1))
                        if 0 <= i - 2 < G:
                            sC[i - 2] = stageC(i - 2, sB.pop(i - 2))
                        if 0 <= i - 3 < G:
                            back(i - 3, *sC.pop(i - 3))
                        yield
                def tail_stream(xb, bi, T, par, dst_ap, outs):
                    uTp, sgsp = uT[par], sgs[par]
                    for c in range(4):
                        P.op("dve", (lambda c=c: V.scalar_tensor_tensor(ypre[:, c, 0:T], uTp[:, c, 0:T], DCOL.ap(0, 128, l * 4 + c, [[1, 1]]), ybank[:, c * 128:c * 128 + T], ALU.mult, ALU.add)), reads=[uTp, DCOL, ybank], writes=[ypre])
                    P.op("act", lambda: A.activation(out=v3(tt, T, 4), in_=v3(ypre, T, 4), func=AF.Square), reads=[ypre], writes=[tt])
                    P.op("dve", lambda: V.tensor_scalar(v3(tt, T, 4), v3(tt, T, 4), 0.044715, 1.0, ALU.mult, ALU.add), reads=[tt], writes=[tt])
                    P.op("dve", lambda: V.tensor_tensor(v3(tt, T, 4), v3(tt, T, 4), v3(ypre, T, 4), ALU.mult), reads=[tt, ypre], writes=[tt])
                    P.op("act", lambda: A.activation(out=v3(tt, T, 4), in_=v3(tt, T, 4), func=AF.Tanh, scale=C_GELU), reads=[tt], writes=[tt])
                    P.op("dve", lambda: V.scalar_tensor_tensor(v3(yg, T, 4), v3(tt, T, 4), 1.0, v3(ypre, T, 4), ALU.add, ALU.mult), reads=[tt, ypre], writes=[yg])
                    yield
                    if outs.get("st") is not None:
                        ci, si = outs["st_cs"]
                        rotate(I_HL, sm(I_GLAST), smb(I_GLAST), ci, si, GEN)
                        P.dma("st", lambda: nc.sync.dma_start(out=outs["st"], in_=sm(I_HL)), reads=smb(I_HL))
                    else:
                        rotate(I_INIT, sm(I_GLAST), smb(I_GLAST), I_CA, I_SA, GEN)
                    bva, bga = GEN2, ybank
                    for oc in (4, 5, 6, 7, 0, 1, 2, 3):
                        bk_ = bva if oc < 4 else bga
                        for kc in range(4):
                            P.op("pe", (lambda oc=oc, kc=kc, bk_=bk_: PE.matmul(bk_[:, (oc % 4) * 128:(oc % 4) * 128 + T], Wglu[:, kc, oc * 128:(oc + 1) * 128], yg[:, kc, 0:T], start=(kc == 0), stop=(kc == 3))), reads=[Wglu, yg], writes=[bk_])
                        if oc % 4 == 3:
                            yield
                    P.op("act", lambda: A.activation(out=v3(sg, T, 4), in_=pv3(bga, T), func=AF.Tanh, scale=0.5), reads=[bga], writes=[sg])
                    P.op("dve", lambda: V.scalar_tensor_tensor(v3(sg, T, 4), v3(sg, T, 4), 1.0, v3(sgsp, T, 4), ALU.add, ALU.mult), reads=[sg, sgsp], writes=[sg])
                    P.op("dve", lambda: V.tensor_tensor(v3(mixT, T, 4), pv3(bva, T), v3(sg, T, 4), ALU.mult), reads=[bva, sg], writes=mixb[0:4])
                    yield
                    bwa, bwb = wring.next(), wring.next()
                    for oc in range(NCH):
                        bk_ = bwa if oc < 4 else bwb
                        for kc in range(NCH):
                            P.op("pe", (lambda oc=oc, kc=kc, bk_=bk_: PE.matmul(bk_[:, (oc % 4) * 128:(oc % 4) * 128 + T], Wout[:, kc, oc * 128:(oc + 1) * 128], mixT[:, kc, 0:T], start=(kc == 0), stop=(kc == NCH - 1))), reads=[Wout, mixb[kc]], writes=[bk_])
                        if oc % 2 == 1:
                            yield
                    P.op("dve", lambda: V.tensor_tensor(v3(xb, T, 4), v3(xb, T, 4), pv3(bwa, T), ALU.add), reads=[xb, bwa], writes=[xb])
                    P.op("dve", lambda: V.tensor_tensor(v3(xb, T, 4, c0=4), v3(xb, T, 4, c0=4), pv3(bwb, T), ALU.add), reads=[xb, bwb], writes=[xb])
                    P.dma("y%d" % bi, lambda: nc.sync.dma_start(out=dst_ap, in_=v3(xb, T, NCH)), reads=[xb])

                def attn_stream(T, ti, par, gk):
                    qTp, sgap = qT[par], sga[par]
                    jl = [j for j in range(5) if ti - 4 + j >= 0]
                    items = [(h, j) for h in range(8) for j in jl]

                    def qk(h, j):
                        hp, hh = h // 2, h % 2
                        sl = (gk - 4 + j) % 6
                        nk = T if j == 4 else 128
                        bk_ = scr.next()
                        P.op("pe", lambda: PE.matmul(bk_[0:nk, 0:T], kwin.ap(64 * hh, 64, (hp * 6 + sl) * 128, [[1, nk]]), qTp.ap(64 * hh, 64, hp * 128, [[1, T]]), start=True, stop=True), reads=[kslot[sl], qTp], writes=[bk_])
                        return bk_, nk, sl

                    def soft(h, j, bk_, nk):
                        pT = pTr.next()
                        if j in (1, 2):
                            P.op("act", lambda: A.activation(out=pT[0:nk, 0:T], in_=bk_[0:nk, 0:T], func=AF.Exp, bias=col(8 + h, nk), scale=1.0), reads=[bk_, COLB], writes=[pT])
                        else:
                            stmp = stmpr.next()
                            if j == 0:
                                badd, bias_, rd = cM0(nk, T), col(8 + h, nk), [CST]
                            elif j == 3:
                                badd, bias_, rd = BT[0:nk, 8 + h, 0:T], col(6, nk), [BT]
                            else:
                                badd, bias_, rd = BT[0:nk, h, 0:T], col(6, nk), [BT]
                            P.op("dve", lambda: V.tensor_tensor(stmp[0:nk, 0:T], bk_[0:nk, 0:T], badd, ALU.add), reads=[bk_] + rd, writes=[stmp])
                            P.op("act", lambda: A.activation(out=pT[0:nk, 0:T], in_=stmp[0:nk, 0:T], func=AF.Exp, bias=bias_, scale=1.0), reads=[stmp, COLB], writes=[pT])
                        return pT

                    def pvmm(h, j, pT, nk, sl):
                        hp, hh = h // 2, h % 2
                        first, last = (j == jl[0]), (j == jl[-1])
                        osb = OS[hp % 2]
                        o_ = osb.ap(64 * hh, 64, 0, [[1, T]])
                        s_ = osb.ap(64 * hh, 64, 128, [[1, T]])
                        P.op("pe", lambda: PE.matmul(o_, vwin[0:nk, sl, h * 64:(h + 1) * 64], pT[0:nk, 0:T], start=first, stop=last, tile_position=(0, 64 * hh), skip_group_check=True), reads=[vslot[sl], pT], writes=[osb])
                        P.op("pe", lambda: PE.matmul(s_, ONE64B[0:nk, :], pT[0:nk, 0:T], start=False, stop=last, tile_position=(0, 64 * hh), skip_group_check=True), reads=[ONE64B, pT], writes=[osb])
                        if last and hh == 1:
                            rs = rsr.next()
                            P.op("dve", lambda: V.reciprocal(rs[:, 0:T], osb[:, 128:128 + T]), reads=[osb], writes=[rs])
                            P.op("pool", lambda: Pl.tensor_tensor(rs[:, 0:T], rs[:, 0:T], sgap[:, hp, 0:T], ALU.mult), reads=[rs, sgap], writes=[rs])
                            P.op("dve", lambda: V.tensor_tensor(mixT[:, 4 + hp, 0:T], osb[:, 0:T], rs[:, 0:T], ALU.mult), reads=[osb, rs], writes=[mixb[4 + hp]])

                    q = []
                    pre = 1
                    for idx in range(min(pre, len(items))):
                        h, j = items[idx]
                        q.append((h, j) + qk(h, j))
                    for idx in range(len(items)):
                        if idx + pre < len(items):
                            h2, j2 = items[idx + pre]
                            q.append((h2, j2) + qk(h2, j2))
                        h, j, bk_, nk, sl = q.pop(0)
                        pT = soft(h, j, bk_, nk)
                        pvmm(h, j, pT, nk, sl)
                        yield

                def run_streams(streams):
                    streams = list(streams)
                    while streams:
                        for s_ in list(streams):
                            try:
                                next(s_)
                            except StopIteration:
                                streams.remove(s_)

                def body(xb, bi, T, ti, par, dst_ap, outs, nxt=None, gk=4):
                    run_streams([s5_stream(T, par, outs), attn_stream(T, ti, par, gk)])
                    streams = [tail_stream(xb, bi, T, par, dst_ap, outs)]
                    if nxt is not None:
                        streams.append(nxt)
                    run_streams(streams)

                P.op("dve", lambda: V.memset(col(7), EPS), writes=[COLB])
                srcp, dstp = (xp, y1p) if l == 0 else (y1p, yp)
                srcs, dsts = (xs, y1s) if l == 0 else (y1s, ys)

                def p_outs(s, i):
                    outs = {}
                    if i >= NT - 4:
                        outs["nk"] = nkp[l, s, i - (NT - 4)]
                        outs["nv"] = nvp[l, s, (i - (NT - 4)) * 128:(i - (NT - 4) + 1) * 128, :]
                    if i == NT - 1:
                        outs["st"] = stp[l, s]
                        outs["st_cs"] = (I_CLP, I_SLP)
                    return outs

                tiles = [(s, i) for s in range(NSP) for i in range(NT)]
                load_x(srcp[0, 0], xT[0], 128, 0)
                run_streams([head_stream(xT[0], 128, 0, 0, p_outs(0, 0), 0)])
                for k, (s, i) in enumerate(tiles):
                    bi = k % 2
                    if i == 0:
                        P.op("dve", lambda: V.memset(sm(I_INIT), 0.0), writes=smb(I_INIT))
                    nxt = None
                    if k + 1 < len(tiles):
                        s2_, i2_ = tiles[k + 1]
                        load_x(srcp[s2_, i2_], xT[1 - bi], 128, 1 - bi)
                        nxt = head_stream(xT[1 - bi], 128, i2_, 1 - bi, p_outs(s2_, i2_), k + 1)
                    body(xT[bi], bi, 128, i, bi, dstp[s, i], p_outs(s, i), nxt, k)
                for s in range(NSS):
                    bi = s % 2
                    cstg = xT[1 - bi]
                    load_x(srcs[s], xT[bi], TS, bi)
                    for hf in range(2):
                        P.dma("cstg", (lambda s=s, hf=hf, cstg=cstg: nc.sync.dma_start(out=cstg.ap(0, 128, 0, [[512, 2], [1, 512]]), in_=ck[l, s, :, 2 * hf:2 * hf + 2, :])), writes=[cstg])
                        P.op("pool", (lambda hf=hf, cstg=cstg: Pl.tensor_copy(kwin.ap(0, 128, 2 * hf * 768, [[768, 2], [128, 4], [1, 128]]), cstg.ap(0, 128, 0, [[512, 2], [128, 4], [1, 128]]))), reads=[cstg], writes=kslot[0:4])
                    for hf in range(2):
                        P.dma("cstg", (lambda s=s, hf=hf, cstg=cstg: nc.sync.dma_start(out=cstg.ap(0, 128, 0, [[512, 2], [1, 512]]), in_=cv[l, s, hf * 256:(hf + 1) * 256, :].rearrange("(j k) e -> k j e", k=128))), writes=[cstg])
                        P.op("act", (lambda hf=hf, cstg=cstg: A.activation(func=AF.Copy, out=vwin[:, 2 * hf:2 * hf + 2, :], in_=cstg.ap(0, 128, 0, [[512, 2], [1, 512]]))), reads=[cstg], writes=vslot[0:4])
                    P.dma("st0", (lambda s=s: nc.sync.dma_start(out=sm(I_ST0), in_=st0[l, s])), writes=smb(I_ST0))
                    rotate(I_INIT, sm(I_ST0), smb(I_ST0), I_C1, I_S1)
                    outs = {"nk": nks[l, s], "nv": nvs[l, s], "st": sts[l, s], "st_cs": (I_CLS, I_SLS)}
                    run_streams([head_stream(xT[bi], TS, 4, bi, outs, 4)])
                    body(xT[bi], bi, TS, 4, bi, dsts[s], outs, None, 4)
                P.phase_end()
        print("[kernel] instructions:", P.n_inst, "semaphores:", P.nsem)
    return nc


def _consts():
    c = np.zeros((128, NCST), np.float32)
    idx = np.arange(128)
    c[idx, C_J + 127 - idx] = 1.0
    p = np.arange(64)
    c[64 + p, C_SWN + p] = -1.0
    c[p, C_SWN + 64 + p] = 1.0
    c[:, C_ONES:C_ONES + 128] = 1.0 / 1024.0
    c[0:64, C_BLK:C_BLK + 64] = 1.0 / 64.0
    c[64:128, C_BLK + 64:C_BLK + 128] = 1.0 / 64.0
    c[:, C_ONE64:C_ONE64 + 64] = 1.0
    c[64:128, C_M4:C_M4 + 64] = NEG
    c[0:64, C_M0 + 64:C_M0 + 128] = NEG
    c[:, C_JROW:C_JROW + 128] = np.arange(128, dtype=np.float32)[None]
    c[0:64, C_SH] = 1.0
    c[64:128, C_SH] = -1.0
    return c


_PROG_CACHE = {}


def kernel(x_prompt, x_sample, cache_k, cache_v, state_ssm_re, state_ssm_im, norm_gain, w_in,
           ssm_a_re, ssm_a_im, ssm_b_re, ssm_b_im, ssm_c_re, ssm_c_im, ssm_d, ssm_log_dt,
           w_glu, q_norm_gain, k_norm_gain, rel_bias, w_out, n_cores=8):
    f = np.float32
    x_prompt = np.asarray(x_prompt, f)
    x_sample = np.asarray(x_sample, f)
    B, SEQ, _ = x_prompt.shape
    BS = x_sample.shape[0]
    NSP, NSS = B // n_cores, BS // n_cores
    NT = SEQ // 128
    R = cache_k.shape[2]
    assert R == 512 and SEQ >= 512 and SEQ % 128 == 0 and x_sample.shape[1] == TS

    xpb = x_prompt.reshape(B, NT, 128, NCH, 128).transpose(0, 1, 4, 3, 2)
    xsb = x_sample.reshape(BS, TS, NCH, 128).transpose(0, 3, 2, 1)
    ckb = np.asarray(cache_k, f).reshape(L, BS, R, 4, 2, 64).transpose(0, 1, 4, 5, 3, 2).reshape(L, BS, 128, 4, R)
    cvb = np.asarray(cache_v, f).reshape(L, BS, R, 512)
    st = np.concatenate([np.asarray(state_ssm_re, f).transpose(0, 1, 3, 2), np.asarray(state_ssm_im, f).transpose(0, 1, 3, 2)], axis=2)
    ngh = np.asarray(norm_gain, f).reshape(L, NCH, 128).transpose(2, 0, 1)
    are = np.asarray(ssm_a_re, f).transpose(2, 0, 1)
    aim = np.asarray(ssm_a_im, f).transpose(2, 0, 1)
    ldt = np.broadcast_to(np.asarray(ssm_log_dt, f)[None], (64, L, G))
    s5 = np.stack([are, aim, ldt], axis=2)
    s5 = np.concatenate([s5, s5], axis=0)
    bre = np.asarray(ssm_b_re, f)
    bim = np.asarray(ssm_b_im, f)
    b1 = np.zeros((L, 128, 8, 128), f)
    b2 = np.zeros((L, 128, 8, 128), f)
    cre = np.asarray(ssm_c_re, f)
    cim = np.asarray(ssm_c_im, f)
    l1 = np.zeros((L, 128, G, 32), f)
    l2 = np.zeros((L, 128, G, 32), f)
    for g in range(G):
        c, band, e = g // 8, (g % 8) // 2, g % 2
        r0 = 32 * band + 16 * e
        b1[:, r0:r0 + 16, c * 2 + e, 0:64] = bre[:, g].transpose(0, 2, 1)
        b1[:, r0:r0 + 16, c * 2 + e, 64:128] = bim[:, g].transpose(0, 2, 1)
        b2[:, r0:r0 + 16, c * 2 + e, 0:64] = bim[:, g].transpose(0, 2, 1)
        b2[:, r0:r0 + 16, c * 2 + e, 64:128] = bre[:, g].transpose(0, 2, 1)
        l1[:, 0:64, g, 16 * e:16 * e + 16] = cre[:, g].transpose(0, 2, 1)
        l1[:, 64:128, g, 16 * e:16 * e + 16] = cim[:, g].transpose(0, 2, 1)
        l2[:, 0:64, g, 16 * e:16 * e + 16] = cim[:, g].transpose(0, 2, 1)
        l2[:, 64:128, g, 16 * e:16 * e + 16] = cre[:, g].transpose(0, 2, 1)
    dch = np.asarray(ssm_d, f).reshape(L, 4, 128).transpose(2, 0, 1)
    qg = np.asarray(q_norm_gain, f)
    kg = np.asarray(k_norm_gain, f)
    qkgh = np.stack([np.concatenate([qg, qg], 1), np.concatenate([kg, kg], 1)], axis=2).transpose(1, 0, 2)
    qkr = np.broadcast_to(np.concatenate([qg, kg], 1)[None], (128, L, 128))
    rbh = np.asarray(rel_bias, f)
    rbr = np.broadcast_to(rbh.reshape(1, L, 8 * 257), (128, L, 8 * 257))
    cst = _consts()

    key = (NT, NSP, NSS)
    nc = build_program(NT, NSP, NSS)
    shared = dict(w_in=np.ascontiguousarray(w_in, f), w_glu=np.ascontiguousarray(w_glu, f), w_out=np.ascontiguousarray(w_out, f),
                  ng=np.ascontiguousarray(ngh), s5a=np.ascontiguousarray(s5), b1h=b1.reshape(L, 128, 1024), b2h=b2.reshape(L, 128, 1024),
                  l1h=l1.reshape(L, 128, 1024), l2h=l2.reshape(L, 128, 1024), dcol=np.ascontiguousarray(dch),
                  qkg=np.ascontiguousarray(qkgh), qkrow=np.ascontiguousarray(qkr), rb=np.ascontiguousarray(rbh),
                  rbrep=np.ascontiguousarray(rbr), cst=cst)
    in_maps = []
    for c in range(n_cores):
        m = dict(shared)
        m["xp"] = np.ascontiguousarray(xpb[c * NSP:(c + 1) * NSP])
        m["xs"] = np.ascontiguousarray(xsb[c * NSS:(c + 1) * NSS])
        m["ck"] = np.ascontiguousarray(ckb[:, c * NSS:(c + 1) * NSS])
        m["cv"] = np.ascontiguousarray(cvb[:, c * NSS:(c + 1) * NSS])
        m["st0"] = np.ascontiguousarray(st[:, c * NSS:(c + 1) * NSS])
        in_maps.append(m)
    res = run_bass_kernel_spmd(nc, in_maps, core_ids=list(range(n_cores)))
    rs = res.results

    def cat(name, axis):
        return np.concatenate([np.asarray(r[name]) for r in rs], axis=axis)

    ypo = cat("yp", 0).transpose(0, 1, 4, 3, 2).reshape(B, SEQ, D)
    yso = cat("ys", 0).transpose(0, 3, 2, 1).reshape(BS, TS, D)
    nkpo = cat("nkp", 1)
    nkpo = nkpo.reshape(L, B, 4, 2, 64, 4, 128).transpose(0, 1, 2, 6, 5, 3, 4).reshape(L, B, 512, 8, 64)
    nvpo = cat("nvp", 1).reshape(L, B, 512, 8, 64)
    stpo = cat("stp", 1)
    pr = stpo[:, :, 0:64].transpose(0, 1, 3, 2)
    pi = stpo[:, :, 64:128].transpose(0, 1, 3, 2)
    nkso = cat("nks", 1).reshape(L, BS, 2, 64, 4, TS).transpose(0, 1, 5, 4, 2, 3).reshape(L, BS, TS, 8, 64)
    nvso = cat("nvs", 1).reshape(L, BS, TS, 8, 64)
    stso = cat("sts", 1)
    sr = stso[:, :, 0:64].transpose(0, 1, 3, 2)
    si = stso[:, :, 64:128].transpose(0, 1, 3, 2)
    c_ = lambda a: np.ascontiguousarray(a, dtype=np.float32)
    return (c_(ypo), c_(yso), c_(nkpo), c_(nvpo), c_(pr), c_(pi), c_(nkso), c_(nvso), c_(sr), c_(si))
```

```python
import math
from contextlib import ExitStack
import numpy as np
import concourse.bass as bass
import concourse.mybir as mybir
from concourse.bass_utils import run_bass_kernel_spmd

F32 = mybir.dt.float32
BF16 = mybir.dt.bfloat16
I32 = mybir.dt.int32
ALU = mybir.AluOpType
AF = mybir.ActivationFunctionType
AX = mybir.AxisListType

L = 2
D = 1024
NCH = 8
DIN = 3072
G = 32
TS = 16
EPS = 1e-6
NEG = -30000.0
TWO_PI = 2.0 * math.pi
STRICT = True

C_J, C_SWN, C_ONES, C_BLK, C_ONE64, C_M4, C_M0, C_JROW, C_SH = 0, 128, 256, 384, 512, 576, 704, 832, 960
NCST = 961


class Buf:
    def __init__(self, name):
        self.name = name
        self.w = None
        self.r = {}
        self.excl = False


class TT:
    def __init__(self, h, shape, dt, name, nbuf=None):
        self.h = h
        self.shape = shape
        self.row = int(np.prod(shape[1:]))
        self.dt = dt
        self.b = Buf(name)

    def __getitem__(self, idx):
        return self.h[idx]

    def ap(self, p0, npart, off, dims):
        return bass.AP(self.h, p0 * self.row + off, [[self.row, npart]] + [list(d) for d in dims])


class Op:
    __slots__ = ("eng", "fn", "deps", "needs_inc", "sem", "target", "is_dma", "epoch", "gen")


class Chan:
    def __init__(self, sem):
        self.sem = sem
        self.count = 0


class Prog:
    ENG = ("pe", "act", "dve", "pool", "sp")

    def __init__(self, nc, stack):
        self.nc = nc
        self.stack = stack
        self.e = {"pe": nc.tensor, "act": nc.scalar, "dve": nc.vector, "pool": nc.gpsimd, "sp": nc.sync}
        self.ops = []
        self.epoch = 0
        self.gen = 0
        self.sems = {}
        self.rank = {}
        self.seen = {}
        self.last = {}
        self.chans = {}
        self.nsem = 0
        self.n_inst = 0

    def _sem(self, name):
        self.nsem += 1
        return self.stack.enter_context(self.nc.semaphore(name))

    def chan(self, name):
        if name not in self.chans:
            self.chans[name] = Chan(self._sem("c_" + name))
        return self.chans[name]

    def _deps(self, eng, reads, writes):
        deps = []
        for b in reads:
            if b.w is not None:
                deps.append(b.w)
            if b.excl:
                for en_, o_ in b.r.items():
                    if en_ != eng:
                        deps.append(o_)
        for b in writes:
            if b.w is not None:
                deps.append(b.w)
            deps.extend(b.r.values())
        out = []
        for d in deps:
            if d.gen != self.gen:
                continue
            if (not d.is_dma) and d.eng == eng and (eng == "pe" or not STRICT):
                continue
            d.needs_inc = True
            out.append(d)
        return out

    def op(self, eng, fn, reads=(), writes=()):
        reads = [x.b if hasattr(x, 'b') else x for x in reads]
        writes = [x.b if hasattr(x, 'b') else x for x in writes]
        o = Op()
        o.eng = eng
        o.fn = fn
        o.is_dma = False
        o.needs_inc = False
        o.sem = None
        o.target = 0
        o.epoch = self.epoch
        o.gen = self.gen
        o.deps = self._deps(eng, reads, writes)
        for b in writes:
            b.w = o
            b.r = {}
        for b in reads:
            b.r[eng] = o
        self.ops.append(o)
        return o

    def dma(self, chan, fn, reads=(), writes=()):
        reads = [x.b if hasattr(x, 'b') else x for x in reads]
        writes = [x.b if hasattr(x, 'b') else x for x in writes]
        ch = self.chan(chan)
        o = Op()
        o.eng = "sp"
        o.fn = fn
        o.is_dma = True
        o.needs_inc = True
        ch.count += 1
        o.sem = ch.sem
        o.target = 16 * ch.count
        o.epoch = self.epoch
        o.gen = self.gen
        o.deps = self._deps("sp", reads, writes)
        for b in writes:
            b.w = o
            b.r = {}
        for b in reads:
            b.r["dma_" + chan] = o
        self.ops.append(o)
        return o

    def emit(self):
        for o in self.ops:
            e = self.e[o.eng]
            need = {}
            for d in o.deps:
                k = id(d.sem)
                if k not in need or need[k][1] < d.target:
                    need[k] = (d.sem, d.target)
            for k, (sem, tgt) in need.items():
                sk = (o.eng, k)
                if self.seen.get(sk, 0) >= tgt:
                    continue
                self.seen[sk] = tgt
                e.wait_ge(sem, tgt)
                self.n_inst += 1
            inst = o.fn()
            self.n_inst += 1
            if o.is_dma:
                inst.then_inc(o.sem, 16)
            else:
                self.last[o.eng] = o
                if o.needs_inc:
                    key = (o.eng, o.epoch)
                    if key not in self.sems:
                        self.sems[key] = self._sem("e_%s_%d" % key)
                        self.rank[key] = 0
                    self.rank[key] += 1
                    o.sem = self.sems[key]
                    o.target = self.rank[key]
                    inst.then_inc(o.sem, 1)
        self.ops = []

    def phase_end(self):
        lastops = {}
        for o in self.ops:
            if not o.is_dma:
                lastops[o.eng] = o
        for o in lastops.values():
            o.needs_inc = True
        self.emit()
        for en in self.ENG:
            e = self.e[en]
            for fn_, o in self.last.items():
                if fn_ == en or o.sem is None:
                    continue
                sk = (en, id(o.sem))
                if self.seen.get(sk, 0) >= o.target:
                    continue
                self.seen[sk] = o.target
                e.wait_ge(o.sem, o.target)
            for ch in self.chans.values():
                if ch.count == 0:
                    continue
                sk = (en, id(ch.sem))
                if self.seen.get(sk, 0) >= 16 * ch.count:
                    continue
                self.seen[sk] = 16 * ch.count
                e.wait_ge(ch.sem, 16 * ch.count)
        self.gen += 1
        self.epoch += 1


class PS:
    def __init__(self, bank, c0, name):
        self.bank = bank
        self.c0 = c0
        self.b = Buf(name)

    def ap(self, p0, n, off, dims):
        return self.bank.ap(p0, n, self.c0 + off, dims)

    def v(self, n, T, off=0, p0=0):
        return self.bank.ap(p0, n, self.c0 + off, [[1, T]])


class Ring:
    def __init__(self, items):
        self.items = items
        self.i = 0

    def next(self):
        x = self.items[self.i % len(self.items)]
        self.i += 1
        return x


def build_program(NT, NSP, NSS):
    nc = bass.Bass("TRN2", target_bir_lowering=False)

    def din(name, shape):
        return nc.dram_tensor(name, list(shape), F32, kind="ExternalInput").ap()

    def dout(name, shape):
        return nc.dram_tensor(name, list(shape), F32, kind="ExternalOutput").ap()

    xp = din("xp", [NSP, NT, 128, NCH, 128])
    xs = din("xs", [NSS, 128, NCH, TS])
    ck = din("ck", [L, NSS, 128, 4, 512])
    cv = din("cv", [L, NSS, 512, 512])
    st0 = din("st0", [L, NSS, 128, G])
    w_in = din("w_in", [L, D, DIN])
    w_glu = din("w_glu", [L, 512, 1024])
    w_out = din("w_out", [L, D, D])
    ng = din("ng", [128, L, NCH])
    s5a = din("s5a", [128, L, 3, G])
    b1h = din("b1h", [L, 128, 1024])
    b2h = din("b2h", [L, 128, 1024])
    l1h = din("l1h", [L, 128, 1024])
    l2h = din("l2h", [L, 128, 1024])
    dcol = din("dcol", [128, L, 4])
    qkg = din("qkg", [128, L, 2])
    qkrow = din("qkrow", [128, L, 128])
    rb = din("rb", [L, 8, 257])
    rbrep = din("rbrep", [128, L, 8 * 257])
    cst = din("cst", [128, NCST])

    yp = dout("yp", [NSP, NT, 128, NCH, 128])
    ys = dout("ys", [NSS, 128, NCH, TS])
    nkp = dout("nkp", [L, NSP, 4, 128, 4, 128])
    nvp = dout("nvp", [L, NSP, 512, 512])
    stp = dout("stp", [L, NSP, 128, G])
    nks = dout("nks", [L, NSS, 128, 4, TS])
    nvs = dout("nvs", [L, NSS, TS, 512])
    sts = dout("sts", [L, NSS, 128, G])
    y1p = nc.dram_tensor("y1p", [NSP, NT, 128, NCH, 128], F32, kind="Internal").ap()
    y1s = nc.dram_tensor("y1s", [NSS, 128, NCH, TS], F32, kind="Internal").ap()
    ext = nc.dram_tensor("ext", [L, 8, 384], F32, kind="Internal").ap()

    with ExitStack() as glob:
        P = Prog(nc, glob)

        uniq = [0]

        def sb(stack, name, shape, dt=F32):
            uniq[0] += 1
            name = "%s_%d" % (name, uniq[0])
            h = stack.enter_context(nc.sbuf_tensor(name, list(shape), dt))
            return TT(h, list(shape), dt, name)

        def ps(stack, name):
            h = stack.enter_context(nc.psum_tensor(name, [128, 512], F32))
            t_ = TT(h, [128, 512], F32, name)
            t_.b.excl = True
            return t_

        Win = sb(glob, "Win", [128, NCH, DIN], BF16)
        Wglu = sb(glob, "Wglu", [128, 4, 1024], BF16)
        Wout = sb(glob, "Wout", [128, NCH, D], BF16)
        PRE1 = sb(glob, "PRE1", [128, G, 128], BF16)
        PRE2 = sb(glob, "PRE2", [128, G, 128], BF16)
        POST1 = sb(glob, "POST1", [128, G, 128], BF16)
        POST2 = sb(glob, "POST2", [128, G, 128], BF16)
        B1 = sb(glob, "B1", [128, 8, 128], BF16)
        B2 = sb(glob, "B2", [128, 8, 128], BF16)
        L1 = sb(glob, "L1", [128, G, 32], BF16)
        L2 = sb(glob, "L2", [128, G, 32], BF16)
        CST = sb(glob, "CST", [128, NCST], F32)
        ONESB = sb(glob, "ONESB", [128, 128], BF16)
        BLKB = sb(glob, "BLKB", [128, 128], BF16)
        ONE64B = sb(glob, "ONE64B", [128, 64], BF16)
        BT = sb(glob, "BT", [128, 16, 128], F32)
        kwin = sb(glob, "kwin", [128, 4, 6, 128], BF16)
        vwin = sb(glob, "vwin", [128, 6, 512], BF16)
        kslot = [Buf("ks%d" % i) for i in range(6)]
        vslot = [Buf("vs%d" % i) for i in range(6)]
        SM = sb(glob, "SM", [128, 40, G], F32)
        SMB = [Buf("sm%d" % i) for i in range(40)]
        COL = sb(glob, "COL", [128, 64], F32)
        COLB = Buf("col")
        NGs = sb(glob, "NGs", [128, L, NCH], F32)
        S5A = sb(glob, "S5A", [128, L, 3, G], F32)
        DCOL = sb(glob, "DCOL", [128, L, 4], F32)
        QKG = sb(glob, "QKG", [128, L, 2], F32)
        banks = [ps(glob, "pb%d" % i) for i in range(8)]
        ybank = banks[0]
        pqr = Ring(banks[1:3])
        scr = Ring(banks[3:7])
        OS = [banks[7], banks[7]]
        GEN = banks[1]
        GEN2 = banks[2]
        prot = Ring(banks[1:8])

        (I_ARE, I_AIM, I_LDT, I_DT, I_R, I_TH, I_S1, I_C1, I_ABRE, I_ABIM, I_NR, I_DEN, I_FRE, I_FIM,
         I_CA, I_SA, I_CLP, I_SLP, I_CLS, I_SLS, I_INIT, I_GLAST, I_T0, I_T1, I_T2, I_T3, I_T4, I_T5, I_T6,
         I_T7, I_HL, I_ST0, I_NFIM) = range(33)

        def sm(i, n=128):
            return SM.ap(0, n, i * G, [[1, G]])

        def col(i, n=128, p0=0):
            return COL.ap(p0, n, i, [[1, 1]])

        cJ = CST.ap(0, 128, C_J, [[1, 128]])
        cSWN = CST.ap(0, 128, C_SWN, [[1, 128]])
        cM4 = lambda nk, T: CST.ap(0, nk, C_M4, [[1, T]])
        cM0 = lambda nk, T: CST.ap(0, nk, C_M0, [[1, T]])
        cSH = CST.ap(0, 128, C_SH, [[1, 1]])

        V, A, Pl, PE = nc.vector, nc.scalar, nc.gpsimd, nc.tensor

        P.dma("cst", lambda: nc.sync.dma_start(out=CST[:], in_=cst[:]), writes=[CST])
        P.dma("cst2", lambda: nc.sync.dma_start(out=NGs[:], in_=ng[:]), writes=[NGs])
        P.dma("cst3", lambda: nc.sync.dma_start(out=S5A[:], in_=s5a[:]), writes=[S5A])
        P.dma("cst4", lambda: nc.sync.dma_start(out=DCOL[:], in_=dcol[:]), writes=[DCOL])
        P.dma("cst5", lambda: nc.sync.dma_start(out=QKG[:], in_=qkg[:]), writes=[QKG])
        P.op("dve", lambda: V.tensor_copy(ONESB[:], CST.ap(0, 128, C_ONES, [[1, 128]])), reads=[CST], writes=[ONESB])
        P.op("dve", lambda: V.tensor_copy(BLKB[:], CST.ap(0, 128, C_BLK, [[1, 128]])), reads=[CST], writes=[BLKB])
        P.op("dve", lambda: V.tensor_copy(ONE64B[:], CST.ap(0, 128, C_ONE64, [[1, 64]])), reads=[CST], writes=[ONE64B])
        P.phase_end()

        for l in range(L):
            with ExitStack() as s1:
                stg = [sb(s1, "stg%d" % i, [128, 1536], F32) for i in range(2)]
                tmpN = 256
                tb = [sb(s1, "tb%d" % i, [128, tmpN], F32) for i in range(8)]
                tbi = sb(s1, "tbi", [128, tmpN], I32)
                HK = sb(s1, "HK", [128, 16, 128], F32)
                RBR = sb(s1, "RBR", [128, 8 * 257], F32)
                QKR = sb(s1, "QKR", [128, 128], F32)
                EXS = sb(s1, "EXS", [8, 384], F32)

                def sa(i):
                    return S5A.ap(0, 128, (l * 3 + i) * G, [[1, G]])

                def sincos(ang, n, out_sin, out_cos, rd, wr):
                    t, kf, d, m = (tb[4].ap(0, 128, 0, [[1, n]]), tb[5].ap(0, 128, 0, [[1, n]]),
                                   tb[6].ap(0, 128, 0, [[1, n]]), tb[7].ap(0, 128, 0, [[1, n]]))
                    ki = tbi.ap(0, 128, 0, [[1, n]])
                    for off, outp in ((0.0, out_sin), (0.25, out_cos)):
                        P.op("dve", (lambda off=off: V.tensor_scalar(t, ang, 1.0 / TWO_PI, off, ALU.mult, ALU.add)), reads=rd + [tb[4]], writes=[tb[4]])
                        P.op("dve", lambda: V.tensor_copy(ki, t), reads=[tb[4]], writes=[tbi])
                        P.op("dve", lambda: V.tensor_copy(kf, ki), reads=[tbi], writes=[tb[5]])
                        P.op("dve", lambda: V.tensor_tensor(d, t, kf, ALU.subtract), reads=[tb[4], tb[5]], writes=[tb[6]])
                        P.op("dve", lambda: V.tensor_scalar(m, d, 0.5, None, ALU.is_gt), reads=[tb[6]], writes=[tb[7]])
                        P.op("dve", lambda: V.tensor_tensor(d, d, m, ALU.subtract), reads=[tb[6], tb[7]], writes=[tb[6]])
                        P.op("dve", lambda: V.tensor_scalar(m, d, -0.5, None, ALU.is_lt), reads=[tb[6]], writes=[tb[7]])
                        P.op("dve", lambda: V.tensor_tensor(d, d, m, ALU.add), reads=[tb[6], tb[7]], writes=[tb[6]])
                        P.op("act", (lambda outp=outp: A.activation(out=outp, in_=d, func=AF.Sin, scale=TWO_PI * (1.0 - 2e-6))), reads=[tb[6]], writes=wr)

                smb = lambda *ids: [SMB[i] for i in ids]
                P.op("act", lambda: A.activation(out=sm(I_DT), in_=sa(2), func=AF.Exp), reads=[S5A], writes=smb(I_DT))
                P.op("dve", lambda: V.tensor_tensor(sm(I_T0), sm(I_DT), sa(0), ALU.mult), reads=[S5A] + smb(I_DT), writes=smb(I_T0))
                P.op("act", lambda: A.activation(out=sm(I_R), in_=sm(I_T0), func=AF.Exp), reads=smb(I_T0), writes=smb(I_R))
                P.op("dve", lambda: V.tensor_tensor(sm(I_TH), sm(I_DT), sa(1), ALU.mult), reads=[S5A] + smb(I_DT), writes=smb(I_TH))
                sincos(sm(I_TH), G, sm(I_S1), sm(I_C1), smb(I_TH), smb(I_S1, I_C1))
                for mult_, (ci, si) in ((128.0, (I_CA, I_SA)), (127.0, (I_CLP, I_SLP)), (float(TS - 1), (I_CLS, I_SLS))):
                    P.op("dve", (lambda mult_=mult_: V.tensor_scalar(sm(I_T1), sm(I_TH), mult_, None, ALU.mult)), reads=smb(I_TH), writes=smb(I_T1))
                    sincos(sm(I_T1), G, sm(si), sm(ci), smb(I_T1), smb(si, ci))
                P.op("dve", lambda: V.tensor_tensor(sm(I_ABRE), sm(I_R), sm(I_C1), ALU.mult), reads=smb(I_R, I_C1), writes=smb(I_ABRE))
                P.op("dve", lambda: V.tensor_tensor(sm(I_ABIM), sm(I_R), sm(I_S1), ALU.mult), reads=smb(I_R, I_S1), writes=smb(I_ABIM))
                P.op("dve", lambda: V.tensor_scalar(sm(I_NR), sm(I_ABRE), -1.0, None, ALU.add), reads=smb(I_ABRE), writes=smb(I_NR))
                P.op("dve", lambda: V.tensor_tensor(sm(I_T0), sa(0), sa(0), ALU.mult), reads=[S5A], writes=smb(I_T0))
                P.op("dve", lambda: V.tensor_tensor(sm(I_T1), sa(1), sa(1), ALU.mult), reads=[S5A], writes=smb(I_T1))
                P.op("dve", lambda: V.tensor_tensor(sm(I_DEN), sm(I_T0), sm(I_T1), ALU.add), reads=smb(I_T0, I_T1), writes=smb(I_DEN))
                P.op("dve", lambda: V.reciprocal(sm(I_DEN), sm(I_DEN)), reads=smb(I_DEN), writes=smb(I_DEN))
                P.op("dve", lambda: V.tensor_tensor(sm(I_T0), sm(I_NR), sa(0), ALU.mult), reads=[S5A] + smb(I_NR), writes=smb(I_T0))
                P.op("dve", lambda: V.tensor_tensor(sm(I_T1), sm(I_ABIM), sa(1), ALU.mult), reads=[S5A] + smb(I_ABIM), writes=smb(I_T1))
                P.op("dve", lambda: V.tensor_tensor(sm(I_T0), sm(I_T0), sm(I_T1), ALU.add), reads=smb(I_T0, I_T1), writes=smb(I_T0))
                P.op("dve", lambda: V.tensor_tensor(sm(I_FRE), sm(I_T0), sm(I_DEN), ALU.mult), reads=smb(I_T0, I_DEN), writes=smb(I_FRE))
                P.op("dve", lambda: V.tensor_tensor(sm(I_T2), sm(I_ABIM), sa(0), ALU.mult), reads=[S5A] + smb(I_ABIM), writes=smb(I_T2))
                P.op("dve", lambda: V.tensor_tensor(sm(I_T3), sm(I_NR), sa(1), ALU.mult), reads=[S5A] + smb(I_NR), writes=smb(I_T3))
                P.op("dve", lambda: V.tensor_tensor(sm(I_T2), sm(I_T2), sm(I_T3), ALU.subtract), reads=smb(I_T2, I_T3), writes=smb(I_T2))
                P.op("dve", lambda: V.tensor_tensor(sm(I_FIM), sm(I_T2), sm(I_DEN), ALU.mult), reads=smb(I_T2, I_DEN), writes=smb(I_FIM))
                P.op("dve", lambda: V.tensor_scalar(sm(I_NFIM), sm(I_FIM), -1.0, None, ALU.mult), reads=smb(I_FIM), writes=smb(I_NFIM))

                def wload():
                    wi = 0
                    for kc2 in range(2 * NCH):
                        kc, hf = kc2 // 2, kc2 % 2
                        st = stg[wi % 2]
                        P.dma("w%d" % (wi % 2), (lambda st=st, kc=kc, hf=hf: nc.sync.dma_start(out=st[:], in_=w_in[l, kc * 128:(kc + 1) * 128, hf * 1536:(hf + 1) * 1536])), writes=[st])
                        gcol = NGs.ap(0, 128, l * NCH + kc, [[1, 1]])
                        P.op("act", (lambda st=st, kc=kc, hf=hf, gcol=gcol: A.activation(out=Win[:, kc, hf * 1536:(hf + 1) * 1536], in_=st[:], func=AF.Identity, scale=gcol)), reads=[st, NGs], writes=[Win])
                        wi += 1
                        yield
                    for kc in range(4):
                        st = stg[wi % 2]
                        P.dma("w%d" % (wi % 2), (lambda st=st, kc=kc: nc.sync.dma_start(out=st[:, 0:1024], in_=w_glu[l, kc * 128:(kc + 1) * 128, :])), writes=[st])
                        P.op("act", (lambda st=st, kc=kc: A.activation(out=Wglu[:, kc, :], in_=st[:, 0:1024], func=AF.Copy, scale=0.5)), reads=[st], writes=[Wglu])
                        wi += 1
                        yield
                    for kc in range(NCH):
                        st = stg[wi % 2]
                        P.dma("w%d" % (wi % 2), (lambda st=st, kc=kc: nc.sync.dma_start(out=st[:, 0:1024], in_=w_out[l, kc * 128:(kc + 1) * 128, :])), writes=[st])
                        wsc = 0.25 if kc < 4 else 0.5
                        P.op("act", (lambda st=st, kc=kc, wsc=wsc: A.activation(out=Wout[:, kc, :], in_=st[:, 0:1024], func=AF.Copy, scale=wsc)), reads=[st], writes=[Wout])
                        wi += 1
                        yield
                    for src, dst in ((b1h, B1), (b2h, B2), (l1h, L1), (l2h, L2)):
                        st = stg[wi % 2]
                        P.dma("w%d" % (wi % 2), (lambda st=st, src=src: nc.sync.dma_start(out=st[:, 0:1024], in_=src[l])), writes=[st])
                        P.op("pool", (lambda st=st, dst=dst: Pl.tensor_copy(dst.ap(0, 128, 0, [[1, 1024]]), st[:, 0:1024])), reads=[st], writes=[dst])
                        wi += 1
                        yield


                def tabgen():
                    NB = 2
                    for gb in range(G // NB):
                        ang = tb[0].ap(0, 128, 0, [[1, NB * 128]])
                        sinb = tb[1].ap(0, 128, 0, [[1, NB * 128]])
                        cosb = tb[2].ap(0, 128, 0, [[1, NB * 128]])
                        a3 = tb[3].ap(0, 128, 0, [[128, NB], [1, 128]])
                        ang3 = tb[0].ap(0, 128, 0, [[128, NB], [1, 128]])
                        sin3 = tb[1].ap(0, 128, 0, [[128, NB], [1, 128]])
                        cos3 = tb[2].ap(0, 128, 0, [[128, NB], [1, 128]])
                        thb = SM.ap(0, 128, I_TH * G + gb * NB, [[1, NB], [0, 128]])
                        freb = SM.ap(0, 128, I_FRE * G + gb * NB, [[1, NB], [0, 128]])
                        fimb = SM.ap(0, 128, I_FIM * G + gb * NB, [[1, NB], [0, 128]])
                        nfimb = SM.ap(0, 128, I_NFIM * G + gb * NB, [[1, NB], [0, 128]])
                        jb = CST.ap(0, 128, C_JROW, [[0, NB], [1, 128]])
                        dst = lambda Tn: Tn.ap(0, 128, gb * NB * 128, [[128, NB], [1, 128]])
                        P.op("dve", (lambda ang3=ang3, thb=thb, jb=jb: V.tensor_tensor(ang3, thb, jb, ALU.mult)), reads=[CST] + smb(I_TH), writes=[tb[0]])
                        sincos(ang, NB * 128, sinb, cosb, [tb[0]], [tb[1], tb[2]])
                        P.op("dve", (lambda a3=a3, cos3=cos3, freb=freb: V.tensor_tensor(a3, cos3, freb, ALU.mult)), reads=[tb[2]] + smb(I_FRE), writes=[tb[3]])
                        P.op("dve", (lambda ang3=ang3, sin3=sin3, fimb=fimb: V.tensor_tensor(ang3, sin3, fimb, ALU.mult)), reads=[tb[1]] + smb(I_FIM), writes=[tb[0]])
                        P.op("dve", (lambda a3=a3, ang3=ang3, d_=dst(PRE1): V.tensor_tensor(d_, a3, ang3, ALU.add)), reads=[tb[3], tb[0]], writes=[PRE1])
                        P.op("dve", (lambda a3=a3, sin3=sin3, freb=freb: V.tensor_tensor(a3, sin3, freb, ALU.mult)), reads=[tb[1]] + smb(I_FRE), writes=[tb[3]])
                        P.op("dve", (lambda ang3=ang3, cos3=cos3, nfimb=nfimb: V.tensor_tensor(ang3, cos3, nfimb, ALU.mult)), reads=[tb[2]] + smb(I_NFIM), writes=[tb[0]])
                        P.op("dve", (lambda a3=a3, ang3=ang3: V.tensor_tensor(a3, a3, ang3, ALU.add)), reads=[tb[3], tb[0]], writes=[tb[3]])
                        P.op("dve", (lambda a3=a3, d_=dst(PRE2): V.tensor_scalar(d_, a3, cSH, None, ALU.mult)), reads=[tb[3], CST], writes=[PRE2])
                        P.op("dve", (lambda cos3=cos3, d_=dst(POST1): V.tensor_scalar(d_, cos3, cSH, None, ALU.mult)), reads=[tb[2], CST], writes=[POST1])
                        P.op("dve", (lambda sin3=sin3, d_=dst(POST2): V.tensor_scalar(d_, sin3, -1.0, None, ALU.mult)), reads=[tb[1]], writes=[POST2])
                        yield


                wg_, tg_ = wload(), tabgen()
                alive = [wg_, tg_]
                while alive:
                    for g_, n_ in ((tg_, 1), (wg_, 3)):
                        if g_ in alive:
                            for _ in range(n_):
                                try:
                                    next(g_)
                                except StopIteration:
                                    alive.remove(g_)
                                    break
                P.dma("rbr", lambda: nc.sync.dma_start(out=RBR[:], in_=rbrep[:, l, :]), writes=[RBR])
                P.dma("qkr", lambda: nc.sync.dma_start(out=QKR[:], in_=qkrow[:, l, :]), writes=[QKR])
                P.dma("exs", lambda: nc.sync.dma_start(out=EXS[0:8, 0:257], in_=rb[l]), writes=[EXS])
                P.op("dve", lambda: V.tensor_copy(EXS[0:8, 257:384], EXS[0:8, 256:257].to_broadcast([8, 127])), reads=[EXS], writes=[EXS])
                P.dma("exd", lambda: nc.sync.dma_start(out=ext[l], in_=EXS[0:8, :]), reads=[EXS], writes=[HK])
                for i in range(16):
                    h = i % 8
                    off = 1 if i < 8 else 129
                    src = bass.AP(ext.tensor, (l * 8 + h) * 384 + off, [[1, 128], [1, 128]])
                    P.dma("hk", (lambda i=i, src=src: nc.sync.dma_start(out=HK[:, i, :], in_=src)), reads=[HK], writes=[HK])
                P.op("act", lambda: A.activation(out=RBR[:], in_=RBR[:], func=AF.Abs), reads=[RBR], writes=[RBR])
                P.op("dve", lambda: V.reduce_max(col(4), RBR[:], AX.X), reads=[RBR], writes=[COLB])
                P.op("act", lambda: A.activation(out=QKR[:], in_=QKR[:], func=AF.Abs), reads=[QKR], writes=[QKR])
                P.op("dve", lambda: V.reduce_max(col(2), QKR[:, 0:64], AX.X), reads=[QKR], writes=[COLB])
                P.op("dve", lambda: V.reduce_max(col(3), QKR[:, 64:128], AX.X), reads=[QKR], writes=[COLB])
                P.op("dve", lambda: V.tensor_tensor(col(5), col(2), col(3), ALU.mult), reads=[COLB], writes=[COLB])
                P.op("dve", lambda: V.tensor_scalar(col(5), col(5), 8.0, col(4), ALU.mult, ALU.add), reads=[COLB], writes=[COLB])
                P.op("dve", lambda: V.tensor_scalar(col(6), col(5), -1.0, None, ALU.mult), reads=[COLB], writes=[COLB])
                P.dma("rbr", lambda: nc.sync.dma_start(out=RBR[:], in_=rbrep[:, l, :]), reads=[RBR], writes=[RBR])
                P.op("dve", lambda: V.tensor_scalar(COL.ap(0, 128, 8, [[1, 8]]), RBR.ap(0, 128, 256, [[257, 8]]), col(5), None, ALU.subtract), reads=[RBR, COLB], writes=[COLB])
                P.op("dve", lambda: V.tensor_scalar(col(0), QKG.ap(0, 128, l * 2, [[1, 1]]), 0.125, None, ALU.mult), reads=[QKG], writes=[COLB])
                P.op("dve", lambda: V.tensor_copy(col(1), QKG.ap(0, 128, l * 2 + 1, [[1, 1]])), reads=[QKG], writes=[COLB])
                for i in range(16):
                    bk = prot.next()
                    P.op("pe", (lambda i=i, bk=bk: PE.matmul(bk[:, 0:128], cJ, HK[:, i, :], start=True, stop=True)), reads=[CST, HK], writes=[bk])
                    if i < 8:
                        P.op("dve", (lambda i=i, bk=bk: V.tensor_tensor(BT[:, i, :], bk[:, 0:128], cM4(128, 128), ALU.add)), reads=[bk, CST], writes=[BT])
                    else:
                        P.op("dve", (lambda i=i, bk=bk: V.tensor_copy(BT[:, i, :], bk[:, 0:128])), reads=[bk], writes=[BT])
                P.phase_end()

            with ExitStack() as s2:
                xT = [sb(s2, "xT%d" % i, [128, NCH, 128], F32) for i in range(2)]
                sq = sb(s2, "sq", [128, NCH, 128], BF16)
                hT = sb(s2, "hT", [128, NCH, 128], BF16)
                uT = [sb(s2, "uT%d" % i, [128, 4, 128], BF16) for i in range(2)]
                sgs = [sb(s2, "sgs%d" % i, [128, 4, 128], F32) for i in range(2)]
                sga = [sb(s2, "sga%d" % i, [128, 4, 128], F32) for i in range(2)]
                qT = [sb(s2, "qT%d" % i, [128, 4, 128], BF16) for i in range(2)]
                qsq = sb(s2, "qsq", [128, 2, 128], BF16)
                rstd = sb(s2, "rstd", [128, 128], F32)
                rq = sb(s2, "rq", [128, 2, 128], F32)
                kn32 = sb(s2, "kn32", [128, 4, 128], F32)
                v32 = sb(s2, "v32", [128, 512], F32)
                t1r = Ring([sb(s2, "t1_%d" % i, [128, 128], F32) for i in range(2)])
                t2r = Ring([sb(s2, "t2_%d" % i, [128, 128], F32) for i in range(2)])
                btr = Ring([sb(s2, "bt_%d" % i, [128, 128], F32) for i in range(2)])
                Gr = Ring([sb(s2, "G_%d" % i, [128, 128], F32) for i in range(2)])
                W1r = Ring([sb(s2, "W1_%d" % i, [128, 128], BF16) for i in range(2)])
                W2r = Ring([sb(s2, "W2_%d" % i, [128, 128], BF16) for i in range(2)])
                ypre = sb(s2, "ypre", [128, 4, 128], F32)
                tt = sb(s2, "tt", [128, 4, 128], F32)
                yg = sb(s2, "yg", [128, 4, 128], BF16)
                sg = sb(s2, "sg", [128, 4, 128], F32)
                mixT = sb(s2, "mixT", [128, NCH, 128], BF16)
                mixb = [Buf("mix%d" % i) for i in range(NCH)]
                stmpr = Ring([sb(s2, "stmp%d" % i, [128, 128], F32) for i in range(4)])
                pTr = Ring([sb(s2, "pT%d" % i, [128, 128], BF16) for i in range(6)])
                rsr = Ring([sb(s2, "rs%d" % i, [128, 128], F32) for i in range(2)])
                hring = Ring([banks[5], banks[6], banks[7]])
                wring = Ring(banks[3:5])
                C_GELU = math.sqrt(2.0 / math.pi)

                def v3(t, T, nchunk, c0=0, n=128, p0=0):
                    return t.ap(p0, n, c0 * 128, [[128, nchunk], [1, T]])

                def pv3(bk, T, nchunk=4, c0=0):
                    return bk.ap(0, 128, c0 * 128, [[128, nchunk], [1, T]])

                def rotate(dst_i, src_ap, src_bufs, ci, si, bk=None):
                    bk = bk if bk is not None else prot.next()
                    P.op("pe", lambda: PE.matmul(bk[:, 0:G], cSWN, src_ap, start=True, stop=True), reads=[CST] + src_bufs, writes=[bk])
                    P.op("dve", lambda: V.tensor_tensor(sm(I_T6), sm(ci), src_ap, ALU.mult), reads=smb(ci) + src_bufs, writes=smb(I_T6))
                    P.op("dve", lambda: V.tensor_tensor(sm(I_T7), sm(si), bk[:, 0:G], ALU.mult), reads=smb(si) + [bk], writes=smb(I_T7))
                    P.op("dve", lambda: V.tensor_tensor(sm(dst_i), sm(I_T6), sm(I_T7), ALU.add), reads=smb(I_T6, I_T7), writes=smb(dst_i))

                def load_x(src_ap, xb, T, bi):
                    P.dma("x%d" % bi, lambda: nc.sync.dma_start(out=v3(xb, T, NCH), in_=src_ap), writes=[xb])

                def rsqrt_act(dst_ap, src_ap, rd, wr):
                    P.op("act", lambda: A.activation(out=dst_ap, in_=src_ap, func=AF.Ln, bias=col(7), scale=1.0), reads=rd + [COLB], writes=wr)
                    P.op("act", lambda: A.activation(out=dst_ap, in_=dst_ap, func=AF.Exp, scale=-0.5), reads=wr, writes=wr)

                def head_stream(xb, T, ti, par, outs, gk):
                    slot = gk % 6
                    uTp, qTp, sgsp, sgap = uT[par], qT[par], sgs[par], sga[par]
                    P.op("act", lambda: A.activation(out=v3(sq, T, NCH), in_=v3(xb, T, NCH), func=AF.Square), reads=[xb], writes=[sq])
                    HB = hring.next()
                    for c in range(NCH):
                        P.op("pe", (lambda c=c, HB=HB: PE.matmul(HB[:, 0:T], ONESB[:], sq[:, c, 0:T], start=(c == 0), stop=(c == NCH - 1))), reads=[ONESB, sq], writes=[HB])
                    rsqrt_act(rstd[:, 0:T], HB[:, 0:T], [HB], [rstd])
                    P.op("dve", lambda: V.tensor_tensor(v3(hT, T, NCH), v3(xb, T, NCH), rstd.ap(0, 128, 0, [[0, NCH], [1, T]]), ALU.mult), reads=[xb, rstd], writes=[hT])
                    yield

                    def win_group(HB, cb, nch=4, c0=0):
                        for c in range(nch):
                            for kc in range(NCH):
                                P.op("pe", (lambda c=c, kc=kc, HB=HB: PE.matmul(HB[:, (c0 + c) * 128:(c0 + c) * 128 + T], Win[:, kc, cb + c * 128:cb + (c + 1) * 128], hT[:, kc, 0:T], start=(kc == 0), stop=(kc == NCH - 1))), reads=[Win, hT], writes=[HB])
                            if c % 2 == 1:
                                yield

                    for hf in range(2):
                        HB = hring.next()
                        yield from win_group(HB, 1024 + hf * 256, nch=2)
                        P.op("act", lambda HB=HB: A.activation(out=v3(qsq, T, 2), in_=pv3(HB, T, 2), func=AF.Square), reads=[HB], writes=[qsq])
                        for c in range(2):
                            P.op("pe", (lambda c=c, HB=HB: PE.matmul(HB[:, (2 + c) * 128:(2 + c) * 128 + T], BLKB[:], qsq[:, c, 0:T], start=True, stop=True)), reads=[BLKB, qsq], writes=[HB])
                        rsqrt_act(v3(rq, T, 2), pv3(HB, T, 2, c0=2), [HB], [rq])
                        P.op("dve", (lambda hf=hf, HB=HB: V.scalar_tensor_tensor(v3(qTp, T, 2, c0=2 * hf), pv3(HB, T, 2), col(0), v3(rq, T, 2), ALU.mult, ALU.mult)), reads=[HB, rq, COLB], writes=[qTp])
                        yield
                    for hf in range(2):
                        HB = hring.next()
                        yield from win_group(HB, 1536 + hf * 256, nch=2)
                        P.op("act", lambda HB=HB: A.activation(out=v3(qsq, T, 2), in_=pv3(HB, T, 2), func=AF.Square), reads=[HB], writes=[qsq])
                        for c in range(2):
                            P.op("pe", (lambda c=c, HB=HB: PE.matmul(HB[:, (2 + c) * 128:(2 + c) * 128 + T], BLKB[:], qsq[:, c, 0:T], start=True, stop=True)), reads=[BLKB, qsq], writes=[HB])
                        rsqrt_act(v3(rq, T, 2), pv3(HB, T, 2, c0=2), [HB], [rq])
                        P.op("dve", (lambda hf=hf, HB=HB: V.scalar_tensor_tensor(v3(kn32, T, 2, c0=2 * hf), pv3(HB, T, 2), col(1), v3(rq, T, 2), ALU.mult, ALU.mult)), reads=[HB, rq, COLB], writes=[kn32])
                        yield
                    P.op("pool", lambda: Pl.tensor_copy(kwin.ap(0, 128, slot * 128, [[768, 4], [1, T]]), v3(kn32, T, 4)), reads=[kn32], writes=[kslot[slot]])
                    if outs.get("nk") is not None:
                        P.dma("nk", lambda: nc.sync.dma_start(out=outs["nk"], in_=v3(kn32, T, 4)), reads=[kn32])
                    HB = hring.next()
                    yield from win_group(HB, 0)
                    P.op("act", lambda HB=HB: A.activation(func=AF.Copy, out=v3(uTp, T, 4), in_=pv3(HB, T)), reads=[HB], writes=[uTp])
                    HB = hring.next()
                    for kc in range(NCH):
                        P.op("pe", (lambda kc=kc, HB=HB: PE.matmul(HB[0:T, :], hT[:, kc, 0:T], Win[:, kc, 2048:2560], start=(kc == 0), stop=(kc == NCH - 1))), reads=[Win, hT], writes=[HB])
                    P.op("act", lambda HB=HB: A.activation(func=AF.Copy, out=vwin[0:T, slot, :], in_=HB[0:T, :]), reads=[HB], writes=[vslot[slot]])
                    if outs.get("nv") is not None:
                        P.op("dve", lambda HB=HB: V.tensor_copy(v32[0:T, :], HB[0:T, :]), reads=[HB], writes=[v32])
                        P.dma("nv", lambda: nc.sync.dma_start(out=outs["nv"], in_=v32[0:T, :]), reads=[v32])
                    yield
                    for cb, dstp in ((512, sgsp), (2560, sgap)):
                        HB = hring.next()
                        yield from win_group(HB, cb)
                        P.op("act", (lambda dstp=dstp, HB=HB: A.activation(out=v3(dstp, T, 4), in_=pv3(HB, T), func=AF.Tanh, scale=0.5)), reads=[HB], writes=[dstp])
                        P.op("dve", (lambda dstp=dstp, HB=HB: V.scalar_tensor_tensor(v3(dstp, T, 4), v3(dstp, T, 4), 1.0, pv3(HB, T), ALU.add, ALU.mult)), reads=[HB, dstp], writes=[dstp])
                        yield

                def s5_stream(T, par, outs):
                    uTp, sgsp = uT[par], sgs[par]

                    def front(g):
                        c, band, e = g // 8, (g % 8) // 2, g % 2
                        r0 = 32 * band
                        bkp = pqr.next()
                        P.op("pe", lambda: PE.matmul(bkp[:, 0:T], B1.ap(r0, 32, (c * 2 + e) * 128, [[1, 128]]), uTp.ap(r0, 32, c * 128, [[1, T]]), start=True, stop=True, tile_position=(r0, 0)), reads=[B1, uTp], writes=[bkp])
                        P.op("pe", lambda: PE.matmul(bkp[:, 128:128 + T], B2.ap(r0, 32, (c * 2 + e) * 128, [[1, 128]]), uTp.ap(r0, 32, c * 128, [[1, T]]), start=True, stop=True, tile_position=(r0, 0)), reads=[B2, uTp], writes=[bkp])
                        return bkp

                    def stageB(g, bkp):
                        t1, t2, bt_ = t1r.next(), t2r.next(), btr.next()
                        P.op("dve", lambda: V.tensor_tensor(t1[:, 0:T], bkp[:, 0:T], PRE1[:, g, 0:T], ALU.mult), reads=[bkp, PRE1], writes=[t1])
                        P.op("dve", lambda: V.tensor_tensor(t2[:, 0:T], bkp[:, 128:128 + T], PRE2[:, g, 0:T], ALU.mult), reads=[bkp, PRE2], writes=[t2])
                        P.op("pool", lambda: Pl.tensor_tensor(bt_[:, 0:T], t1[:, 0:T], t2[:, 0:T], ALU.add), reads=[t1, t2], writes=[bt_])
                        return bt_

                    def stageC(g, bt_):
                        Gt, W1, W2 = Gr.next(), W1r.next(), W2r.next()
                        P.op("dve", lambda: V.tensor_tensor_scan(Gt[:, 0:T], SM.ap(0, 128, I_R * G + g, [[0, T]]), bt_[:, 0:T], SM.ap(0, 128, I_INIT * G + g, [[1, 1]]), ALU.mult, ALU.add), reads=[bt_] + smb(I_R, I_INIT), writes=[Gt])
                        P.op("pool", lambda: Pl.tensor_tensor(W1[:, 0:T], Gt[:, 0:T], POST1[:, g, 0:T], ALU.mult), reads=[Gt, POST1], writes=[W1])
                        P.op("pool", lambda: Pl.tensor_tensor(W2[:, 0:T], Gt[:, 0:T], POST2[:, g, 0:T], ALU.mult), reads=[Gt, POST2], writes=[W2])
                        P.op("act", lambda: A.activation(func=AF.Copy, out=SM.ap(0, 128, I_GLAST * G + g, [[1, 1]]), in_=Gt[:, T - 1:T]), reads=[Gt], writes=smb(I_GLAST))
                        return W1, W2

                    def back(g, W1, W2):
                        c, band, e = g // 8, (g % 8) // 2, g % 2
                        r0 = 32 * band
                        o_ = ybank.ap(r0, 32, c * 128, [[1, T]])
                        P.op("pe", lambda: PE.matmul(o_, L1[:, g, :], W1[:, 0:T], start=(e == 0), stop=False, tile_position=(0, r0)), reads=[L1, W1], writes=[ybank])
                        P.op("pe", lambda: PE.matmul(o_, L2[:, g, :], W2[:, 0:T], start=False, stop=(e == 1), tile_position=(0, r0)), reads=[L2, W2], writes=[ybank])

                    sA, sB, sC = {}, {}, {}
                    for i in range(G + 3):
                        if i < G:
                            sA[i] = front(i)
                        if 0 <= i - 1 < G:
                            sB[i - 1] = stageB(i - 1, sA.pop(i - 1))
                        if 0 <= i - 2 < G:
                            sC[i - 2] = stageC(i - 2, sB.pop(i - 2))
                        if 0 <= i - 3 < G:
                            back(i - 3, *sC.pop(i - 3))
                        yield
                def tail_stream(xb, bi, T, par, dst_ap, outs):
                    uTp, sgsp = uT[par], sgs[par]
                    for c in range(4):
                        P.op("dve", (lambda c=c: V.scalar_tensor_tensor(ypre[:, c, 0:T], uTp[:, c, 0:T], DCOL.ap(0, 128, l * 4 + c, [[1, 1]]), ybank[:, c * 128:c * 128 + T], ALU.mult, ALU.add)), reads=[uTp, DCOL, ybank], writes=[ypre])
                    P.op("act", lambda: A.activation(out=v3(tt, T, 4), in_=v3(ypre, T, 4), func=AF.Square), reads=[ypre], writes=[tt])
                    P.op("dve", lambda: V.tensor_scalar(v3(tt, T, 4), v3(tt, T, 4), 0.044715, 1.0, ALU.mult, ALU.add), reads=[tt], writes=[tt])
                    P.op("dve", lambda: V.tensor_tensor(v3(tt, T, 4), v3(tt, T, 4), v3(ypre, T, 4), ALU.mult), reads=[tt, ypre], writes=[tt])
                    P.op("act", lambda: A.activation(out=v3(tt, T, 4), in_=v3(tt, T, 4), func=AF.Tanh, scale=C_GELU), reads=[tt], writes=[tt])
                    P.op("dve", lambda: V.scalar_tensor_tensor(v3(yg, T, 4), v3(tt, T, 4), 1.0, v3(ypre, T, 4), ALU.add, ALU.mult), reads=[tt, ypre], writes=[yg])
                    yield
                    if outs.get("st") is not None:
                        ci, si = outs["st_cs"]
                        rotate(I_HL, sm(I_GLAST), smb(I_GLAST), ci, si, GEN)
                        P.dma("st", lambda: nc.sync.dma_start(out=outs["st"], in_=sm(I_HL)), reads=smb(I_HL))
                    else:
                        rotate(I_INIT, sm(I_GLAST), smb(I_GLAST), I_CA, I_SA, GEN)
                    bva, bga = GEN2, ybank
                    for oc in (4, 5, 6, 7, 0, 1, 2, 3):
                        bk_ = bva if oc < 4 else bga
                        for kc in range(4):
                            P.op("pe", (lambda oc=oc, kc=kc, bk_=bk_: PE.matmul(bk_[:, (oc % 4) * 128:(oc % 4) * 128 + T], Wglu[:, kc, oc * 128:(oc + 1) * 128], yg[:, kc, 0:T], start=(kc == 0), stop=(kc == 3))), reads=[Wglu, yg], writes=[bk_])
                        if oc % 4 == 3:
                            yield
                    P.op("act", lambda: A.activation(out=v3(sg, T, 4), in_=pv3(bga, T), func=AF.Tanh, scale=0.5), reads=[bga], writes=[sg])
                    P.op("dve", lambda: V.scalar_tensor_tensor(v3(sg, T, 4), v3(sg, T, 4), 1.0, v3(sgsp, T, 4), ALU.add, ALU.mult), reads=[sg, sgsp], writes=[sg])
                    P.op("dve", lambda: V.tensor_tensor(v3(mixT, T, 4), pv3(bva, T), v3(sg, T, 4), ALU.mult), reads=[bva, sg], writes=mixb[0:4])
                    yield
                    bwa, bwb = wring.next(), wring.next()
                    for oc in range(NCH):
                        bk_ = bwa if oc < 4 else bwb
                        for kc in range(NCH):
                            P.op("pe", (lambda oc=oc, kc=kc, bk_=bk_: PE.matmul(bk_[:, (oc % 4) * 128:(oc % 4) * 128 + T], Wout[:, kc, oc * 128:(oc + 1) * 128], mixT[:, kc, 0:T], start=(kc == 0), stop=(kc == NCH - 1))), reads=[Wout, mixb[kc]], writes=[bk_])
                        if oc % 2 == 1:
                            yield
                    P.op("dve", lambda: V.tensor_tensor(v3(xb, T, 4), v3(xb, T, 4), pv3(bwa, T), ALU.add), reads=[xb, bwa], writes=[xb])
                    P.op("dve", lambda: V.tensor_tensor(v3(xb, T, 4, c0=4), v3(xb, T, 4, c0=4), pv3(bwb, T), ALU.add), reads=[xb, bwb], writes=[xb])
                    P.dma("y%d" % bi, lambda: nc.sync.dma_start(out=dst_ap, in_=v3(xb, T, NCH)), reads=[xb])

                def attn_stream(T, ti, par, gk):
                    qTp, sgap = qT[par], sga[par]
                    jl = [j for j in range(5) if ti - 4 + j >= 0]
                    items = [(h, j) for h in range(8) for j in jl]

                    def qk(h, j):
                        hp, hh = h // 2, h % 2
                        sl = (gk - 4 + j) % 6
                        nk = T if j == 4 else 128
                        bk_ = scr.next()
                        P.op("pe", lambda: PE.matmul(bk_[0:nk, 0:T], kwin.ap(64 * hh, 64, (hp * 6 + sl) * 128, [[1, nk]]), qTp.ap(64 * hh, 64, hp * 128, [[1, T]]), start=True, stop=True), reads=[kslot[sl], qTp], writes=[bk_])
                        return bk_, nk, sl

                    def soft(h, j, bk_, nk):
                        pT = pTr.next()
                        if j in (1, 2):
                            P.op("act", lambda: A.activation(out=pT[0:nk, 0:T], in_=bk_[0:nk, 0:T], func=AF.Exp, bias=col(8 + h, nk), scale=1.0), reads=[bk_, COLB], writes=[pT])
                        else:
                            stmp = stmpr.next()
                            if j == 0:
                                badd, bias_, rd = cM0(nk, T), col(8 + h, nk), [CST]
                            elif j == 3:
                                badd, bias_, rd = BT[0:nk, 8 + h, 0:T], col(6, nk), [BT]
                            else:
                                badd, bias_, rd = BT[0:nk, h, 0:T], col(6, nk), [BT]
                            P.op("dve", lambda: V.tensor_tensor(stmp[0:nk, 0:T], bk_[0:nk, 0:T], badd, ALU.add), reads=[bk_] + rd, writes=[stmp])
                            P.op("act", lambda: A.activation(out=pT[0:nk, 0:T], in_=stmp[0:nk, 0:T], func=AF.Exp, bias=bias_, scale=1.0), reads=[stmp, COLB], writes=[pT])
                        return pT

                    def pvmm_pair(hp, j, pts, cur):
                        first, last = (j == jl[0]), (j == jl[-1])
                        osb = OS[hp % 2]
                        for hh in range(2):
                            h = 2 * hp + hh
                            pT, (bk_, nk, sl) = pts[hh], cur[hh]
                            o_ = osb.ap(64 * hh, 64, 0, [[1, T]])
                            P.op("pe", (lambda h=h, hh=hh, pT=pT, nk=nk, sl=sl, o_=o_: PE.matmul(o_, vwin[0:nk, sl, h * 64:(h + 1) * 64], pT[0:nk, 0:T], start=first, stop=last, tile_position=(0, 64 * hh), skip_group_check=True)), reads=[vslot[sl], pT], writes=[osb])
                        for hh in range(2):
                            pT, (bk_, nk, sl) = pts[hh], cur[hh]
                            s_ = osb.ap(64 * hh, 64, 128, [[1, T]])
                            P.op("pe", (lambda hh=hh, pT=pT, nk=nk, s_=s_: PE.matmul(s_, ONE64B[0:nk, :], pT[0:nk, 0:T], start=False, stop=last, tile_position=(0, 64 * hh), skip_group_check=True)), reads=[ONE64B, pT], writes=[osb])
                        if last:
                            rs = rsr.next()
                            P.op("dve", lambda: V.reciprocal(rs[:, 0:T], osb[:, 128:128 + T]), reads=[osb], writes=[rs])
                            P.op("pool", lambda: Pl.tensor_tensor(rs[:, 0:T], rs[:, 0:T], sgap[:, hp, 0:T], ALU.mult), reads=[rs, sgap], writes=[rs])
                            P.op("dve", lambda: V.tensor_tensor(mixT[:, 4 + hp, 0:T], osb[:, 0:T], rs[:, 0:T], ALU.mult), reads=[osb, rs], writes=[mixb[4 + hp]])

                    pitems = [(hp, j) for hp in range(4) for j in jl]

                    def qk_pair(hp, j):
                        return [qk(2 * hp + hh, j) for hh in range(2)]

                    q = [qk_pair(*pitems[0])]
                    for idx in range(len(pitems)):
                        if idx + 1 < len(pitems):
                            q.append(qk_pair(*pitems[idx + 1]))
                        hp, j = pitems[idx]
                        cur = q.pop(0)
                        pts = [soft(2 * hp + hh, j, cur[hh][0], cur[hh][1]) for hh in range(2)]
                        pvmm_pair(hp, j, pts, cur)
                        yield

                def run_streams(streams):
                    streams = list(streams)
                    while streams:
                        for s_ in list(streams):
                            try:
                                next(s_)
                            except StopIteration:
                                streams.remove(s_)

                def body(xb, bi, T, ti, par, dst_ap, outs, nxt=None, gk=4):
                    run_streams([s5_stream(T, par, outs), attn_stream(T, ti, par, gk)])
                    streams = [tail_stream(xb, bi, T, par, dst_ap, outs)]
                    if nxt is not None:
                        streams.append(nxt)
                    run_streams(streams)

                P.op("dve", lambda: V.memset(col(7), EPS), writes=[COLB])
                srcp, dstp = (xp, y1p) if l == 0 else (y1p, yp)
                srcs, dsts = (xs, y1s) if l == 0 else (y1s, ys)

                def p_outs(s, i):
                    outs = {}
                    if i >= NT - 4:
                        outs["nk"] = nkp[l, s, i - (NT - 4)]
                        outs["nv"] = nvp[l, s, (i - (NT - 4)) * 128:(i - (NT - 4) + 1) * 128, :]
                    if i == NT - 1:
                        outs["st"] = stp[l, s]
                        outs["st_cs"] = (I_CLP, I_SLP)
                    return outs

                tiles = [(s, i) for s in range(NSP) for i in range(NT)]
                load_x(srcp[0, 0], xT[0], 128, 0)
                run_streams([head_stream(xT[0], 128, 0, 0, p_outs(0, 0), 0)])
                for k, (s, i) in enumerate(tiles):
                    bi = k % 2
                    if i == 0:
                        P.op("dve", lambda: V.memset(sm(I_INIT), 0.0), writes=smb(I_INIT))
                    nxt = None
                    if k + 1 < len(tiles):
                        s2_, i2_ = tiles[k + 1]
                        load_x(srcp[s2_, i2_], xT[1 - bi], 128, 1 - bi)
                        nxt = head_stream(xT[1 - bi], 128, i2_, 1 - bi, p_outs(s2_, i2_), k + 1)
                    body(xT[bi], bi, 128, i, bi, dstp[s, i], p_outs(s, i), nxt, k)
                for s in range(NSS):
                    bi = s % 2
                    cstg = xT[1 - bi]
                    load_x(srcs[s], xT[bi], TS, bi)
                    for hf in range(2):
                        P.dma("cstg", (lambda s=s, hf=hf, cstg=cstg: nc.sync.dma_start(out=cstg.ap(0, 128, 0, [[512, 2], [1, 512]]), in_=ck[l, s, :, 2 * hf:2 * hf + 2, :])), writes=[cstg])
                        P.op("pool", (lambda hf=hf, cstg=cstg: Pl.tensor_copy(kwin.ap(0, 128, 2 * hf * 768, [[768, 2], [128, 4], [1, 128]]), cstg.ap(0, 128, 0, [[512, 2], [128, 4], [1, 128]]))), reads=[cstg], writes=kslot[0:4])
                    for hf in range(2):
                        P.dma("cstg", (lambda s=s, hf=hf, cstg=cstg: nc.sync.dma_start(out=cstg.ap(0, 128, 0, [[512, 2], [1, 512]]), in_=cv[l, s, hf * 256:(hf + 1) * 256, :].rearrange("(j k) e -> k j e", k=128))), writes=[cstg])
                        P.op("act", (lambda hf=hf, cstg=cstg: A.activation(func=AF.Copy, out=vwin[:, 2 * hf:2 * hf + 2, :], in_=cstg.ap(0, 128, 0, [[512, 2], [1, 512]]))), reads=[cstg], writes=vslot[0:4])
                    P.dma("st0", (lambda s=s: nc.sync.dma_start(out=sm(I_ST0), in_=st0[l, s])), writes=smb(I_ST0))
                    rotate(I_INIT, sm(I_ST0), smb(I_ST0), I_C1, I_S1)
                    outs = {"nk": nks[l, s], "nv": nvs[l, s], "st": sts[l, s], "st_cs": (I_CLS, I_SLS)}
                    run_streams([head_stream(xT[bi], TS, 4, bi, outs, 4)])
                    body(xT[bi], bi, TS, 4, bi, dsts[s], outs, None, 4)
                P.phase_end()
        print("[kernel] instructions:", P.n_inst, "semaphores:", P.nsem)
    return nc


def _consts():
    c = np.zeros((128, NCST), np.float32)
    idx = np.arange(128)
    c[idx, C_J + 127 - idx] = 1.0
    p = np.arange(64)
    c[64 + p, C_SWN + p] = -1.0
    c[p, C_SWN + 64 + p] = 1.0
    c[:, C_ONES:C_ONES + 128] = 1.0 / 1024.0
    c[0:64, C_BLK:C_BLK + 64] = 1.0 / 64.0
    c[64:128, C_BLK + 64:C_BLK + 128] = 1.0 / 64.0
    c[:, C_ONE64:C_ONE64 + 64] = 1.0
    c[64:128, C_M4:C_M4 + 64] = NEG
    c[0:64, C_M0 + 64:C_M0 + 128] = NEG
    c[:, C_JROW:C_JROW + 128] = np.arange(128, dtype=np.float32)[None]
    c[0:64, C_SH] = 1.0
    c[64:128, C_SH] = -1.0
    return c


_PROG_CACHE = {}


def kernel(x_prompt, x_sample, cache_k, cache_v, state_ssm_re, state_ssm_im, norm_gain, w_in,
           ssm_a_re, ssm_a_im, ssm_b_re, ssm_b_im, ssm_c_re, ssm_c_im, ssm_d, ssm_log_dt,
           w_glu, q_norm_gain, k_norm_gain, rel_bias, w_out, n_cores=8):
    f = np.float32
    x_prompt = np.asarray(x_prompt, f)
    x_sample = np.asarray(x_sample, f)
    B, SEQ, _ = x_prompt.shape
    BS = x_sample.shape[0]
    NSP, NSS = B // n_cores, BS // n_cores
    NT = SEQ // 128
    R = cache_k.shape[2]
    assert R == 512 and SEQ >= 512 and SEQ % 128 == 0 and x_sample.shape[1] == TS

    xpb = x_prompt.reshape(B, NT, 128, NCH, 128).transpose(0, 1, 4, 3, 2)
    xsb = x_sample.reshape(BS, TS, NCH, 128).transpose(0, 3, 2, 1)
    ckb = np.asarray(cache_k, f).reshape(L, BS, R, 4, 2, 64).transpose(0, 1, 4, 5, 3, 2).reshape(L, BS, 128, 4, R)
    cvb = np.asarray(cache_v, f).reshape(L, BS, R, 512)
    st = np.concatenate([np.asarray(state_ssm_re, f).transpose(0, 1, 3, 2), np.asarray(state_ssm_im, f).transpose(0, 1, 3, 2)], axis=2)
    ngh = np.asarray(norm_gain, f).reshape(L, NCH, 128).transpose(2, 0, 1)
    are = np.asarray(ssm_a_re, f).transpose(2, 0, 1)
    aim = np.asarray(ssm_a_im, f).transpose(2, 0, 1)
    ldt = np.broadcast_to(np.asarray(ssm_log_dt, f)[None], (64, L, G))
    s5 = np.stack([are, aim, ldt], axis=2)
    s5 = np.concatenate([s5, s5], axis=0)
    bre = np.asarray(ssm_b_re, f)
    bim = np.asarray(ssm_b_im, f)
    b1 = np.zeros((L, 128, 8, 128), f)
    b2 = np.zeros((L, 128, 8, 128), f)
    cre = np.asarray(ssm_c_re, f)
    cim = np.asarray(ssm_c_im, f)
    l1 = np.zeros((L, 128, G, 32), f)
    l2 = np.zeros((L, 128, G, 32), f)
    for g in range(G):
        c, band, e = g // 8, (g % 8) // 2, g % 2
        r0 = 32 * band + 16 * e
        b1[:, r0:r0 + 16, c * 2 + e, 0:64] = bre[:, g].transpose(0, 2, 1)
        b1[:, r0:r0 + 16, c * 2 + e, 64:128] = bim[:, g].transpose(0, 2, 1)
        b2[:, r0:r0 + 16, c * 2 + e, 0:64] = bim[:, g].transpose(0, 2, 1)
        b2[:, r0:r0 + 16, c * 2 + e, 64:128] = bre[:, g].transpose(0, 2, 1)
        l1[:, 0:64, g, 16 * e:16 * e + 16] = cre[:, g].transpose(0, 2, 1)
        l1[:, 64:128, g, 16 * e:16 * e + 16] = cim[:, g].transpose(0, 2, 1)
        l2[:, 0:64, g, 16 * e:16 * e + 16] = cim[:, g].transpose(0, 2, 1)
        l2[:, 64:128, g, 16 * e:16 * e + 16] = cre[:, g].transpose(0, 2, 1)
    dch = np.asarray(ssm_d, f).reshape(L, 4, 128).transpose(2, 0, 1)
    qg = np.asarray(q_norm_gain, f)
    kg = np.asarray(k_norm_gain, f)
    qkgh = np.stack([np.concatenate([qg, qg], 1), np.concatenate([kg, kg], 1)], axis=2).transpose(1, 0, 2)
    qkr = np.broadcast_to(np.concatenate([qg, kg], 1)[None], (128, L, 128))
    rbh = np.asarray(rel_bias, f)
    rbr = np.broadcast_to(rbh.reshape(1, L, 8 * 257), (128, L, 8 * 257))
    cst = _consts()

    key = (NT, NSP, NSS)
    nc = build_program(NT, NSP, NSS)
    shared = dict(w_in=np.ascontiguousarray(w_in, f), w_glu=np.ascontiguousarray(w_glu, f), w_out=np.ascontiguousarray(w_out, f),
                  ng=np.ascontiguousarray(ngh), s5a=np.ascontiguousarray(s5), b1h=b1.reshape(L, 128, 1024), b2h=b2.reshape(L, 128, 1024),
                  l1h=l1.reshape(L, 128, 1024), l2h=l2.reshape(L, 128, 1024), dcol=np.ascontiguousarray(dch),
                  qkg=np.ascontiguousarray(qkgh), qkrow=np.ascontiguousarray(qkr), rb=np.ascontiguousarray(rbh),
                  rbrep=np.ascontiguousarray(rbr), cst=cst)
    in_maps = []
    for c in range(n_cores):
        m = dict(shared)
        m["xp"] = np.ascontiguousarray(xpb[c * NSP:(c + 1) * NSP])
        m["xs"] = np.ascontiguousarray(xsb[c * NSS:(c + 1) * NSS])
        m["ck"] = np.ascontiguousarray(ckb[:, c * NSS:(c + 1) * NSS])
        m["cv"] = np.ascontiguousarray(cvb[:, c * NSS:(c + 1) * NSS])
        m["st0"] = np.ascontiguousarray(st[:, c * NSS:(c + 1) * NSS])
        in_maps.append(m)
    res = run_bass_kernel_spmd(nc, in_maps, core_ids=list(range(n_cores)))
    rs = res.results

    def cat(name, axis):
        return np.concatenate([np.asarray(r[name]) for r in rs], axis=axis)

    ypo = cat("yp", 0).transpose(0, 1, 4, 3, 2).reshape(B, SEQ, D)
    yso = cat("ys", 0).transpose(0, 3, 2, 1).reshape(BS, TS, D)
    nkpo = cat("nkp", 1)
    nkpo = nkpo.reshape(L, B, 4, 2, 64, 4, 128).transpose(0, 1, 2, 6, 5, 3, 4).reshape(L, B, 512, 8, 64)
    nvpo = cat("nvp", 1).reshape(L, B, 512, 8, 64)
    stpo = cat("stp", 1)
    pr = stpo[:, :, 0:64].transpose(0, 1, 3, 2)
    pi = stpo[:, :, 64:128].transpose(0, 1, 3, 2)
    nkso = cat("nks", 1).reshape(L, BS, 2, 64, 4, TS).transpose(0, 1, 5, 4, 2, 3).reshape(L, BS, TS, 8, 64)
    nvso = cat("nvs", 1).reshape(L, BS, TS, 8, 64)
    stso = cat("sts", 1)
    sr = stso[:, :, 0:64].transpose(0, 1, 3, 2)
    si = stso[:, :, 64:128].transpose(0, 1, 3, 2)
    c_ = lambda a: np.ascontiguousarray(a, dtype=np.float32)
    return (c_(ypo), c_(yso), c_(nkpo), c_(nvpo), c_(pr), c_(pi), c_(nkso), c_(nvso), c_(sr), c_(si))
```

```python
import math
from contextlib import ExitStack
import numpy as np
import concourse.bass as bass
import concourse.mybir as mybir
from concourse.bass_utils import run_bass_kernel_spmd

F32 = mybir.dt.float32
BF16 = mybir.dt.bfloat16
I32 = mybir.dt.int32
ALU = mybir.AluOpType
AF = mybir.ActivationFunctionType
AX = mybir.AxisListType

L = 2
D = 1024
NCH = 8
DIN = 3072
G = 32
TS = 16
EPS = 1e-6
NEG = -30000.0
TWO_PI = 2.0 * math.pi
STRICT = True

C_J, C_SWN, C_ONES, C_BLK, C_ONE64, C_M4, C_M0, C_JROW, C_SH = 0, 128, 256, 384, 512, 576, 704, 832, 960
NCST = 961


class Buf:
    def __init__(self, name):
        self.name = name
        self.w = None
        self.r = {}
        self.excl = False


class TT:
    def __init__(self, h, shape, dt, name, nbuf=None):
        self.h = h
        self.shape = shape
        self.row = int(np.prod(shape[1:]))
        self.dt = dt
        self.b = Buf(name)

    def __getitem__(self, idx):
        return self.h[idx]

    def ap(self, p0, npart, off, dims):
        return bass.AP(self.h, p0 * self.row + off, [[self.row, npart]] + [list(d) for d in dims])


class Op:
    __slots__ = ("eng", "fn", "deps", "needs_inc", "sem", "target", "is_dma", "epoch", "gen", "vc")


class Chan:
    def __init__(self, sem):
        self.sem = sem
        self.count = 0


class Prog:
    ENG = ("pe", "act", "dve", "pool", "sp")

    def __init__(self, nc, stack):
        self.nc = nc
        self.stack = stack
        self.e = {"pe": nc.tensor, "act": nc.scalar, "dve": nc.vector, "pool": nc.gpsimd, "sp": nc.sync}
        self.ops = []
        self.epoch = 0
        self.gen = 0
        self.sems = {}
        self.rank = {}
        self.seen = {}
        self.known = {}
        self.last = {}
        self.chans = {}
        self.nsem = 0
        self.n_inst = 0

    def _sem(self, name):
        self.nsem += 1
        return self.stack.enter_context(self.nc.semaphore(name))

    def chan(self, name):
        if name not in self.chans:
            self.chans[name] = Chan(self._sem("c_" + name))
        return self.chans[name]

    def _deps(self, eng, reads, writes):
        deps = []
        for b in reads:
            if b.w is not None:
                deps.append(b.w)
            if b.excl:
                for en_, o_ in b.r.items():
                    if en_ != eng:
                        deps.append(o_)
        for b in writes:
            if b.w is not None:
                deps.append(b.w)
            deps.extend(b.r.values())
        out = []
        for d in deps:
            if d.gen != self.gen:
                continue
            if (not d.is_dma) and d.eng == eng and (eng == "pe" or not STRICT):
                continue
            d.needs_inc = True
            out.append(d)
        return out

    def op(self, eng, fn, reads=(), writes=()):
        reads = [x.b if hasattr(x, 'b') else x for x in reads]
        writes = [x.b if hasattr(x, 'b') else x for x in writes]
        o = Op()
        o.eng = eng
        o.fn = fn
        o.is_dma = False
        o.needs_inc = False
        o.sem = None
        o.target = 0
        o.epoch = self.epoch
        o.gen = self.gen
        o.deps = self._deps(eng, reads, writes)
        for b in writes:
            b.w = o
            b.r = {}
        for b in reads:
            b.r[eng] = o
        self.ops.append(o)
        return o

    def dma(self, chan, fn, reads=(), writes=()):
        reads = [x.b if hasattr(x, 'b') else x for x in reads]
        writes = [x.b if hasattr(x, 'b') else x for x in writes]
        ch = self.chan(chan)
        o = Op()
        o.eng = "sp"
        o.fn = fn
        o.is_dma = True
        o.needs_inc = True
        ch.count += 1
        o.sem = ch.sem
        o.target = 16 * ch.count
        o.epoch = self.epoch
        o.gen = self.gen
        o.deps = self._deps("sp", reads, writes)
        for b in writes:
            b.w = o
            b.r = {}
        for b in reads:
            b.r["dma_" + chan] = o
        self.ops.append(o)
        return o

    def emit(self):
        known = self.known
        for o in self.ops:
            e = self.e[o.eng]
            kn = known.setdefault(o.eng, {})
            need = {}
            for d in o.deps:
                k = id(d.sem)
                if k not in need or need[k][1] < d.target:
                    need[k] = (d.sem, d.target, d)
            for k, (sem, tgt, d) in need.items():
                if kn.get(k, 0) >= tgt:
                    continue
                e.wait_ge(sem, tgt)
                self.n_inst += 1
                kn[k] = tgt
                for k2, v2 in d.vc.items():
                    if kn.get(k2, 0) < v2:
                        kn[k2] = v2
            inst = o.fn()
            self.n_inst += 1
            if o.is_dma:
                inst.then_inc(o.sem, 16)
                o.vc = dict(kn)
            else:
                self.last[o.eng] = o
                if o.needs_inc:
                    key = (o.eng, o.epoch)
                    if key not in self.sems:
                        self.sems[key] = self._sem("e_%s_%d" % key)
                        self.rank[key] = 0
                    self.rank[key] += 1
                    o.sem = self.sems[key]
                    o.target = self.rank[key]
                    inst.then_inc(o.sem, 1)
                    o.vc = dict(kn)
        self.ops = []

    def phase_end(self):
        lastops = {}
        for o in self.ops:
            if not o.is_dma:
                lastops[o.eng] = o
        for o in lastops.values():
            o.needs_inc = True
        self.emit()
        for en in self.ENG:
            e = self.e[en]
            kn = self.known.setdefault(en, {})
            for fn_, o in self.last.items():
                if fn_ == en or o.sem is None:
                    continue
                if kn.get(id(o.sem), 0) >= o.target:
                    continue
                kn[id(o.sem)] = o.target
                e.wait_ge(o.sem, o.target)
            for ch in self.chans.values():
                if ch.count == 0:
                    continue
                if kn.get(id(ch.sem), 0) >= 16 * ch.count:
                    continue
                kn[id(ch.sem)] = 16 * ch.count
                e.wait_ge(ch.sem, 16 * ch.count)
        self.gen += 1
        self.epoch += 1


class PS:
    def __init__(self, bank, c0, name):
        self.bank = bank
        self.c0 = c0
        self.b = Buf(name)

    def ap(self, p0, n, off, dims):
        return self.bank.ap(p0, n, self.c0 + off, dims)

    def v(self, n, T, off=0, p0=0):
        return self.bank.ap(p0, n, self.c0 + off, [[1, T]])


class Ring:
    def __init__(self, items):
        self.items = items
        self.i = 0

    def next(self):
        x = self.items[self.i % len(self.items)]
        self.i += 1
        return x


def build_program(NT, NSP, NSS):
    nc = bass.Bass("TRN2", target_bir_lowering=False)

    def din(name, shape):
        return nc.dram_tensor(name, list(shape), F32, kind="ExternalInput").ap()

    def dout(name, shape):
        return nc.dram_tensor(name, list(shape), F32, kind="ExternalOutput").ap()

    xp = din("xp", [NSP, NT, 128, NCH, 128])
    xs = din("xs", [NSS, 128, NCH, TS])
    ck = din("ck", [L, NSS, 128, 4, 512])
    cv = din("cv", [L, NSS, 512, 512])
    st0 = din("st0", [L, NSS, 128, G])
    w_in = din("w_in", [L, D, DIN])
    w_glu = din("w_glu", [L, 512, 1024])
    w_out = din("w_out", [L, D, D])
    ng = din("ng", [128, L, NCH])
    s5a = din("s5a", [128, L, 3, G])
    b1h = din("b1h", [L, 128, 1024])
    b2h = din("b2h", [L, 128, 1024])
    l1h = din("l1h", [L, 128, 1024])
    l2h = din("l2h", [L, 128, 1024])
    dcol = din("dcol", [128, L, 4])
    qkg = din("qkg", [128, L, 2])
    qkrow = din("qkrow", [128, L, 128])
    rb = din("rb", [L, 8, 257])
    rbrep = din("rbrep", [128, L, 8 * 257])
    cst = din("cst", [128, NCST])

    yp = dout("yp", [NSP, NT, 128, NCH, 128])
    ys = dout("ys", [NSS, 128, NCH, TS])
    nkp = dout("nkp", [L, NSP, 4, 128, 4, 128])
    nvp = dout("nvp", [L, NSP, 512, 512])
    stp = dout("stp", [L, NSP, 128, G])
    nks = dout("nks", [L, NSS, 128, 4, TS])
    nvs = dout("nvs", [L, NSS, TS, 512])
    sts = dout("sts", [L, NSS, 128, G])
    y1p = nc.dram_tensor("y1p", [NSP, NT, 128, NCH, 128], F32, kind="Internal").ap()
    y1s = nc.dram_tensor("y1s", [NSS, 128, NCH, TS], F32, kind="Internal").ap()
    ext = nc.dram_tensor("ext", [L, 8, 384], F32, kind="Internal").ap()

    with ExitStack() as glob:
        P = Prog(nc, glob)

        uniq = [0]

        def sb(stack, name, shape, dt=F32):
            uniq[0] += 1
            name = "%s_%d" % (name, uniq[0])
            h = stack.enter_context(nc.sbuf_tensor(name, list(shape), dt))
            return TT(h, list(shape), dt, name)

        def ps(stack, name):
            h = stack.enter_context(nc.psum_tensor(name, [128, 512], F32))
            t_ = TT(h, [128, 512], F32, name)
            t_.b.excl = True
            return t_

        Win = sb(glob, "Win", [128, NCH, DIN], BF16)
        Wglu = sb(glob, "Wglu", [128, 4, 1024], BF16)
        Wout = sb(glob, "Wout", [128, NCH, D], BF16)
        PRE1 = sb(glob, "PRE1", [128, G, 128], BF16)
        PRE2 = sb(glob, "PRE2", [128, G, 128], BF16)
        POST1 = sb(glob, "POST1", [128, G, 128], BF16)
        POST2 = sb(glob, "POST2", [128, G, 128], BF16)
        B1 = sb(glob, "B1", [128, 8, 128], BF16)
        B2 = sb(glob, "B2", [128, 8, 128], BF16)
        L1 = sb(glob, "L1", [128, G, 32], BF16)
        L2 = sb(glob, "L2", [128, G, 32], BF16)
        CST = sb(glob, "CST", [128, NCST], F32)
        ONESB = sb(glob, "ONESB", [128, 128], BF16)
        BLKB = sb(glob, "BLKB", [128, 128], BF16)
        ONE64B = sb(glob, "ONE64B", [128, 64], BF16)
        BT = sb(glob, "BT", [128, 16, 128], F32)
        kwin = sb(glob, "kwin", [128, 4, 6, 128], BF16)
        vwin = sb(glob, "vwin", [128, 6, 512], BF16)
        kslot = [Buf("ks%d" % i) for i in range(6)]
        vslot = [Buf("vs%d" % i) for i in range(6)]
        SM = sb(glob, "SM", [128, 40, G], F32)
        SMB = [Buf("sm%d" % i) for i in range(40)]
        COL = sb(glob, "COL", [128, 64], F32)
        COLB = Buf("col")
        NGs = sb(glob, "NGs", [128, L, NCH], F32)
        S5A = sb(glob, "S5A", [128, L, 3, G], F32)
        DCOL = sb(glob, "DCOL", [128, L, 4], F32)
        QKG = sb(glob, "QKG", [128, L, 2], F32)
        banks = [ps(glob, "pb%d" % i) for i in range(8)]
        ybank = banks[0]
        pqr = Ring(banks[1:3])
        scr = Ring(banks[3:7])
        OS = [banks[7], banks[7]]
        GEN = banks[1]
        GEN2 = banks[2]
        prot = Ring(banks[1:8])

        (I_ARE, I_AIM, I_LDT, I_DT, I_R, I_TH, I_S1, I_C1, I_ABRE, I_ABIM, I_NR, I_DEN, I_FRE, I_FIM,
         I_CA, I_SA, I_CLP, I_SLP, I_CLS, I_SLS, I_INIT, I_GLAST, I_T0, I_T1, I_T2, I_T3, I_T4, I_T5, I_T6,
         I_T7, I_HL, I_ST0, I_NFIM) = range(33)

        def sm(i, n=128):
            return SM.ap(0, n, i * G, [[1, G]])

        def col(i, n=128, p0=0):
            return COL.ap(p0, n, i, [[1, 1]])

        cJ = CST.ap(0, 128, C_J, [[1, 128]])
        cSWN = CST.ap(0, 128, C_SWN, [[1, 128]])
        cM4 = lambda nk, T: CST.ap(0, nk, C_M4, [[1, T]])
        cM0 = lambda nk, T: CST.ap(0, nk, C_M0, [[1, T]])
        cSH = CST.ap(0, 128, C_SH, [[1, 1]])

        V, A, Pl, PE = nc.vector, nc.scalar, nc.gpsimd, nc.tensor

        P.dma("cst", lambda: nc.sync.dma_start(out=CST[:], in_=cst[:]), writes=[CST])
        P.dma("cst2", lambda: nc.sync.dma_start(out=NGs[:], in_=ng[:]), writes=[NGs])
        P.dma("cst3", lambda: nc.sync.dma_start(out=S5A[:], in_=s5a[:]), writes=[S5A])
        P.dma("cst4", lambda: nc.sync.dma_start(out=DCOL[:], in_=dcol[:]), writes=[DCOL])
        P.dma("cst5", lambda: nc.sync.dma_start(out=QKG[:], in_=qkg[:]), writes=[QKG])
        P.op("dve", lambda: V.tensor_copy(ONESB[:], CST.ap(0, 128, C_ONES, [[1, 128]])), reads=[CST], writes=[ONESB])
        P.op("dve", lambda: V.tensor_copy(BLKB[:], CST.ap(0, 128, C_BLK, [[1, 128]])), reads=[CST], writes=[BLKB])
        P.op("dve", lambda: V.tensor_copy(ONE64B[:], CST.ap(0, 128, C_ONE64, [[1, 64]])), reads=[CST], writes=[ONE64B])
        P.phase_end()

        for l in range(L):
            with ExitStack() as s1:
                stg = [sb(s1, "stg%d" % i, [128, 1536], F32) for i in range(2)]
                tmpN = 256
                tb = [sb(s1, "tb%d" % i, [128, tmpN], F32) for i in range(8)]
                tbi = sb(s1, "tbi", [128, tmpN], I32)
                HK = sb(s1, "HK", [128, 16, 128], F32)
                RBR = sb(s1, "RBR", [128, 8 * 257], F32)
                QKR = sb(s1, "QKR", [128, 128], F32)
                EXS = sb(s1, "EXS", [8, 384], F32)

                def sa(i):
                    return S5A.ap(0, 128, (l * 3 + i) * G, [[1, G]])

                def sincos(ang, n, out_sin, out_cos, rd, wr):
                    t, kf, d, m = (tb[4].ap(0, 128, 0, [[1, n]]), tb[5].ap(0, 128, 0, [[1, n]]),
                                   tb[6].ap(0, 128, 0, [[1, n]]), tb[7].ap(0, 128, 0, [[1, n]]))
                    ki = tbi.ap(0, 128, 0, [[1, n]])
                    for off, outp in ((0.0, out_sin), (0.25, out_cos)):
                        P.op("dve", (lambda off=off: V.tensor_scalar(t, ang, 1.0 / TWO_PI, off, ALU.mult, ALU.add)), reads=rd + [tb[4]], writes=[tb[4]])
                        P.op("dve", lambda: V.tensor_copy(ki, t), reads=[tb[4]], writes=[tbi])
                        P.op("dve", lambda: V.tensor_copy(kf, ki), reads=[tbi], writes=[tb[5]])
                        P.op("dve", lambda: V.tensor_tensor(d, t, kf, ALU.subtract), reads=[tb[4], tb[5]], writes=[tb[6]])
                        P.op("dve", lambda: V.tensor_scalar(m, d, 0.5, None, ALU.is_gt), reads=[tb[6]], writes=[tb[7]])
                        P.op("dve", lambda: V.tensor_tensor(d, d, m, ALU.subtract), reads=[tb[6], tb[7]], writes=[tb[6]])
                        P.op("dve", lambda: V.tensor_scalar(m, d, -0.5, None, ALU.is_lt), reads=[tb[6]], writes=[tb[7]])
                        P.op("dve", lambda: V.tensor_tensor(d, d, m, ALU.add), reads=[tb[6], tb[7]], writes=[tb[6]])
                        P.op("act", (lambda outp=outp: A.activation(out=outp, in_=d, func=AF.Sin, scale=TWO_PI * (1.0 - 2e-6))), reads=[tb[6]], writes=wr)

                smb = lambda *ids: [SMB[i] for i in ids]
                P.op("act", lambda: A.activation(out=sm(I_DT), in_=sa(2), func=AF.Exp), reads=[S5A], writes=smb(I_DT))
                P.op("dve", lambda: V.tensor_tensor(sm(I_T0), sm(I_DT), sa(0), ALU.mult), reads=[S5A] + smb(I_DT), writes=smb(I_T0))
                P.op("act", lambda: A.activation(out=sm(I_R), in_=sm(I_T0), func=AF.Exp), reads=smb(I_T0), writes=smb(I_R))
                P.op("dve", lambda: V.tensor_tensor(sm(I_TH), sm(I_DT), sa(1), ALU.mult), reads=[S5A] + smb(I_DT), writes=smb(I_TH))
                sincos(sm(I_TH), G, sm(I_S1), sm(I_C1), smb(I_TH), smb(I_S1, I_C1))
                for mult_, (ci, si) in ((128.0, (I_CA, I_SA)), (127.0, (I_CLP, I_SLP)), (float(TS - 1), (I_CLS, I_SLS))):
                    P.op("dve", (lambda mult_=mult_: V.tensor_scalar(sm(I_T1), sm(I_TH), mult_, None, ALU.mult)), reads=smb(I_TH), writes=smb(I_T1))
                    sincos(sm(I_T1), G, sm(si), sm(ci), smb(I_T1), smb(si, ci))
                P.op("dve", lambda: V.tensor_tensor(sm(I_ABRE), sm(I_R), sm(I_C1), ALU.mult), reads=smb(I_R, I_C1), writes=smb(I_ABRE))
                P.op("dve", lambda: V.tensor_tensor(sm(I_ABIM), sm(I_R), sm(I_S1), ALU.mult), reads=smb(I_R, I_S1), writes=smb(I_ABIM))
                P.op("dve", lambda: V.tensor_scalar(sm(I_NR), sm(I_ABRE), -1.0, None, ALU.add), reads=smb(I_ABRE), writes=smb(I_NR))
                P.op("dve", lambda: V.tensor_tensor(sm(I_T0), sa(0), sa(0), ALU.mult), reads=[S5A], writes=smb(I_T0))
                P.op("dve", lambda: V.tensor_tensor(sm(I_T1), sa(1), sa(1), ALU.mult), reads=[S5A], writes=smb(I_T1))
                P.op("dve", lambda: V.tensor_tensor(sm(I_DEN), sm(I_T0), sm(I_T1), ALU.add), reads=smb(I_T0, I_T1), writes=smb(I_DEN))
                P.op("dve", lambda: V.reciprocal(sm(I_DEN), sm(I_DEN)), reads=smb(I_DEN), writes=smb(I_DEN))
                P.op("dve", lambda: V.tensor_tensor(sm(I_T0), sm(I_NR), sa(0), ALU.mult), reads=[S5A] + smb(I_NR), writes=smb(I_T0))
                P.op("dve", lambda: V.tensor_tensor(sm(I_T1), sm(I_ABIM), sa(1), ALU.mult), reads=[S5A] + smb(I_ABIM), writes=smb(I_T1))
                P.op("dve", lambda: V.tensor_tensor(sm(I_T0), sm(I_T0), sm(I_T1), ALU.add), reads=smb(I_T0, I_T1), writes=smb(I_T0))
                P.op("dve", lambda: V.tensor_tensor(sm(I_FRE), sm(I_T0), sm(I_DEN), ALU.mult), reads=smb(I_T0, I_DEN), writes=smb(I_FRE))
                P.op("dve", lambda: V.tensor_tensor(sm(I_T2), sm(I_ABIM), sa(0), ALU.mult), reads=[S5A] + smb(I_ABIM), writes=smb(I_T2))
                P.op("dve", lambda: V.tensor_tensor(sm(I_T3), sm(I_NR), sa(1), ALU.mult), reads=[S5A] + smb(I_NR), writes=smb(I_T3))
                P.op("dve", lambda: V.tensor_tensor(sm(I_T2), sm(I_T2), sm(I_T3), ALU.subtract), reads=smb(I_T2, I_T3), writes=smb(I_T2))
                P.op("dve", lambda: V.tensor_tensor(sm(I_FIM), sm(I_T2), sm(I_DEN), ALU.mult), reads=smb(I_T2, I_DEN), writes=smb(I_FIM))
                P.op("dve", lambda: V.tensor_scalar(sm(I_NFIM), sm(I_FIM), -1.0, None, ALU.mult), reads=smb(I_FIM), writes=smb(I_NFIM))

                def wload():
                    wi = 0
                    for kc2 in range(2 * NCH):
                        kc, hf = kc2 // 2, kc2 % 2
                        st = stg[wi % 2]
                        P.dma("w%d" % (wi % 2), (lambda st=st, kc=kc, hf=hf: nc.sync.dma_start(out=st[:], in_=w_in[l, kc * 128:(kc + 1) * 128, hf * 1536:(hf + 1) * 1536])), writes=[st])
                        gcol = NGs.ap(0, 128, l * NCH + kc, [[1, 1]])
                        P.op("act", (lambda st=st, kc=kc, hf=hf, gcol=gcol: A.activation(out=Win[:, kc, hf * 1536:(hf + 1) * 1536], in_=st[:], func=AF.Identity, scale=gcol)), reads=[st, NGs], writes=[Win])
                        wi += 1
                        yield
                    for kc in range(4):
                        st = stg[wi % 2]
                        P.dma("w%d" % (wi % 2), (lambda st=st, kc=kc: nc.sync.dma_start(out=st[:, 0:1024], in_=w_glu[l, kc * 128:(kc + 1) * 128, :])), writes=[st])
                        P.op("act", (lambda st=st, kc=kc: A.activation(out=Wglu[:, kc, :], in_=st[:, 0:1024], func=AF.Copy, scale=0.5)), reads=[st], writes=[Wglu])
                        wi += 1
                        yield
                    for kc in range(NCH):
                        st = stg[wi % 2]
                        P.dma("w%d" % (wi % 2), (lambda st=st, kc=kc: nc.sync.dma_start(out=st[:, 0:1024], in_=w_out[l, kc * 128:(kc + 1) * 128, :])), writes=[st])
                        wsc = 0.25 if kc < 4 else 0.5
                        P.op("act", (lambda st=st, kc=kc, wsc=wsc: A.activation(out=Wout[:, kc, :], in_=st[:, 0:1024], func=AF.Copy, scale=wsc)), reads=[st], writes=[Wout])
                        wi += 1
                        yield
                    for src, dst in ((b1h, B1), (b2h, B2), (l1h, L1), (l2h, L2)):
                        st = stg[wi % 2]
                        P.dma("w%d" % (wi % 2), (lambda st=st, src=src: nc.sync.dma_start(out=st[:, 0:1024], in_=src[l])), writes=[st])
                        P.op("pool", (lambda st=st, dst=dst: Pl.tensor_copy(dst.ap(0, 128, 0, [[1, 1024]]), st[:, 0:1024])), reads=[st], writes=[dst])
                        wi += 1
                        yield


                def tabgen():
                    NB = 2
                    for gb in range(G // NB):
                        ang = tb[0].ap(0, 128, 0, [[1, NB * 128]])
                        sinb = tb[1].ap(0, 128, 0, [[1, NB * 128]])
                        cosb = tb[2].ap(0, 128, 0, [[1, NB * 128]])
                        a3 = tb[3].ap(0, 128, 0, [[128, NB], [1, 128]])
                        ang3 = tb[0].ap(0, 128, 0, [[128, NB], [1, 128]])
                        sin3 = tb[1].ap(0, 128, 0, [[128, NB], [1, 128]])
                        cos3 = tb[2].ap(0, 128, 0, [[128, NB], [1, 128]])
                        thb = SM.ap(0, 128, I_TH * G + gb * NB, [[1, NB], [0, 128]])
                        freb = SM.ap(0, 128, I_FRE * G + gb * NB, [[1, NB], [0, 128]])
                        fimb = SM.ap(0, 128, I_FIM * G + gb * NB, [[1, NB], [0, 128]])
                        nfimb = SM.ap(0, 128, I_NFIM * G + gb * NB, [[1, NB], [0, 128]])
                        jb = CST.ap(0, 128, C_JROW, [[0, NB], [1, 128]])
                        dst = lambda Tn: Tn.ap(0, 128, gb * NB * 128, [[128, NB], [1, 128]])
                        P.op("dve", (lambda ang3=ang3, thb=thb, jb=jb: V.tensor_tensor(ang3, thb, jb, ALU.mult)), reads=[CST] + smb(I_TH), writes=[tb[0]])
                        sincos(ang, NB * 128, sinb, cosb, [tb[0]], [tb[1], tb[2]])
                        P.op("dve", (lambda a3=a3, cos3=cos3, freb=freb: V.tensor_tensor(a3, cos3, freb, ALU.mult)), reads=[tb[2]] + smb(I_FRE), writes=[tb[3]])
                        P.op("dve", (lambda ang3=ang3, sin3=sin3, fimb=fimb: V.tensor_tensor(ang3, sin3, fimb, ALU.mult)), reads=[tb[1]] + smb(I_FIM), writes=[tb[0]])
                        P.op("dve", (lambda a3=a3, ang3=ang3, d_=dst(PRE1): V.tensor_tensor(d_, a3, ang3, ALU.add)), reads=[tb[3], tb[0]], writes=[PRE1])
                        P.op("dve", (lambda a3=a3, sin3=sin3, freb=freb: V.tensor_tensor(a3, sin3, freb, ALU.mult)), reads=[tb[1]] + smb(I_FRE), writes=[tb[3]])
                        P.op("dve", (lambda ang3=ang3, cos3=cos3, nfimb=nfimb: V.tensor_tensor(ang3, cos3, nfimb, ALU.mult)), reads=[tb[2]] + smb(I_NFIM), writes=[tb[0]])
                        P.op("dve", (lambda a3=a3, ang3=ang3: V.tensor_tensor(a3, a3, ang3, ALU.add)), reads=[tb[3], tb[0]], writes=[tb[3]])
                        P.op("dve", (lambda a3=a3, d_=dst(PRE2): V.tensor_scalar(d_, a3, cSH, None, ALU.mult)), reads=[tb[3], CST], writes=[PRE2])
                        P.op("dve", (lambda cos3=cos3, d_=dst(POST1): V.tensor_scalar(d_, cos3, cSH, None, ALU.mult)), reads=[tb[2], CST], writes=[POST1])
                        P.op("dve", (lambda sin3=sin3, d_=dst(POST2): V.tensor_scalar(d_, sin3, -1.0, None, ALU.mult)), reads=[tb[1]], writes=[POST2])
                        yield


                wg_, tg_ = wload(), tabgen()
                alive = [wg_, tg_]
                while alive:
                    for g_, n_ in ((tg_, 1), (wg_, 3)):
                        if g_ in alive:
                            for _ in range(n_):
                                try:
                                    next(g_)
                                except StopIteration:
                                    alive.remove(g_)
                                    break
                P.dma("rbr", lambda: nc.sync.dma_start(out=RBR[:], in_=rbrep[:, l, :]), writes=[RBR])
                P.dma("qkr", lambda: nc.sync.dma_start(out=QKR[:], in_=qkrow[:, l, :]), writes=[QKR])
                P.dma("exs", lambda: nc.sync.dma_start(out=EXS[0:8, 0:257], in_=rb[l]), writes=[EXS])
                P.op("dve", lambda: V.tensor_copy(EXS[0:8, 257:384], EXS[0:8, 256:257].to_broadcast([8, 127])), reads=[EXS], writes=[EXS])
                P.dma("exd", lambda: nc.sync.dma_start(out=ext[l], in_=EXS[0:8, :]), reads=[EXS], writes=[HK])
                for i in range(16):
                    h = i % 8
                    off = 1 if i < 8 else 129
                    src = bass.AP(ext.tensor, (l * 8 + h) * 384 + off, [[1, 128], [1, 128]])
                    P.dma("hk", (lambda i=i, src=src: nc.sync.dma_start(out=HK[:, i, :], in_=src)), reads=[HK], writes=[HK])
                P.op("act", lambda: A.activation(out=RBR[:], in_=RBR[:], func=AF.Abs), reads=[RBR], writes=[RBR])
                P.op("dve", lambda: V.reduce_max(col(4), RBR[:], AX.X), reads=[RBR], writes=[COLB])
                P.op("act", lambda: A.activation(out=QKR[:], in_=QKR[:], func=AF.Abs), reads=[QKR], writes=[QKR])
                P.op("dve", lambda: V.reduce_max(col(2), QKR[:, 0:64], AX.X), reads=[QKR], writes=[COLB])
                P.op("dve", lambda: V.reduce_max(col(3), QKR[:, 64:128], AX.X), reads=[QKR], writes=[COLB])
                P.op("dve", lambda: V.tensor_tensor(col(5), col(2), col(3), ALU.mult), reads=[COLB], writes=[COLB])
                P.op("dve", lambda: V.tensor_scalar(col(5), col(5), 8.0, col(4), ALU.mult, ALU.add), reads=[COLB], writes=[COLB])
                P.op("dve", lambda: V.tensor_scalar(col(6), col(5), -1.0, None, ALU.mult), reads=[COLB], writes=[COLB])
                P.dma("rbr", lambda: nc.sync.dma_start(out=RBR[:], in_=rbrep[:, l, :]), reads=[RBR], writes=[RBR])
                P.op("dve", lambda: V.tensor_scalar(COL.ap(0, 128, 8, [[1, 8]]), RBR.ap(0, 128, 256, [[257, 8]]), col(5), None, ALU.subtract), reads=[RBR, COLB], writes=[COLB])
                P.op("dve", lambda: V.tensor_scalar(col(0), QKG.ap(0, 128, l * 2, [[1, 1]]), 0.125, None, ALU.mult), reads=[QKG], writes=[COLB])
                P.op("dve", lambda: V.tensor_copy(col(1), QKG.ap(0, 128, l * 2 + 1, [[1, 1]])), reads=[QKG], writes=[COLB])
                for i in range(16):
                    bk = prot.next()
                    P.op("pe", (lambda i=i, bk=bk: PE.matmul(bk[:, 0:128], cJ, HK[:, i, :], start=True, stop=True)), reads=[CST, HK], writes=[bk])
                    if i < 8:
                        P.op("dve", (lambda i=i, bk=bk: V.tensor_tensor(BT[:, i, :], bk[:, 0:128], cM4(128, 128), ALU.add)), reads=[bk, CST], writes=[BT])
                    else:
                        P.op("dve", (lambda i=i, bk=bk: V.tensor_copy(BT[:, i, :], bk[:, 0:128])), reads=[bk], writes=[BT])
                P.phase_end()

            with ExitStack() as s2:
                xT = [sb(s2, "xT%d" % i, [128, NCH, 128], F32) for i in range(2)]
                sq = sb(s2, "sq", [128, NCH, 128], BF16)
                hT = sb(s2, "hT", [128, NCH, 128], BF16)
                uT = [sb(s2, "uT%d" % i, [128, 4, 128], BF16) for i in range(2)]
                sgs = [sb(s2, "sgs%d" % i, [128, 4, 128], F32) for i in range(2)]
                sga = [sb(s2, "sga%d" % i, [128, 4, 128], F32) for i in range(2)]
                qT = [sb(s2, "qT%d" % i, [128, 4, 128], BF16) for i in range(2)]
                qsq = sb(s2, "qsq", [128, 2, 128], BF16)
                rstd = sb(s2, "rstd", [128, 128], F32)
                rq = sb(s2, "rq", [128, 2, 128], F32)
                kn32 = sb(s2, "kn32", [128, 4, 128], F32)
                v32 = sb(s2, "v32", [128, 512], F32)
                t1r = Ring([sb(s2, "t1_%d" % i, [128, 128], F32) for i in range(2)])
                t2r = Ring([sb(s2, "t2_%d" % i, [128, 128], F32) for i in range(2)])
                btr = Ring([sb(s2, "bt_%d" % i, [128, 128], F32) for i in range(2)])
                Gr = Ring([sb(s2, "G_%d" % i, [128, 128], F32) for i in range(2)])
                W1r = Ring([sb(s2, "W1_%d" % i, [128, 128], BF16) for i in range(2)])
                W2r = Ring([sb(s2, "W2_%d" % i, [128, 128], BF16) for i in range(2)])
                ypre = sb(s2, "ypre", [128, 4, 128], F32)
                tt = sb(s2, "tt", [128, 4, 128], F32)
                yg = sb(s2, "yg", [128, 4, 128], BF16)
                sg = sb(s2, "sg", [128, 4, 128], F32)
                mixT = sb(s2, "mixT", [128, NCH, 128], BF16)
                mixb = [Buf("mix%d" % i) for i in range(NCH)]
                stmpr = Ring([sb(s2, "stmp%d" % i, [128, 128], F32) for i in range(4)])
                pTr = Ring([sb(s2, "pT%d" % i, [128, 128], BF16) for i in range(6)])
                rsr = Ring([sb(s2, "rs%d" % i, [128, 128], F32) for i in range(2)])
                hring = Ring([banks[5], banks[6], banks[7]])
                wring = Ring(banks[3:5])
                C_GELU = math.sqrt(2.0 / math.pi)

                def v3(t, T, nchunk, c0=0, n=128, p0=0):
                    return t.ap(p0, n, c0 * 128, [[128, nchunk], [1, T]])

                def pv3(bk, T, nchunk=4, c0=0):
                    return bk.ap(0, 128, c0 * 128, [[128, nchunk], [1, T]])

                def rotate(dst_i, src_ap, src_bufs, ci, si, bk=None):
                    bk = bk if bk is not None else prot.next()
                    P.op("pe", lambda: PE.matmul(bk[:, 0:G], cSWN, src_ap, start=True, stop=True), reads=[CST] + src_bufs, writes=[bk])
                    P.op("dve", lambda: V.tensor_tensor(sm(I_T6), sm(ci), src_ap, ALU.mult), reads=smb(ci) + src_bufs, writes=smb(I_T6))
                    P.op("dve", lambda: V.tensor_tensor(sm(I_T7), sm(si), bk[:, 0:G], ALU.mult), reads=smb(si) + [bk], writes=smb(I_T7))
                    P.op("dve", lambda: V.tensor_tensor(sm(dst_i), sm(I_T6), sm(I_T7), ALU.add), reads=smb(I_T6, I_T7), writes=smb(dst_i))

                def load_x(src_ap, xb, T, bi):
                    P.dma("x%d" % bi, lambda: nc.sync.dma_start(out=v3(xb, T, NCH), in_=src_ap), writes=[xb])

                def rsqrt_act(dst_ap, src_ap, rd, wr):
                    P.op("act", lambda: A.activation(out=dst_ap, in_=src_ap, func=AF.Ln, bias=col(7), scale=1.0), reads=rd + [COLB], writes=wr)
                    P.op("act", lambda: A.activation(out=dst_ap, in_=dst_ap, func=AF.Exp, scale=-0.5), reads=wr, writes=wr)

                def head_stream(xb, T, ti, par, outs, gk):
                    slot = gk % 6
                    uTp, qTp, sgsp, sgap = uT[par], qT[par], sgs[par], sga[par]
                    P.op("act", lambda: A.activation(out=v3(sq, T, NCH), in_=v3(xb, T, NCH), func=AF.Square), reads=[xb], writes=[sq])
                    HB = hring.next()
                    for c in range(NCH):
                        P.op("pe", (lambda c=c, HB=HB: PE.matmul(HB[:, 0:T], ONESB[:], sq[:, c, 0:T], start=(c == 0), stop=(c == NCH - 1))), reads=[ONESB, sq], writes=[HB])
                    rsqrt_act(rstd[:, 0:T], HB[:, 0:T], [HB], [rstd])
                    P.op("dve", lambda: V.tensor_tensor(v3(hT, T, NCH), v3(xb, T, NCH), rstd.ap(0, 128, 0, [[0, NCH], [1, T]]), ALU.mult), reads=[xb, rstd], writes=[hT])
                    yield

                    def win_group(HB, cb, nch=4, c0=0):
                        for c in range(nch):
                            for kc in range(NCH):
                                P.op("pe", (lambda c=c, kc=kc, HB=HB: PE.matmul(HB[:, (c0 + c) * 128:(c0 + c) * 128 + T], Win[:, kc, cb + c * 128:cb + (c + 1) * 128], hT[:, kc, 0:T], start=(kc == 0), stop=(kc == NCH - 1))), reads=[Win, hT], writes=[HB])
                            if c % 2 == 1:
                                yield

                    for hf in range(2):
                        HB = hring.next()
                        yield from win_group(HB, 1024 + hf * 256, nch=2)
                        P.op("act", lambda HB=HB: A.activation(out=v3(qsq, T, 2), in_=pv3(HB, T, 2), func=AF.Square), reads=[HB], writes=[qsq])
                        for c in range(2):
                            P.op("pe", (lambda c=c, HB=HB: PE.matmul(HB[:, (2 + c) * 128:(2 + c) * 128 + T], BLKB[:], qsq[:, c, 0:T], start=True, stop=True)), reads=[BLKB, qsq], writes=[HB])
                        rsqrt_act(v3(rq, T, 2), pv3(HB, T, 2, c0=2), [HB], [rq])
                        P.op("dve", (lambda hf=hf, HB=HB: V.scalar_tensor_tensor(v3(qTp, T, 2, c0=2 * hf), pv3(HB, T, 2), col(0), v3(rq, T, 2), ALU.mult, ALU.mult)), reads=[HB, rq, COLB], writes=[qTp])
                        yield
                    for hf in range(2):
                        HB = hring.next()
                        yield from win_group(HB, 1536 + hf * 256, nch=2)
                        P.op("act", lambda HB=HB: A.activation(out=v3(qsq, T, 2), in_=pv3(HB, T, 2), func=AF.Square), reads=[HB], writes=[qsq])
                        for c in range(2):
                            P.op("pe", (lambda c=c, HB=HB: PE.matmul(HB[:, (2 + c) * 128:(2 + c) * 128 + T], BLKB[:], qsq[:, c, 0:T], start=True, stop=True)), reads=[BLKB, qsq], writes=[HB])
                        rsqrt_act(v3(rq, T, 2), pv3(HB, T, 2, c0=2), [HB], [rq])
                        P.op("dve", (lambda hf=hf, HB=HB: V.scalar_tensor_tensor(v3(kn32, T, 2, c0=2 * hf), pv3(HB, T, 2), col(1), v3(rq, T, 2), ALU.mult, ALU.mult)), reads=[HB, rq, COLB], writes=[kn32])
                        yield
                    P.op("pool", lambda: Pl.tensor_copy(kwin.ap(0, 128, slot * 128, [[768, 4], [1, T]]), v3(kn32, T, 4)), reads=[kn32], writes=[kslot[slot]])
                    if outs.get("nk") is not None:
                        P.dma("nk", lambda: nc.sync.dma_start(out=outs["nk"], in_=v3(kn32, T, 4)), reads=[kn32])
                    HB = hring.next()
                    yield from win_group(HB, 0)
                    P.op("act", lambda HB=HB: A.activation(func=AF.Copy, out=v3(uTp, T, 4), in_=pv3(HB, T)), reads=[HB], writes=[uTp])
                    HB = hring.next()
                    for kc in range(NCH):
                        P.op("pe", (lambda kc=kc, HB=HB: PE.matmul(HB[0:T, :], hT[:, kc, 0:T], Win[:, kc, 2048:2560], start=(kc == 0), stop=(kc == NCH - 1))), reads=[Win, hT], writes=[HB])
                    P.op("act", lambda HB=HB: A.activation(func=AF.Copy, out=vwin[0:T, slot, :], in_=HB[0:T, :]), reads=[HB], writes=[vslot[slot]])
                    if outs.get("nv") is not None:
                        P.op("dve", lambda HB=HB: V.tensor_copy(v32[0:T, :], HB[0:T, :]), reads=[HB], writes=[v32])
                        P.dma("nv", lambda: nc.sync.dma_start(out=outs["nv"], in_=v32[0:T, :]), reads=[v32])
                    yield
                    for cb, dstp in ((512, sgsp), (2560, sgap)):
                        HB = hring.next()
                        yield from win_group(HB, cb)
                        P.op("act", (lambda dstp=dstp, HB=HB: A.activation(out=v3(dstp, T, 4), in_=pv3(HB, T), func=AF.Tanh, scale=0.5)), reads=[HB], writes=[dstp])
                        P.op("dve", (lambda dstp=dstp, HB=HB: V.scalar_tensor_tensor(v3(dstp, T, 4), v3(dstp, T, 4), 1.0, pv3(HB, T), ALU.add, ALU.mult)), reads=[HB, dstp], writes=[dstp])
                        yield

                def s5_stream(T, par, outs):
                    uTp, sgsp = uT[par], sgs[par]

                    def front(g):
                        c, band, e = g // 8, (g % 8) // 2, g % 2
                        r0 = 32 * band
                        bkp = pqr.next()
                        P.op("pe", lambda: PE.matmul(bkp[:, 0:T], B1.ap(r0, 32, (c * 2 + e) * 128, [[1, 128]]), uTp.ap(r0, 32, c * 128, [[1, T]]), start=True, stop=True, tile_position=(r0, 0)), reads=[B1, uTp], writes=[bkp])
                        P.op("pe", lambda: PE.matmul(bkp[:, 128:128 + T], B2.ap(r0, 32, (c * 2 + e) * 128, [[1, 128]]), uTp.ap(r0, 32, c * 128, [[1, T]]), start=True, stop=True, tile_position=(r0, 0)), reads=[B2, uTp], writes=[bkp])
                        return bkp

                    def stageB(g, bkp):
                        t1, t2, bt_ = t1r.next(), t2r.next(), btr.next()
                        P.op("dve", lambda: V.tensor_tensor(t1[:, 0:T], bkp[:, 0:T], PRE1[:, g, 0:T], ALU.mult), reads=[bkp, PRE1], writes=[t1])
                        P.op("dve", lambda: V.tensor_tensor(t2[:, 0:T], bkp[:, 128:128 + T], PRE2[:, g, 0:T], ALU.mult), reads=[bkp, PRE2], writes=[t2])
                        P.op("pool", lambda: Pl.tensor_tensor(bt_[:, 0:T], t1[:, 0:T], t2[:, 0:T], ALU.add), reads=[t1, t2], writes=[bt_])
                        return bt_

                    def stageC(g, bt_):
                        Gt, W1, W2 = Gr.next(), W1r.next(), W2r.next()
                        P.op("dve", lambda: V.tensor_tensor_scan(Gt[:, 0:T], SM.ap(0, 128, I_R * G + g, [[0, T]]), bt_[:, 0:T], SM.ap(0, 128, I_INIT * G + g, [[1, 1]]), ALU.mult, ALU.add), reads=[bt_] + smb(I_R, I_INIT), writes=[Gt])
                        P.op("pool", lambda: Pl.tensor_tensor(W1[:, 0:T], Gt[:, 0:T], POST1[:, g, 0:T], ALU.mult), reads=[Gt, POST1], writes=[W1])
                        P.op("pool", lambda: Pl.tensor_tensor(W2[:, 0:T], Gt[:, 0:T], POST2[:, g, 0:T], ALU.mult), reads=[Gt, POST2], writes=[W2])
                        P.op("act", lambda: A.activation(func=AF.Copy, out=SM.ap(0, 128, I_GLAST * G + g, [[1, 1]]), in_=Gt[:, T - 1:T]), reads=[Gt], writes=smb(I_GLAST))
                        return W1, W2

                    def back(g, W1, W2):
                        c, band, e = g // 8, (g % 8) // 2, g % 2
                        r0 = 32 * band
                        o_ = ybank.ap(r0, 32, c * 128, [[1, T]])
                        P.op("pe", lambda: PE.matmul(o_, L1[:, g, :], W1[:, 0:T], start=(e == 0), stop=False, tile_position=(0, r0)), reads=[L1, W1], writes=[ybank])
                        P.op("pe", lambda: PE.matmul(o_, L2[:, g, :], W2[:, 0:T], start=False, stop=(e == 1), tile_position=(0, r0)), reads=[L2, W2], writes=[ybank])

                    sA, sB, sC = {}, {}, {}
                    for i in range(G + 3):
                        if i < G:
                            sA[i] = front(i)
                        if 0 <= i - 1 < G:
                            sB[i - 1] = stageB(i - 1, sA.pop(i - 1))
                        if 0 <= i - 2 < G:
                            sC[i - 2] = stageC(i - 2, sB.pop(i - 2))
                        if 0 <= i - 3 < G:
                            back(i - 3, *sC.pop(i - 3))
                        yield
                def tail_stream(xb, bi, T, par, dst_ap, outs):
                    uTp, sgsp = uT[par], sgs[par]
                    for c in range(4):
                        P.op("dve", (lambda c=c: V.scalar_tensor_tensor(ypre[:, c, 0:T], uTp[:, c, 0:T], DCOL.ap(0, 128, l * 4 + c, [[1, 1]]), ybank[:, c * 128:c * 128 + T], ALU.mult, ALU.add)), reads=[uTp, DCOL, ybank], writes=[ypre])
                    P.op("act", lambda: A.activation(out=v3(tt, T, 4), in_=v3(ypre, T, 4), func=AF.Square), reads=[ypre], writes=[tt])
                    P.op("dve", lambda: V.tensor_scalar(v3(tt, T, 4), v3(tt, T, 4), 0.044715, 1.0, ALU.mult, ALU.add), reads=[tt], writes=[tt])
                    P.op("dve", lambda: V.tensor_tensor(v3(tt, T, 4), v3(tt, T, 4), v3(ypre, T, 4), ALU.mult), reads=[tt, ypre], writes=[tt])
                    P.op("act", lambda: A.activation(out=v3(tt, T, 4), in_=v3(tt, T, 4), func=AF.Tanh, scale=C_GELU), reads=[tt], writes=[tt])
                    P.op("dve", lambda: V.scalar_tensor_tensor(v3(yg, T, 4), v3(tt, T, 4), 1.0, v3(ypre, T, 4), ALU.add, ALU.mult), reads=[tt, ypre], writes=[yg])
                    yield
                    if outs.get("st") is not None:
                        ci, si = outs["st_cs"]
                        rotate(I_HL, sm(I_GLAST), smb(I_GLAST), ci, si, GEN)
                        P.dma("st", lambda: nc.sync.dma_start(out=outs["st"], in_=sm(I_HL)), reads=smb(I_HL))
                    else:
                        rotate(I_INIT, sm(I_GLAST), smb(I_GLAST), I_CA, I_SA, GEN)
                    bva, bga = GEN2, ybank
                    for oc in (4, 5, 6, 7, 0, 1, 2, 3):
                        bk_ = bva if oc < 4 else bga
                        for kc in range(4):
                            P.op("pe", (lambda oc=oc, kc=kc, bk_=bk_: PE.matmul(bk_[:, (oc % 4) * 128:(oc % 4) * 128 + T], Wglu[:, kc, oc * 128:(oc + 1) * 128], yg[:, kc, 0:T], start=(kc == 0), stop=(kc == 3))), reads=[Wglu, yg], writes=[bk_])
                        if oc % 4 == 3:
                            yield
                    P.op("act", lambda: A.activation(out=v3(sg, T, 4), in_=pv3(bga, T), func=AF.Tanh, scale=0.5), reads=[bga], writes=[sg])
                    P.op("dve", lambda: V.scalar_tensor_tensor(v3(sg, T, 4), v3(sg, T, 4), 1.0, v3(sgsp, T, 4), ALU.add, ALU.mult), reads=[sg, sgsp], writes=[sg])
                    P.op("dve", lambda: V.tensor_tensor(v3(mixT, T, 4), pv3(bva, T), v3(sg, T, 4), ALU.mult), reads=[bva, sg], writes=mixb[0:4])
                    yield
                    bwa, bwb = wring.next(), wring.next()
                    for oc in range(NCH):
                        bk_ = bwa if oc < 4 else bwb
                        for kc in range(NCH):
                            P.op("pe", (lambda oc=oc, kc=kc, bk_=bk_: PE.matmul(bk_[:, (oc % 4) * 128:(oc % 4) * 128 + T], Wout[:, kc, oc * 128:(oc + 1) * 128], mixT[:, kc, 0:T], start=(kc == 0), stop=(kc == NCH - 1))), reads=[Wout, mixb[kc]], writes=[bk_])
                        if oc % 2 == 1:
                            yield
                    P.op("dve", lambda: V.tensor_tensor(v3(xb, T, 4), v3(xb, T, 4), pv3(bwa, T), ALU.add), reads=[xb, bwa], writes=[xb])
                    P.op("dve", lambda: V.tensor_tensor(v3(xb, T, 4, c0=4), v3(xb, T, 4, c0=4), pv3(bwb, T), ALU.add), reads=[xb, bwb], writes=[xb])
                    P.dma("y%d" % bi, lambda: nc.sync.dma_start(out=dst_ap, in_=v3(xb, T, NCH)), reads=[xb])

                def attn_stream(T, ti, par, gk):
                    qTp, sgap = qT[par], sga[par]
                    jl = [j for j in range(5) if ti - 4 + j >= 0]
                    items = [(h, j) for h in range(8) for j in jl]

                    def qk(h, j):
                        hp, hh = h // 2, h % 2
                        sl = (gk - 4 + j) % 6
                        nk = T if j == 4 else 128
                        bk_ = scr.next()
                        P.op("pe", lambda: PE.matmul(bk_[0:nk, 0:T], kwin.ap(64 * hh, 64, (hp * 6 + sl) * 128, [[1, nk]]), qTp.ap(64 * hh, 64, hp * 128, [[1, T]]), start=True, stop=True), reads=[kslot[sl], qTp], writes=[bk_])
                        return bk_, nk, sl

                    def soft(h, j, bk_, nk):
                        pT = pTr.next()
                        if j in (1, 2):
                            P.op("act", lambda: A.activation(out=pT[0:nk, 0:T], in_=bk_[0:nk, 0:T], func=AF.Exp, bias=col(8 + h, nk), scale=1.0), reads=[bk_, COLB], writes=[pT])
                        else:
                            stmp = stmpr.next()
                            if j == 0:
                                badd, bias_, rd = cM0(nk, T), col(8 + h, nk), [CST]
                            elif j == 3:
                                badd, bias_, rd = BT[0:nk, 8 + h, 0:T], col(6, nk), [BT]
                            else:
                                badd, bias_, rd = BT[0:nk, h, 0:T], col(6, nk), [BT]
                            P.op("dve", lambda: V.tensor_tensor(stmp[0:nk, 0:T], bk_[0:nk, 0:T], badd, ALU.add), reads=[bk_] + rd, writes=[stmp])
                            P.op("act", lambda: A.activation(out=pT[0:nk, 0:T], in_=stmp[0:nk, 0:T], func=AF.Exp, bias=bias_, scale=1.0), reads=[stmp, COLB], writes=[pT])
                        return pT

                    def pvmm_pair(hp, j, pts, cur):
                        first, last = (j == jl[0]), (j == jl[-1])
                        osb = OS[hp % 2]
                        for hh in range(2):
                            h = 2 * hp + hh
                            pT, (bk_, nk, sl) = pts[hh], cur[hh]
                            o_ = osb.ap(64 * hh, 64, 0, [[1, T]])
                            P.op("pe", (lambda h=h, hh=hh, pT=pT, nk=nk, sl=sl, o_=o_: PE.matmul(o_, vwin[0:nk, sl, h * 64:(h + 1) * 64], pT[0:nk, 0:T], start=first, stop=last, tile_position=(0, 64 * hh), skip_group_check=True)), reads=[vslot[sl], pT], writes=[osb])
                        for hh in range(2):
                            pT, (bk_, nk, sl) = pts[hh], cur[hh]
                            s_ = osb.ap(64 * hh, 64, 128, [[1, T]])
                            P.op("pe", (lambda hh=hh, pT=pT, nk=nk, s_=s_: PE.matmul(s_, ONE64B[0:nk, :], pT[0:nk, 0:T], start=False, stop=last, tile_position=(0, 64 * hh), skip_group_check=True)), reads=[ONE64B, pT], writes=[osb])
                        if last:
                            rs = rsr.next()
                            P.op("dve", lambda: V.reciprocal(rs[:, 0:T], osb[:, 128:128 + T]), reads=[osb], writes=[rs])
                            P.op("pool", lambda: Pl.tensor_tensor(rs[:, 0:T], rs[:, 0:T], sgap[:, hp, 0:T], ALU.mult), reads=[rs, sgap], writes=[rs])
                            P.op("dve", lambda: V.tensor_tensor(mixT[:, 4 + hp, 0:T], osb[:, 0:T], rs[:, 0:T], ALU.mult), reads=[osb, rs], writes=[mixb[4 + hp]])

                    pitems = [(hp, j) for hp in range(4) for j in jl]

                    def qk_pair(hp, j):
                        return [qk(2 * hp + hh, j) for hh in range(2)]

                    q = [qk_pair(*pitems[0])]
                    for idx in range(len(pitems)):
                        if idx + 1 < len(pitems):
                            q.append(qk_pair(*pitems[idx + 1]))
                        hp, j = pitems[idx]
                        cur = q.pop(0)
                        pts = [soft(2 * hp + hh, j, cur[hh][0], cur[hh][1]) for hh in range(2)]
                        pvmm_pair(hp, j, pts, cur)
                        yield

                def run_streams(streams):
                    streams = list(streams)
                    while streams:
                        for s_ in list(streams):
                            try:
                                next(s_)
                            except StopIteration:
                                streams.remove(s_)

                def body(xb, bi, T, ti, par, dst_ap, outs, nxt=None, gk=4):
                    run_streams([s5_stream(T, par, outs), attn_stream(T, ti, par, gk)])
                    streams = [tail_stream(xb, bi, T, par, dst_ap, outs)]
                    if nxt is not None:
                        streams.insert(0, nxt)
                    run_streams(streams)

                P.op("dve", lambda: V.memset(col(7), EPS), writes=[COLB])
                srcp, dstp = (xp, y1p) if l == 0 else (y1p, yp)
                srcs, dsts = (xs, y1s) if l == 0 else (y1s, ys)

                def p_outs(s, i):
                    outs = {}
                    if i >= NT - 4:
                        outs["nk"] = nkp[l, s, i - (NT - 4)]
                        outs["nv"] = nvp[l, s, (i - (NT - 4)) * 128:(i - (NT - 4) + 1) * 128, :]
                    if i == NT - 1:
                        outs["st"] = stp[l, s]
                        outs["st_cs"] = (I_CLP, I_SLP)
                    return outs

                tiles = [(s, i) for s in range(NSP) for i in range(NT)]
                load_x(srcp[0, 0], xT[0], 128, 0)
                run_streams([head_stream(xT[0], 128, 0, 0, p_outs(0, 0), 0)])
                for k, (s, i) in enumerate(tiles):
                    bi = k % 2
                    if i == 0:
                        P.op("dve", lambda: V.memset(sm(I_INIT), 0.0), writes=smb(I_INIT))
                    nxt = None
                    if k + 1 < len(tiles):
                        s2_, i2_ = tiles[k + 1]
                        load_x(srcp[s2_, i2_], xT[1 - bi], 128, 1 - bi)
                        nxt = head_stream(xT[1 - bi], 128, i2_, 1 - bi, p_outs(s2_, i2_), k + 1)
                    body(xT[bi], bi, 128, i, bi, dstp[s, i], p_outs(s, i), nxt, k)
                for s in range(NSS):
                    bi = s % 2
                    cstg = xT[1 - bi]
                    load_x(srcs[s], xT[bi], TS, bi)
                    for hf in range(2):
                        P.dma("cstg", (lambda s=s, hf=hf, cstg=cstg: nc.sync.dma_start(out=cstg.ap(0, 128, 0, [[512, 2], [1, 512]]), in_=ck[l, s, :, 2 * hf:2 * hf + 2, :])), writes=[cstg])
                        P.op("pool", (lambda hf=hf, cstg=cstg: Pl.tensor_copy(kwin.ap(0, 128, 2 * hf * 768, [[768, 2], [128, 4], [1, 128]]), cstg.ap(0, 128, 0, [[512, 2], [128, 4], [1, 128]]))), reads=[cstg], writes=kslot[0:4])
                    for hf in range(2):
                        P.dma("cstg", (lambda s=s, hf=hf, cstg=cstg: nc.sync.dma_start(out=cstg.ap(0, 128, 0, [[512, 2], [1, 512]]), in_=cv[l, s, hf * 256:(hf + 1) * 256, :].rearrange("(j k) e -> k j e", k=128))), writes=[cstg])
                        P.op("act", (lambda hf=hf, cstg=cstg: A.activation(func=AF.Copy, out=vwin[:, 2 * hf:2 * hf + 2, :], in_=cstg.ap(0, 128, 0, [[512, 2], [1, 512]]))), reads=[cstg], writes=vslot[0:4])
                    P.dma("st0", (lambda s=s: nc.sync.dma_start(out=sm(I_ST0), in_=st0[l, s])), writes=smb(I_ST0))
                    rotate(I_INIT, sm(I_ST0), smb(I_ST0), I_C1, I_S1)
                    outs = {"nk": nks[l, s], "nv": nvs[l, s], "st": sts[l, s], "st_cs": (I_CLS, I_SLS)}
                    run_streams([head_stream(xT[bi], TS, 4, bi, outs, 4)])
                    body(xT[bi], bi, TS, 4, bi, dsts[s], outs, None, 4)
                P.phase_end()
        print("[kernel] instructions:", P.n_inst, "semaphores:", P.nsem)
    return nc


def _consts():
    c = np.zeros((128, NCST), np.float32)
    idx = np.arange(128)
    c[idx, C_J + 127 - idx] = 1.0
    p = np.arange(64)
    c[64 + p, C_SWN + p] = -1.0
    c[p, C_SWN + 64 + p] = 1.0
    c[:, C_ONES:C_ONES + 128] = 1.0 / 1024.0
    c[0:64, C_BLK:C_BLK + 64] = 1.0 / 64.0
    c[64:128, C_BLK + 64:C_BLK + 128] = 1.0 / 64.0
    c[:, C_ONE64:C_ONE64 + 64] = 1.0
    c[64:128, C_M4:C_M4 + 64] = NEG
    c[0:64, C_M0 + 64:C_M0 + 128] = NEG
    c[:, C_JROW:C_JROW + 128] = np.arange(128, dtype=np.float32)[None]
    c[0:64, C_SH] = 1.0
    c[64:128, C_SH] = -1.0
    return c


_PROG_CACHE = {}


def kernel(x_prompt, x_sample, cache_k, cache_v, state_ssm_re, state_ssm_im, norm_gain, w_in,
           ssm_a_re, ssm_a_im, ssm_b_re, ssm_b_im, ssm_c_re, ssm_c_im, ssm_d, ssm_log_dt,
           w_glu, q_norm_gain, k_norm_gain, rel_bias, w_out, n_cores=8):
    f = np.float32
    x_prompt = np.asarray(x_prompt, f)
    x_sample = np.asarray(x_sample, f)
    B, SEQ, _ = x_prompt.shape
    BS = x_sample.shape[0]
    NSP, NSS = B // n_cores, BS // n_cores
    NT = SEQ // 128
    R = cache_k.shape[2]
    assert R == 512 and SEQ >= 512 and SEQ % 128 == 0 and x_sample.shape[1] == TS

    xpb = x_prompt.reshape(B, NT, 128, NCH, 128).transpose(0, 1, 4, 3, 2)
    xsb = x_sample.reshape(BS, TS, NCH, 128).transpose(0, 3, 2, 1)
    ckb = np.asarray(cache_k, f).reshape(L, BS, R, 4, 2, 64).transpose(0, 1, 4, 5, 3, 2).reshape(L, BS, 128, 4, R)
    cvb = np.asarray(cache_v, f).reshape(L, BS, R, 512)
    st = np.concatenate([np.asarray(state_ssm_re, f).transpose(0, 1, 3, 2), np.asarray(state_ssm_im, f).transpose(0, 1, 3, 2)], axis=2)
    ngh = np.asarray(norm_gain, f).reshape(L, NCH, 128).transpose(2, 0, 1)
    are = np.asarray(ssm_a_re, f).transpose(2, 0, 1)
    aim = np.asarray(ssm_a_im, f).transpose(2, 0, 1)
    ldt = np.broadcast_to(np.asarray(ssm_log_dt, f)[None], (64, L, G))
    s5 = np.stack([are, aim, ldt], axis=2)
    s5 = np.concatenate([s5, s5], axis=0)
    bre = np.asarray(ssm_b_re, f)
    bim = np.asarray(ssm_b_im, f)
    b1 = np.zeros((L, 128, 8, 128), f)
    b2 = np.zeros((L, 128, 8, 128), f)
    cre = np.asarray(ssm_c_re, f)
    cim = np.asarray(ssm_c_im, f)
    l1 = np.zeros((L, 128, G, 32), f)
    l2 = np.zeros((L, 128, G, 32), f)
    for g in range(G):
        c, band, e = g // 8, (g % 8) // 2, g % 2
        r0 = 32 * band + 16 * e
        b1[:, r0:r0 + 16, c * 2 + e, 0:64] = bre[:, g].transpose(0, 2, 1)
        b1[:, r0:r0 + 16, c * 2 + e, 64:128] = bim[:, g].transpose(0, 2, 1)
        b2[:, r0:r0 + 16, c * 2 + e, 0:64] = bim[:, g].transpose(0, 2, 1)
        b2[:, r0:r0 + 16, c * 2 + e, 64:128] = bre[:, g].transpose(0, 2, 1)
        l1[:, 0:64, g, 16 * e:16 * e + 16] = cre[:, g].transpose(0, 2, 1)
        l1[:, 64:128, g, 16 * e:16 * e + 16] = cim[:, g].transpose(0, 2, 1)
        l2[:, 0:64, g, 16 * e:16 * e + 16] = cim[:, g].transpose(0, 2, 1)
        l2[:, 64:128, g, 16 * e:16 * e + 16] = cre[:, g].transpose(0, 2, 1)
    dch = np.asarray(ssm_d, f).reshape(L, 4, 128).transpose(2, 0, 1)
    qg = np.asarray(q_norm_gain, f)
    kg = np.asarray(k_norm_gain, f)
    qkgh = np.stack([np.concatenate([qg, qg], 1), np.concatenate([kg, kg], 1)], axis=2).transpose(1, 0, 2)
    qkr = np.broadcast_to(np.concatenate([qg, kg], 1)[None], (128, L, 128))
    rbh = np.asarray(rel_bias, f)
    rbr = np.broadcast_to(rbh.reshape(1, L, 8 * 257), (128, L, 8 * 257))
    cst = _consts()

    key = (NT, NSP, NSS)
    nc = build_program(NT, NSP, NSS)
    shared = dict(w_in=np.ascontiguousarray(w_in, f), w_glu=np.ascontiguousarray(w_glu, f), w_out=np.ascontiguousarray(w_out, f),
                  ng=np.ascontiguousarray(ngh), s5a=np.ascontiguousarray(s5), b1h=b1.reshape(L, 128, 1024), b2h=b2.reshape(L, 128, 1024),
                  l1h=l1.reshape(L, 128, 1024), l2h=l2.reshape(L, 128, 1024), dcol=np.ascontiguousarray(dch),
                  qkg=np.ascontiguousarray(qkgh), qkrow=np.ascontiguousarray(qkr), rb=np.ascontiguousarray(rbh),
                  rbrep=np.ascontiguousarray(rbr), cst=cst)
    in_maps = []
    for c in range(n_cores):
        m = dict(shared)
        m["xp"] = np.ascontiguousarray(xpb[c * NSP:(c + 1) * NSP])
        m["xs"] = np.ascontiguousarray(xsb[c * NSS:(c + 1) * NSS])
        m["ck"] = np.ascontiguousarray(ckb[:, c * NSS:(c + 1) * NSS])
        m["cv"] = np.ascontiguousarray(cvb[:, c * NSS:(c + 1) * NSS])
        m["st0"] = np.ascontiguousarray(st[:, c * NSS:(c + 1) * NSS])
        in_maps.append(m)
    res = run_bass_kernel_spmd(nc, in_maps, core_ids=list(range(n_cores)))
    rs = res.results

    def cat(name, axis):
        return np.concatenate([np.asarray(r[name]) for r in rs], axis=axis)

    ypo = cat("yp", 0).transpose(0, 1, 4, 3, 2).reshape(B, SEQ, D)
    yso = cat("ys", 0).transpose(0, 3, 2, 1).reshape(BS, TS, D)
    nkpo = cat("nkp", 1)
    nkpo = nkpo.reshape(L, B, 4, 2, 64, 4, 128).transpose(0, 1, 2, 6, 5, 3, 4).reshape(L, B, 512, 8, 64)
    nvpo = cat("nvp", 1).reshape(L, B, 512, 8, 64)
    stpo = cat("stp", 1)
    pr = stpo[:, :, 0:64].transpose(0, 1, 3, 2)
    pi = stpo[:, :, 64:128].transpose(0, 1, 3, 2)
    nkso = cat("nks", 1).reshape(L, BS, 2, 64, 4, TS).transpose(0, 1, 5, 4, 2, 3).reshape(L, BS, TS, 8, 64)
    nvso = cat("nvs", 1).reshape(L, BS, TS, 8, 64)
    stso = cat("sts", 1)
    sr = stso[:, :, 0:64].transpose(0, 1, 3, 2)
    si = stso[:, :, 64:128].transpose(0, 1, 3, 2)
    c_ = lambda a: np.ascontiguousarray(a, dtype=np.float32)
    return (c_(ypo), c_(yso), c_(nkpo), c_(nvpo), c_(pr), c_(pi), c_(nkso), c_(nvso), c_(sr), c_(si))
```

```python
import math
from contextlib import ExitStack
import numpy as np
import concourse.bass as bass
import concourse.mybir as mybir
from concourse.bass_utils import run_bass_kernel_spmd

F32 = mybir.dt.float32
BF16 = mybir.dt.bfloat16
I32 = mybir.dt.int32
ALU = mybir.AluOpType
AF = mybir.ActivationFunctionType
AX = mybir.AxisListType

L = 2
D = 1024
NCH = 8
DIN = 3072
G = 32
TS = 16
EPS = 1e-6
NEG = -30000.0
TWO_PI = 2.0 * math.pi
STRICT = True

C_J, C_SWN, C_ONES, C_BLK, C_ONE64, C_M4, C_M0, C_JROW, C_SH = 0, 128, 256, 384, 512, 576, 704, 832, 960
NCST = 961


class Buf:
    def __init__(self, name):
        self.name = name
        self.w = None
        self.r = {}
        self.excl = False


class TT:
    def __init__(self, h, shape, dt, name, nbuf=None):
        self.h = h
        self.shape = shape
        self.row = int(np.prod(shape[1:]))
        self.dt = dt
        self.b = Buf(name)

    def __getitem__(self, idx):
        return self.h[idx]

    def ap(self, p0, npart, off, dims):
        return bass.AP(self.h, p0 * self.row + off, [[self.row, npart]] + [list(d) for d in dims])


class Op:
    __slots__ = ("eng", "fn", "deps", "needs_inc", "sem", "target", "is_dma", "epoch", "gen", "vc")


class Chan:
    def __init__(self, sem):
        self.sem = sem
        self.count = 0


class Prog:
    ENG = ("pe", "act", "dve", "pool", "sp")

    def __init__(self, nc, stack):
        self.nc = nc
        self.stack = stack
        self.e = {"pe": nc.tensor, "act": nc.scalar, "dve": nc.vector, "pool": nc.gpsimd, "sp": nc.sync}
        self.ops = []
        self.epoch = 0
        self.gen = 0
        self.sems = {}
        self.rank = {}
        self.seen = {}
        self.known = {}
        self.last = {}
        self.chans = {}
        self.nsem = 0
        self.n_inst = 0

    def _sem(self, name):
        self.nsem += 1
        return self.stack.enter_context(self.nc.semaphore(name))

    def chan(self, name):
        if name not in self.chans:
            self.chans[name] = Chan(self._sem("c_" + name))
        return self.chans[name]

    def _deps(self, eng, reads, writes):
        deps = []
        for b in reads:
            if b.w is not None:
                deps.append(b.w)
            if b.excl:
                for en_, o_ in b.r.items():
                    if en_ != eng:
                        deps.append(o_)
        for b in writes:
            if b.w is not None:
                deps.append(b.w)
            deps.extend(b.r.values())
        out = []
        for d in deps:
            if d.gen != self.gen:
                continue
            if (not d.is_dma) and d.eng == eng and (eng == "pe" or not STRICT):
                continue
            d.needs_inc = True
            out.append(d)
        return out

    def op(self, eng, fn, reads=(), writes=()):
        reads = [x.b if hasattr(x, 'b') else x for x in reads]
        writes = [x.b if hasattr(x, 'b') else x for x in writes]
        o = Op()
        o.eng = eng
        o.fn = fn
        o.is_dma = False
        o.needs_inc = False
        o.sem = None
        o.target = 0
        o.epoch = self.epoch
        o.gen = self.gen
        o.deps = self._deps(eng, reads, writes)
        for b in writes:
            b.w = o
            b.r = {}
        for b in reads:
            b.r[eng] = o
        self.ops.append(o)
        return o

    def dma(self, chan, fn, reads=(), writes=()):
        reads = [x.b if hasattr(x, 'b') else x for x in reads]
        writes = [x.b if hasattr(x, 'b') else x for x in writes]
        ch = self.chan(chan)
        o = Op()
        o.eng = "sp"
        o.fn = fn
        o.is_dma = True
        o.needs_inc = True
        ch.count += 1
        o.sem = ch.sem
        o.target = 16 * ch.count
        o.epoch = self.epoch
        o.gen = self.gen
        o.deps = self._deps("sp", reads, writes)
        for b in writes:
            b.w = o
            b.r = {}
        for b in reads:
            b.r["dma_" + chan] = o
        self.ops.append(o)
        return o

    def emit(self):
        known = self.known
        for o in self.ops:
            e = self.e[o.eng]
            kn = known.setdefault(o.eng, {})
            need = {}
            for d in o.deps:
                k = id(d.sem)
                if k not in need or need[k][1] < d.target:
                    need[k] = (d.sem, d.target, d)
            for k, (sem, tgt, d) in need.items():
                if kn.get(k, 0) >= tgt:
                    continue
                e.wait_ge(sem, tgt)
                self.n_inst += 1
                kn[k] = tgt
                for k2, v2 in d.vc.items():
                    if kn.get(k2, 0) < v2:
                        kn[k2] = v2
            inst = o.fn()
            self.n_inst += 1
            if o.is_dma:
                inst.then_inc(o.sem, 16)
                o.vc = dict(kn)
            else:
                self.last[o.eng] = o
                if o.needs_inc:
                    key = (o.eng, o.epoch)
                    if key not in self.sems:
                        self.sems[key] = self._sem("e_%s_%d" % key)
                        self.rank[key] = 0
                    self.rank[key] += 1
                    o.sem = self.sems[key]
                    o.target = self.rank[key]
                    inst.then_inc(o.sem, 1)
                    o.vc = dict(kn)
        self.ops = []

    def phase_end(self):
        lastops = {}
        for o in self.ops:
            if not o.is_dma:
                lastops[o.eng] = o
        for o in lastops.values():
            o.needs_inc = True
        self.emit()
        for en in self.ENG:
            e = self.e[en]
            kn = self.known.setdefault(en, {})
            for fn_, o in self.last.items():
                if fn_ == en or o.sem is None:
                    continue
                if kn.get(id(o.sem), 0) >= o.target:
                    continue
                kn[id(o.sem)] = o.target
                e.wait_ge(o.sem, o.target)
            for ch in self.chans.values():
                if ch.count == 0:
                    continue
                if kn.get(id(ch.sem), 0) >= 16 * ch.count:
                    continue
                kn[id(ch.sem)] = 16 * ch.count
                e.wait_ge(ch.sem, 16 * ch.count)
        self.gen += 1
        self.epoch += 1


class PS:
    def __init__(self, bank, c0, name):
        self.bank = bank
        self.c0 = c0
        self.b = Buf(name)

    def ap(self, p0, n, off, dims):
        return self.bank.ap(p0, n, self.c0 + off, dims)

    def v(self, n, T, off=0, p0=0):
        return self.bank.ap(p0, n, self.c0 + off, [[1, T]])


class Ring:
    def __init__(self, items):
        self.items = items
        self.i = 0

    def next(self):
        x = self.items[self.i % len(self.items)]
        self.i += 1
        return x


def build_program(NT, NSP, NSS):
    nc = bass.Bass("TRN2", target_bir_lowering=False)

    def din(name, shape):
        return nc.dram_tensor(name, list(shape), F32, kind="ExternalInput").ap()

    def dout(name, shape):
        return nc.dram_tensor(name, list(shape), F32, kind="ExternalOutput").ap()

    xp = din("xp", [NSP, NT, 128, NCH, 128])
    xs = din("xs", [NSS, 128, NCH, TS])
    ck = din("ck", [L, NSS, 128, 4, 512])
    cv = din("cv", [L, NSS, 512, 512])
    st0 = din("st0", [L, NSS, 128, G])
    w_in = din("w_in", [L, D, DIN])
    w_glu = din("w_glu", [L, 512, 1024])
    w_out = din("w_out", [L, D, D])
    ng = din("ng", [128, L, NCH])
    s5a = din("s5a", [128, L, 3, G])
    b1h = din("b1h", [L, 128, 1024])
    b2h = din("b2h", [L, 128, 1024])
    l1h = din("l1h", [L, 128, 1024])
    l2h = din("l2h", [L, 128, 1024])
    dcol = din("dcol", [128, L, 4])
    qkg = din("qkg", [128, L, 2])
    qkrow = din("qkrow", [128, L, 128])
    rb = din("rb", [L, 8, 257])
    rbrep = din("rbrep", [128, L, 8 * 257])
    cst = din("cst", [128, NCST])

    yp = dout("yp", [NSP, NT, 128, NCH, 128])
    ys = dout("ys", [NSS, 128, NCH, TS])
    nkp = dout("nkp", [L, NSP, 4, 128, 4, 128])
    nvp = dout("nvp", [L, NSP, 512, 512])
    stp = dout("stp", [L, NSP, 128, G])
    nks = dout("nks", [L, NSS, 128, 4, TS])
    nvs = dout("nvs", [L, NSS, TS, 512])
    sts = dout("sts", [L, NSS, 128, G])
    y1p = nc.dram_tensor("y1p", [NSP, NT, 128, NCH, 128], F32, kind="Internal").ap()
    y1s = nc.dram_tensor("y1s", [NSS, 128, NCH, TS], F32, kind="Internal").ap()
    ext = nc.dram_tensor("ext", [L, 8, 384], F32, kind="Internal").ap()

    with ExitStack() as glob:
        P = Prog(nc, glob)

        uniq = [0]

        def sb(stack, name, shape, dt=F32):
            uniq[0] += 1
            name = "%s_%d" % (name, uniq[0])
            h = stack.enter_context(nc.sbuf_tensor(name, list(shape), dt))
            return TT(h, list(shape), dt, name)

        def ps(stack, name):
            h = stack.enter_context(nc.psum_tensor(name, [128, 512], F32))
            t_ = TT(h, [128, 512], F32, name)
            t_.b.excl = True
            return t_

        Win = sb(glob, "Win", [128, NCH, DIN], BF16)
        Wglu = sb(glob, "Wglu", [128, 4, 1024], BF16)
        Wout = sb(glob, "Wout", [128, NCH, D], BF16)
        PRE1 = sb(glob, "PRE1", [128, G, 128], BF16)
        PRE2 = sb(glob, "PRE2", [128, G, 128], BF16)
        POST1 = sb(glob, "POST1", [128, G, 128], BF16)
        POST2 = sb(glob, "POST2", [128, G, 128], BF16)
        B1 = sb(glob, "B1", [128, 8, 128], BF16)
        B2 = sb(glob, "B2", [128, 8, 128], BF16)
        L1 = sb(glob, "L1", [128, G, 32], BF16)
        L2 = sb(glob, "L2", [128, G, 32], BF16)
        CST = sb(glob, "CST", [128, NCST], F32)
        ONESB = sb(glob, "ONESB", [128, 128], BF16)
        BLKB = sb(glob, "BLKB", [128, 128], BF16)
        ONE64B = sb(glob, "ONE64B", [128, 64], BF16)
        BT = sb(glob, "BT", [128, 16, 128], F32)
        kwin = sb(glob, "kwin", [128, 4, 6, 128], BF16)
        vwin = sb(glob, "vwin", [128, 6, 512], BF16)
        kslot = [Buf("ks%d" % i) for i in range(6)]
        vslot = [Buf("vs%d" % i) for i in range(6)]
        SM = sb(glob, "SM", [128, 40, G], F32)
        SMB = [Buf("sm%d" % i) for i in range(40)]
        COL = sb(glob, "COL", [128, 64], F32)
        COLB = Buf("col")
        NGs = sb(glob, "NGs", [128, L, NCH], F32)
        S5A = sb(glob, "S5A", [128, L, 3, G], F32)
        DCOL = sb(glob, "DCOL", [128, L, 4], F32)
        QKG = sb(glob, "QKG", [128, L, 2], F32)
        banks = [ps(glob, "pb%d" % i) for i in range(8)]
        ybank = banks[0]
        pqr = Ring(banks[1:3])
        scr = Ring(banks[3:7])
        OS = [banks[7], banks[7]]
        GEN = banks[1]
        GEN2 = banks[2]
        prot = Ring(banks[1:8])

        (I_ARE, I_AIM, I_LDT, I_DT, I_R, I_TH, I_S1, I_C1, I_ABRE, I_ABIM, I_NR, I_DEN, I_FRE, I_FIM,
         I_CA, I_SA, I_CLP, I_SLP, I_CLS, I_SLS, I_INIT, I_GLAST, I_T0, I_T1, I_T2, I_T3, I_T4, I_T5, I_T6,
         I_T7, I_HL, I_ST0, I_NFIM) = range(33)

        def sm(i, n=128):
            return SM.ap(0, n, i * G, [[1, G]])

        def col(i, n=128, p0=0):
            return COL.ap(p0, n, i, [[1, 1]])

        cJ = CST.ap(0, 128, C_J, [[1, 128]])
        cSWN = CST.ap(0, 128, C_SWN, [[1, 128]])
        cM4 = lambda nk, T: CST.ap(0, nk, C_M4, [[1, T]])
        cM0 = lambda nk, T: CST.ap(0, nk, C_M0, [[1, T]])
        cSH = CST.ap(0, 128, C_SH, [[1, 1]])

        V, A, Pl, PE = nc.vector, nc.scalar, nc.gpsimd, nc.tensor

        P.dma("cst", lambda: nc.sync.dma_start(out=CST[:], in_=cst[:]), writes=[CST])
        P.dma("cst2", lambda: nc.sync.dma_start(out=NGs[:], in_=ng[:]), writes=[NGs])
        P.dma("cst3", lambda: nc.sync.dma_start(out=S5A[:], in_=s5a[:]), writes=[S5A])
        P.dma("cst4", lambda: nc.sync.dma_start(out=DCOL[:], in_=dcol[:]), writes=[DCOL])
        P.dma("cst5", lambda: nc.sync.dma_start(out=QKG[:], in_=qkg[:]), writes=[QKG])
        P.op("dve", lambda: V.tensor_copy(ONESB[:], CST.ap(0, 128, C_ONES, [[1, 128]])), reads=[CST], writes=[ONESB])
        P.op("dve", lambda: V.tensor_copy(BLKB[:], CST.ap(0, 128, C_BLK, [[1, 128]])), reads=[CST], writes=[BLKB])
        P.op("dve", lambda: V.tensor_copy(ONE64B[:], CST.ap(0, 128, C_ONE64, [[1, 64]])), reads=[CST], writes=[ONE64B])
        P.phase_end()

        for l in range(L):
            with ExitStack() as s1:
                stg = [sb(s1, "stg%d" % i, [128, 1536], F32) for i in range(2)]
                tmpN = 256
                tb = [sb(s1, "tb%d" % i, [128, tmpN], F32) for i in range(8)]
                tbi = sb(s1, "tbi", [128, tmpN], I32)
                HK = sb(s1, "HK", [128, 16, 128], F32)
                RBR = sb(s1, "RBR", [128, 8 * 257], F32)
                QKR = sb(s1, "QKR", [128, 128], F32)
                EXS = sb(s1, "EXS", [8, 384], F32)

                def sa(i):
                    return S5A.ap(0, 128, (l * 3 + i) * G, [[1, G]])

                def sincos(ang, n, out_sin, out_cos, rd, wr):
                    t, kf, d, m = (tb[4].ap(0, 128, 0, [[1, n]]), tb[5].ap(0, 128, 0, [[1, n]]),
                                   tb[6].ap(0, 128, 0, [[1, n]]), tb[7].ap(0, 128, 0, [[1, n]]))
                    ki = tbi.ap(0, 128, 0, [[1, n]])
                    for off, outp in ((0.0, out_sin), (0.25, out_cos)):
                        P.op("dve", (lambda off=off: V.tensor_scalar(t, ang, 1.0 / TWO_PI, off, ALU.mult, ALU.add)), reads=rd + [tb[4]], writes=[tb[4]])
                        P.op("dve", lambda: V.tensor_copy(ki, t), reads=[tb[4]], writes=[tbi])
                        P.op("dve", lambda: V.tensor_copy(kf, ki), reads=[tbi], writes=[tb[5]])
                        P.op("dve", lambda: V.tensor_tensor(d, t, kf, ALU.subtract), reads=[tb[4], tb[5]], writes=[tb[6]])
                        P.op("dve", lambda: V.tensor_scalar(m, d, 0.5, None, ALU.is_gt), reads=[tb[6]], writes=[tb[7]])
                        P.op("dve", lambda: V.tensor_tensor(d, d, m, ALU.subtract), reads=[tb[6], tb[7]], writes=[tb[6]])
                        P.op("dve", lambda: V.tensor_scalar(m, d, -0.5, None, ALU.is_lt), reads=[tb[6]], writes=[tb[7]])
                        P.op("dve", lambda: V.tensor_tensor(d, d, m, ALU.add), reads=[tb[6], tb[7]], writes=[tb[6]])
                        P.op("act", (lambda outp=outp: A.activation(out=outp, in_=d, func=AF.Sin, scale=TWO_PI * (1.0 - 2e-6))), reads=[tb[6]], writes=wr)

                smb = lambda *ids: [SMB[i] for i in ids]
                P.op("act", lambda: A.activation(out=sm(I_DT), in_=sa(2), func=AF.Exp), reads=[S5A], writes=smb(I_DT))
                P.op("dve", lambda: V.tensor_tensor(sm(I_T0), sm(I_DT), sa(0), ALU.mult), reads=[S5A] + smb(I_DT), writes=smb(I_T0))
                P.op("act", lambda: A.activation(out=sm(I_R), in_=sm(I_T0), func=AF.Exp), reads=smb(I_T0), writes=smb(I_R))
                P.op("dve", lambda: V.tensor_tensor(sm(I_TH), sm(I_DT), sa(1), ALU.mult), reads=[S5A] + smb(I_DT), writes=smb(I_TH))
                sincos(sm(I_TH), G, sm(I_S1), sm(I_C1), smb(I_TH), smb(I_S1, I_C1))
                for mult_, (ci, si) in ((128.0, (I_CA, I_SA)), (127.0, (I_CLP, I_SLP)), (float(TS - 1), (I_CLS, I_SLS))):
                    P.op("dve", (lambda mult_=mult_: V.tensor_scalar(sm(I_T1), sm(I_TH), mult_, None, ALU.mult)), reads=smb(I_TH), writes=smb(I_T1))
                    sincos(sm(I_T1), G, sm(si), sm(ci), smb(I_T1), smb(si, ci))
                P.op("dve", lambda: V.tensor_tensor(sm(I_ABRE), sm(I_R), sm(I_C1), ALU.mult), reads=smb(I_R, I_C1), writes=smb(I_ABRE))
                P.op("dve", lambda: V.tensor_tensor(sm(I_ABIM), sm(I_R), sm(I_S1), ALU.mult), reads=smb(I_R, I_S1), writes=smb(I_ABIM))
                P.op("dve", lambda: V.tensor_scalar(sm(I_NR), sm(I_ABRE), -1.0, None, ALU.add), reads=smb(I_ABRE), writes=smb(I_NR))
                P.op("dve", lambda: V.tensor_tensor(sm(I_T0), sa(0), sa(0), ALU.mult), reads=[S5A], writes=smb(I_T0))
                P.op("dve", lambda: V.tensor_tensor(sm(I_T1), sa(1), sa(1), ALU.mult), reads=[S5A], writes=smb(I_T1))
                P.op("dve", lambda: V.tensor_tensor(sm(I_DEN), sm(I_T0), sm(I_T1), ALU.add), reads=smb(I_T0, I_T1), writes=smb(I_DEN))
                P.op("dve", lambda: V.reciprocal(sm(I_DEN), sm(I_DEN)), reads=smb(I_DEN), writes=smb(I_DEN))
                P.op("dve", lambda: V.tensor_tensor(sm(I_T0), sm(I_NR), sa(0), ALU.mult), reads=[S5A] + smb(I_NR), writes=smb(I_T0))
                P.op("dve", lambda: V.tensor_tensor(sm(I_T1), sm(I_ABIM), sa(1), ALU.mult), reads=[S5A] + smb(I_ABIM), writes=smb(I_T1))
                P.op("dve", lambda: V.tensor_tensor(sm(I_T0), sm(I_T0), sm(I_T1), ALU.add), reads=smb(I_T0, I_T1), writes=smb(I_T0))
                P.op("dve", lambda: V.tensor_tensor(sm(I_FRE), sm(I_T0), sm(I_DEN), ALU.mult), reads=smb(I_T0, I_DEN), writes=smb(I_FRE))
                P.op("dve", lambda: V.tensor_tensor(sm(I_T2), sm(I_ABIM), sa(0), ALU.mult), reads=[S5A] + smb(I_ABIM), writes=smb(I_T2))
                P.op("dve", lambda: V.tensor_tensor(sm(I_T3), sm(I_NR), sa(1), ALU.mult), reads=[S5A] + smb(I_NR), writes=smb(I_T3))
                P.op("dve", lambda: V.tensor_tensor(sm(I_T2), sm(I_T2), sm(I_T3), ALU.subtract), reads=smb(I_T2, I_T3), writes=smb(I_T2))
                P.op("dve", lambda: V.tensor_tensor(sm(I_FIM), sm(I_T2), sm(I_DEN), ALU.mult), reads=smb(I_T2, I_DEN), writes=smb(I_FIM))
                P.op("dve", lambda: V.tensor_scalar(sm(I_NFIM), sm(I_FIM), -1.0, None, ALU.mult), reads=smb(I_FIM), writes=smb(I_NFIM))

                def wload():
                    wi = 0
                    for kc2 in range(2 * NCH):
                        kc, hf = kc2 // 2, kc2 % 2
                        st = stg[wi % 2]
                        P.dma("w%d" % (wi % 2), (lambda st=st, kc=kc, hf=hf: nc.sync.dma_start(out=st[:], in_=w_in[l, kc * 128:(kc + 1) * 128, hf * 1536:(hf + 1) * 1536])), writes=[st])
                        gcol = NGs.ap(0, 128, l * NCH + kc, [[1, 1]])
                        P.op("act", (lambda st=st, kc=kc, hf=hf, gcol=gcol: A.activation(out=Win[:, kc, hf * 1536:(hf + 1) * 1536], in_=st[:], func=AF.Identity, scale=gcol)), reads=[st, NGs], writes=[Win])
                        wi += 1
                        yield
                    for kc in range(4):
                        st = stg[wi % 2]
                        P.dma("w%d" % (wi % 2), (lambda st=st, kc=kc: nc.sync.dma_start(out=st[:, 0:1024], in_=w_glu[l, kc * 128:(kc + 1) * 128, :])), writes=[st])
                        P.op("act", (lambda st=st, kc=kc: A.activation(out=Wglu[:, kc, :], in_=st[:, 0:1024], func=AF.Copy, scale=0.5)), reads=[st], writes=[Wglu])
                        wi += 1
                        yield
                    for kc in range(NCH):
                        st = stg[wi % 2]
                        P.dma("w%d" % (wi % 2), (lambda st=st, kc=kc: nc.sync.dma_start(out=st[:, 0:1024], in_=w_out[l, kc * 128:(kc + 1) * 128, :])), writes=[st])
                        wsc = 0.25 if kc < 4 else 0.5
                        P.op("act", (lambda st=st, kc=kc, wsc=wsc: A.activation(out=Wout[:, kc, :], in_=st[:, 0:1024], func=AF.Copy, scale=wsc)), reads=[st], writes=[Wout])
                        wi += 1
                        yield
                    for src, dst in ((b1h, B1), (b2h, B2), (l1h, L1), (l2h, L2)):
                        st = stg[wi % 2]
                        P.dma("w%d" % (wi % 2), (lambda st=st, src=src: nc.sync.dma_start(out=st[:, 0:1024], in_=src[l])), writes=[st])
                        P.op("pool", (lambda st=st, dst=dst: Pl.tensor_copy(dst.ap(0, 128, 0, [[1, 1024]]), st[:, 0:1024])), reads=[st], writes=[dst])
                        wi += 1
                        yield


                def tabgen():
                    NB = 2
                    for gb in range(G // NB):
                        ang = tb[0].ap(0, 128, 0, [[1, NB * 128]])
                        sinb = tb[1].ap(0, 128, 0, [[1, NB * 128]])
                        cosb = tb[2].ap(0, 128, 0, [[1, NB * 128]])
                        a3 = tb[3].ap(0, 128, 0, [[128, NB], [1, 128]])
                        ang3 = tb[0].ap(0, 128, 0, [[128, NB], [1, 128]])
                        sin3 = tb[1].ap(0, 128, 0, [[128, NB], [1, 128]])
                        cos3 = tb[2].ap(0, 128, 0, [[128, NB], [1, 128]])
                        thb = SM.ap(0, 128, I_TH * G + gb * NB, [[1, NB], [0, 128]])
                        freb = SM.ap(0, 128, I_FRE * G + gb * NB, [[1, NB], [0, 128]])
                        fimb = SM.ap(0, 128, I_FIM * G + gb * NB, [[1, NB], [0, 128]])
                        nfimb = SM.ap(0, 128, I_NFIM * G + gb * NB, [[1, NB], [0, 128]])
                        jb = CST.ap(0, 128, C_JROW, [[0, NB], [1, 128]])
                        dst = lambda Tn: Tn.ap(0, 128, gb * NB * 128, [[128, NB], [1, 128]])
                        P.op("dve", (lambda ang3=ang3, thb=thb, jb=jb: V.tensor_tensor(ang3, thb, jb, ALU.mult)), reads=[CST] + smb(I_TH), writes=[tb[0]])
                        sincos(ang, NB * 128, sinb, cosb, [tb[0]], [tb[1], tb[2]])
                        P.op("dve", (lambda a3=a3, cos3=cos3, freb=freb: V.tensor_tensor(a3, cos3, freb, ALU.mult)), reads=[tb[2]] + smb(I_FRE), writes=[tb[3]])
                        P.op("dve", (lambda ang3=ang3, sin3=sin3, fimb=fimb: V.tensor_tensor(ang3, sin3, fimb, ALU.mult)), reads=[tb[1]] + smb(I_FIM), writes=[tb[0]])
                        P.op("dve", (lambda a3=a3, ang3=ang3, d_=dst(PRE1): V.tensor_tensor(d_, a3, ang3, ALU.add)), reads=[tb[3], tb[0]], writes=[PRE1])
                        P.op("dve", (lambda a3=a3, sin3=sin3, freb=freb: V.tensor_tensor(a3, sin3, freb, ALU.mult)), reads=[tb[1]] + smb(I_FRE), writes=[tb[3]])
                        P.op("dve", (lambda ang3=ang3, cos3=cos3, nfimb=nfimb: V.tensor_tensor(ang3, cos3, nfimb, ALU.mult)), reads=[tb[2]] + smb(I_NFIM), writes=[tb[0]])
                        P.op("dve", (lambda a3=a3, ang3=ang3: V.tensor_tensor(a3, a3, ang3, ALU.add)), reads=[tb[3], tb[0]], writes=[tb[3]])
                        P.op("dve", (lambda a3=a3, d_=dst(PRE2): V.tensor_scalar(d_, a3, cSH, None, ALU.mult)), reads=[tb[3], CST], writes=[PRE2])
                        P.op("dve", (lambda cos3=cos3, d_=dst(POST1): V.tensor_scalar(d_, cos3, cSH, None, ALU.mult)), reads=[tb[2], CST], writes=[POST1])
                        P.op("dve", (lambda sin3=sin3, d_=dst(POST2): V.tensor_scalar(d_, sin3, -1.0, None, ALU.mult)), reads=[tb[1]], writes=[POST2])
                        yield


                wg_, tg_ = wload(), tabgen()
                alive = [wg_, tg_]
                while alive:
                    for g_, n_ in ((tg_, 1), (wg_, 3)):
                        if g_ in alive:
                            for _ in range(n_):
                                try:
                                    next(g_)
                                except StopIteration:
                                    alive.remove(g_)
                                    break
                P.dma("rbr", lambda: nc.sync.dma_start(out=RBR[:], in_=rbrep[:, l, :]), writes=[RBR])
                P.dma("qkr", lambda: nc.sync.dma_start(out=QKR[:], in_=qkrow[:, l, :]), writes=[QKR])
                P.dma("exs", lambda: nc.sync.dma_start(out=EXS[0:8, 0:257], in_=rb[l]), writes=[EXS])
                P.op("dve", lambda: V.tensor_copy(EXS[0:8, 257:384], EXS[0:8, 256:257].to_broadcast([8, 127])), reads=[EXS], writes=[EXS])
                P.dma("exd", lambda: nc.sync.dma_start(out=ext[l], in_=EXS[0:8, :]), reads=[EXS], writes=[HK])
                for i in range(16):
                    h = i % 8
                    off = 1 if i < 8 else 129
                    src = bass.AP(ext.tensor, (l * 8 + h) * 384 + off, [[1, 128], [1, 128]])
                    P.dma("hk", (lambda i=i, src=src: nc.sync.dma_start(out=HK[:, i, :], in_=src)), reads=[HK], writes=[HK])
                P.op("act", lambda: A.activation(out=RBR[:], in_=RBR[:], func=AF.Abs), reads=[RBR], writes=[RBR])
                P.op("dve", lambda: V.reduce_max(col(4), RBR[:], AX.X), reads=[RBR], writes=[COLB])
                P.op("act", lambda: A.activation(out=QKR[:], in_=QKR[:], func=AF.Abs), reads=[QKR], writes=[QKR])
                P.op("dve", lambda: V.reduce_max(col(2), QKR[:, 0:64], AX.X), reads=[QKR], writes=[COLB])
                P.op("dve", lambda: V.reduce_max(col(3), QKR[:, 64:128], AX.X), reads=[QKR], writes=[COLB])
                P.op("dve", lambda: V.tensor_tensor(col(5), col(2), col(3), ALU.mult), reads=[COLB], writes=[COLB])
                P.op("dve", lambda: V.tensor_scalar(col(5), col(5), 8.0, col(4), ALU.mult, ALU.add), reads=[COLB], writes=[COLB])
                P.op("dve", lambda: V.tensor_scalar(col(6), col(5), -1.0, None, ALU.mult), reads=[COLB], writes=[COLB])
                P.dma("rbr", lambda: nc.sync.dma_start(out=RBR[:], in_=rbrep[:, l, :]), reads=[RBR], writes=[RBR])
                P.op("dve", lambda: V.tensor_scalar(COL.ap(0, 128, 8, [[1, 8]]), RBR.ap(0, 128, 256, [[257, 8]]), col(5), None, ALU.subtract), reads=[RBR, COLB], writes=[COLB])
                P.op("dve", lambda: V.tensor_scalar(col(0), QKG.ap(0, 128, l * 2, [[1, 1]]), 0.125, None, ALU.mult), reads=[QKG], writes=[COLB])
                P.op("dve", lambda: V.tensor_copy(col(1), QKG.ap(0, 128, l * 2 + 1, [[1, 1]])), reads=[QKG], writes=[COLB])
                for i in range(16):
                    bk = prot.next()
                    P.op("pe", (lambda i=i, bk=bk: PE.matmul(bk[:, 0:128], cJ, HK[:, i, :], start=True, stop=True)), reads=[CST, HK], writes=[bk])
                    if i < 8:
                        P.op("dve", (lambda i=i, bk=bk: V.tensor_tensor(BT[:, i, :], bk[:, 0:128], cM4(128, 128), ALU.add)), reads=[bk, CST], writes=[BT])
                    else:
                        P.op("dve", (lambda i=i, bk=bk: V.tensor_copy(BT[:, i, :], bk[:, 0:128])), reads=[bk], writes=[BT])
                P.phase_end()

            with ExitStack() as s2:
                xT = [sb(s2, "xT%d" % i, [128, NCH, 128], F32) for i in range(2)]
                sq = sb(s2, "sq", [128, NCH, 128], BF16)
                hT = sb(s2, "hT", [128, NCH, 128], BF16)
                uT = [sb(s2, "uT%d" % i, [128, 4, 128], BF16) for i in range(2)]
                sgs = [sb(s2, "sgs%d" % i, [128, 4, 128], F32) for i in range(2)]
                sga = [sb(s2, "sga%d" % i, [128, 4, 128], F32) for i in range(2)]
                qT = [sb(s2, "qT%d" % i, [128, 4, 128], BF16) for i in range(2)]
                qsq = sb(s2, "qsq", [128, 2, 128], BF16)
                rstd = sb(s2, "rstd", [128, 128], F32)
                rq = sb(s2, "rq", [128, 2, 128], F32)
                kn32 = sb(s2, "kn32", [128, 4, 128], F32)
                v32 = sb(s2, "v32", [128, 512], F32)
                t1r = Ring([sb(s2, "t1_%d" % i, [128, 2, 128], F32) for i in range(2)])
                t2r = Ring([sb(s2, "t2_%d" % i, [128, 2, 128], F32) for i in range(2)])
                btr = Ring([sb(s2, "bt_%d" % i, [128, 2, 128], F32) for i in range(2)])
                Gr = Ring([sb(s2, "G_%d" % i, [128, 2, 128], F32) for i in range(2)])
                W1r = Ring([sb(s2, "W1_%d" % i, [128, 2, 128], BF16) for i in range(2)])
                W2r = Ring([sb(s2, "W2_%d" % i, [128, 2, 128], BF16) for i in range(2)])
                ypre = sb(s2, "ypre", [128, 4, 128], F32)
                tt = sb(s2, "tt", [128, 4, 128], F32)
                yg = sb(s2, "yg", [128, 4, 128], BF16)
                sg = sb(s2, "sg", [128, 4, 128], F32)
                mixT = sb(s2, "mixT", [128, NCH, 128], BF16)
                mixb = [Buf("mix%d" % i) for i in range(NCH)]
                stmpr = Ring([sb(s2, "stmp%d" % i, [128, 128], F32) for i in range(4)])
                pTr = Ring([sb(s2, "pT%d" % i, [128, 128], BF16) for i in range(6)])
                rsr = Ring([sb(s2, "rs%d" % i, [128, 128], F32) for i in range(2)])
                hring = Ring([banks[5], banks[6], banks[7]])
                wring = Ring(banks[3:5])
                C_GELU = math.sqrt(2.0 / math.pi)

                def v3(t, T, nchunk, c0=0, n=128, p0=0):
                    return t.ap(p0, n, c0 * 128, [[128, nchunk], [1, T]])

                def pv3(bk, T, nchunk=4, c0=0):
                    return bk.ap(0, 128, c0 * 128, [[128, nchunk], [1, T]])

                def rotate(dst_i, src_ap, src_bufs, ci, si, bk=None):
                    bk = bk if bk is not None else prot.next()
                    P.op("pe", lambda: PE.matmul(bk[:, 0:G], cSWN, src_ap, start=True, stop=True), reads=[CST] + src_bufs, writes=[bk])
                    P.op("dve", lambda: V.tensor_tensor(sm(I_T6), sm(ci), src_ap, ALU.mult), reads=smb(ci) + src_bufs, writes=smb(I_T6))
                    P.op("dve", lambda: V.tensor_tensor(sm(I_T7), sm(si), bk[:, 0:G], ALU.mult), reads=smb(si) + [bk], writes=smb(I_T7))
                    P.op("dve", lambda: V.tensor_tensor(sm(dst_i), sm(I_T6), sm(I_T7), ALU.add), reads=smb(I_T6, I_T7), writes=smb(dst_i))

                def load_x(src_ap, xb, T, bi):
                    P.dma("x%d" % bi, lambda: nc.sync.dma_start(out=v3(xb, T, NCH), in_=src_ap), writes=[xb])

                def rsqrt_act(dst_ap, src_ap, rd, wr):
                    P.op("act", lambda: A.activation(out=dst_ap, in_=src_ap, func=AF.Ln, bias=col(7), scale=1.0), reads=rd + [COLB], writes=wr)
                    P.op("act", lambda: A.activation(out=dst_ap, in_=dst_ap, func=AF.Exp, scale=-0.5), reads=wr, writes=wr)

                def head_stream(xb, T, ti, par, outs, gk):
                    slot = gk % 6
                    uTp, qTp, sgsp, sgap = uT[par], qT[par], sgs[par], sga[par]
                    P.op("act", lambda: A.activation(out=v3(sq, T, NCH), in_=v3(xb, T, NCH), func=AF.Square), reads=[xb], writes=[sq])
                    HB = hring.next()
                    for c in range(NCH):
                        P.op("pe", (lambda c=c, HB=HB: PE.matmul(HB[:, 0:T], ONESB[:], sq[:, c, 0:T], start=(c == 0), stop=(c == NCH - 1))), reads=[ONESB, sq], writes=[HB])
                    rsqrt_act(rstd[:, 0:T], HB[:, 0:T], [HB], [rstd])
                    P.op("dve", lambda: V.tensor_tensor(v3(hT, T, NCH), v3(xb, T, NCH), rstd.ap(0, 128, 0, [[0, NCH], [1, T]]), ALU.mult), reads=[xb, rstd], writes=[hT])
                    yield

                    def win_group(HB, cb, nch=4, c0=0):
                        for c in range(nch):
                            for kc in range(NCH):
                                P.op("pe", (lambda c=c, kc=kc, HB=HB: PE.matmul(HB[:, (c0 + c) * 128:(c0 + c) * 128 + T], Win[:, kc, cb + c * 128:cb + (c + 1) * 128], hT[:, kc, 0:T], start=(kc == 0), stop=(kc == NCH - 1))), reads=[Win, hT], writes=[HB])
                            if c % 2 == 1:
                                yield

                    for hf in range(2):
                        HB = hring.next()
                        yield from win_group(HB, 1024 + hf * 256, nch=2)
                        P.op("act", lambda HB=HB: A.activation(out=v3(qsq, T, 2), in_=pv3(HB, T, 2), func=AF.Square), reads=[HB], writes=[qsq])
                        for c in range(2):
                            P.op("pe", (lambda c=c, HB=HB: PE.matmul(HB[:, (2 + c) * 128:(2 + c) * 128 + T], BLKB[:], qsq[:, c, 0:T], start=True, stop=True)), reads=[BLKB, qsq], writes=[HB])
                        rsqrt_act(v3(rq, T, 2), pv3(HB, T, 2, c0=2), [HB], [rq])
                        P.op("dve", (lambda hf=hf, HB=HB: V.scalar_tensor_tensor(v3(qTp, T, 2, c0=2 * hf), pv3(HB, T, 2), col(0), v3(rq, T, 2), ALU.mult, ALU.mult)), reads=[HB, rq, COLB], writes=[qTp])
                        yield
                    for hf in range(2):
                        HB = hring.next()
                        yield from win_group(HB, 1536 + hf * 256, nch=2)
                        P.op("act", lambda HB=HB: A.activation(out=v3(qsq, T, 2), in_=pv3(HB, T, 2), func=AF.Square), reads=[HB], writes=[qsq])
                        for c in range(2):
                            P.op("pe", (lambda c=c, HB=HB: PE.matmul(HB[:, (2 + c) * 128:(2 + c) * 128 + T], BLKB[:], qsq[:, c, 0:T], start=True, stop=True)), reads=[BLKB, qsq], writes=[HB])
                        rsqrt_act(v3(rq, T, 2), pv3(HB, T, 2, c0=2), [HB], [rq])
                        P.op("dve", (lambda hf=hf, HB=HB: V.scalar_tensor_tensor(v3(kn32, T, 2, c0=2 * hf), pv3(HB, T, 2), col(1), v3(rq, T, 2), ALU.mult, ALU.mult)), reads=[HB, rq, COLB], writes=[kn32])
                        yield
                    P.op("pool", lambda: Pl.tensor_copy(kwin.ap(0, 128, slot * 128, [[768, 4], [1, T]]), v3(kn32, T, 4)), reads=[kn32], writes=[kslot[slot]])
                    if outs.get("nk") is not None:
                        P.dma("nk", lambda: nc.sync.dma_start(out=outs["nk"], in_=v3(kn32, T, 4)), reads=[kn32])
                    HB = hring.next()
                    yield from win_group(HB, 0)
                    P.op("act", lambda HB=HB: A.activation(func=AF.Copy, out=v3(uTp, T, 4), in_=pv3(HB, T)), reads=[HB], writes=[uTp])
                    HB = hring.next()
                    for kc in range(NCH):
                        P.op("pe", (lambda kc=kc, HB=HB: PE.matmul(HB[0:T, :], hT[:, kc, 0:T], Win[:, kc, 2048:2560], start=(kc == 0), stop=(kc == NCH - 1))), reads=[Win, hT], writes=[HB])
                    P.op("act", lambda HB=HB: A.activation(func=AF.Copy, out=vwin[0:T, slot, :], in_=HB[0:T, :]), reads=[HB], writes=[vslot[slot]])
                    if outs.get("nv") is not None:
                        P.op("dve", lambda HB=HB: V.tensor_copy(v32[0:T, :], HB[0:T, :]), reads=[HB], writes=[v32])
                        P.dma("nv", lambda: nc.sync.dma_start(out=outs["nv"], in_=v32[0:T, :]), reads=[v32])
                    yield
                    for cb, dstp in ((512, sgsp), (2560, sgap)):
                        HB = hring.next()
                        yield from win_group(HB, cb)
                        P.op("act", (lambda dstp=dstp, HB=HB: A.activation(out=v3(dstp, T, 4), in_=pv3(HB, T), func=AF.Tanh, scale=0.5)), reads=[HB], writes=[dstp])
                        P.op("dve", (lambda dstp=dstp, HB=HB: V.scalar_tensor_tensor(v3(dstp, T, 4), v3(dstp, T, 4), 1.0, pv3(HB, T), ALU.add, ALU.mult)), reads=[HB, dstp], writes=[dstp])
                        yield

                def s5_stream(T, par, outs):
                    uTp, sgsp = uT[par], sgs[par]

                    def p2(t, off=0):
                        return t.ap(0, 128, off, [[128, 2], [1, T]])

                    def front(pi):
                        g0 = 2 * pi
                        c, band = g0 // 8, (g0 % 8) // 2
                        r0 = 32 * band
                        bkp = pqr.next()
                        for q_, Bm in ((0, B1), (1, B2)):
                            for e in range(2):
                                P.op("pe", (lambda q_=q_, Bm=Bm, e=e: PE.matmul(bkp[:, (2 * q_ + e) * 128:(2 * q_ + e) * 128 + T], Bm.ap(r0, 32, (c * 2 + e) * 128, [[1, 128]]), uTp.ap(r0, 32, c * 128, [[1, T]]), start=True, stop=True, tile_position=(r0, 0))), reads=[Bm, uTp], writes=[bkp])
                        return bkp

                    def stageB(pi, bkp):
                        g0 = 2 * pi
                        t1, t2, bt_ = t1r.next(), t2r.next(), btr.next()
                        P.op("dve", lambda: V.tensor_tensor(p2(t1), p2(bkp), p2(PRE1, g0 * 128), ALU.mult), reads=[bkp, PRE1], writes=[t1])
                        P.op("dve", lambda: V.tensor_tensor(p2(t2), p2(bkp, 256), p2(PRE2, g0 * 128), ALU.mult), reads=[bkp, PRE2], writes=[t2])
                        P.op("pool", lambda: Pl.tensor_tensor(p2(bt_), p2(t1), p2(t2), ALU.add), reads=[t1, t2], writes=[bt_])
                        return bt_

                    def stageC(pi, bt_):
                        g0 = 2 * pi
                        Gt, W1, W2 = Gr.next(), W1r.next(), W2r.next()
                        for e in range(2):
                            g = g0 + e
                            P.op("dve", (lambda e=e, g=g: V.tensor_tensor_scan(Gt[:, e, 0:T], SM.ap(0, 128, I_R * G + g, [[0, T]]), bt_[:, e, 0:T], SM.ap(0, 128, I_INIT * G + g, [[1, 1]]), ALU.mult, ALU.add)), reads=[bt_] + smb(I_R, I_INIT), writes=[Gt])
                        P.op("pool", lambda: Pl.tensor_tensor(p2(W1), p2(Gt), p2(POST1, g0 * 128), ALU.mult), reads=[Gt, POST1], writes=[W1])
                        P.op("pool", lambda: Pl.tensor_tensor(p2(W2), p2(Gt), p2(POST2, g0 * 128), ALU.mult), reads=[Gt, POST2], writes=[W2])
                        P.op("act", lambda: A.activation(func=AF.Copy, out=SM.ap(0, 128, I_GLAST * G + g0, [[1, 2]]), in_=Gt.ap(0, 128, T - 1, [[128, 2]])), reads=[Gt], writes=smb(I_GLAST))
                        return W1, W2

                    def back(pi, W1, W2):
                        g0 = 2 * pi
                        c, band = g0 // 8, (g0 % 8) // 2
                        r0 = 32 * band
                        o_ = ybank.ap(r0, 32, c * 128, [[1, T]])
                        seq = [(L1, W1, 0), (L2, W2, 0), (L1, W1, 1), (L2, W2, 1)]
                        for n_, (Lm, Wm, e) in enumerate(seq):
                            P.op("pe", (lambda n_=n_, Lm=Lm, Wm=Wm, e=e: PE.matmul(o_, Lm[:, g0 + e, :], Wm[:, e, 0:T], start=(n_ == 0), stop=(n_ == 3), tile_position=(0, r0))), reads=[Lm, Wm], writes=[ybank])

                    NPAIR = G // 2
                    sA, sB, sC = {}, {}, {}
                    for i in range(NPAIR + 3):
                        if i < NPAIR:
                            sA[i] = front(i)
                        if 0 <= i - 1 < NPAIR:
                            sB[i - 1] = stageB(i - 1, sA.pop(i - 1))
                        if 0 <= i - 2 < NPAIR:
                            sC[i - 2] = stageC(i - 2, sB.pop(i - 2))
                        if 0 <= i - 3 < NPAIR:
                            back(i - 3, *sC.pop(i - 3))
                        yield

                def tail_stream(xb, bi, T, par, dst_ap, outs):
                    uTp, sgsp = uT[par], sgs[par]
                    for c in range(4):
                        P.op("dve", (lambda c=c: V.scalar_tensor_tensor(ypre[:, c, 0:T], uTp[:, c, 0:T], DCOL.ap(0, 128, l * 4 + c, [[1, 1]]), ybank[:, c * 128:c * 128 + T], ALU.mult, ALU.add)), reads=[uTp, DCOL, ybank], writes=[ypre])
                    P.op("act", lambda: A.activation(out=v3(tt, T, 4), in_=v3(ypre, T, 4), func=AF.Square), reads=[ypre], writes=[tt])
                    P.op("dve", lambda: V.tensor_scalar(v3(tt, T, 4), v3(tt, T, 4), 0.044715, 1.0, ALU.mult, ALU.add), reads=[tt], writes=[tt])
                    P.op("dve", lambda: V.tensor_tensor(v3(tt, T, 4), v3(tt, T, 4), v3(ypre, T, 4), ALU.mult), reads=[tt, ypre], writes=[tt])
                    P.op("act", lambda: A.activation(out=v3(tt, T, 4), in_=v3(tt, T, 4), func=AF.Tanh, scale=C_GELU), reads=[tt], writes=[tt])
                    P.op("dve", lambda: V.scalar_tensor_tensor(v3(yg, T, 4), v3(tt, T, 4), 1.0, v3(ypre, T, 4), ALU.add, ALU.mult), reads=[tt, ypre], writes=[yg])
                    yield
                    if outs.get("st") is not None:
                        ci, si = outs["st_cs"]
                        rotate(I_HL, sm(I_GLAST), smb(I_GLAST), ci, si, GEN)
                        P.dma("st", lambda: nc.sync.dma_start(out=outs["st"], in_=sm(I_HL)), reads=smb(I_HL))
                    else:
                        rotate(I_INIT, sm(I_GLAST), smb(I_GLAST), I_CA, I_SA, GEN)
                    bva, bga = GEN2, ybank
                    for oc in (4, 5, 6, 7, 0, 1, 2, 3):
                        bk_ = bva if oc < 4 else bga
                        for kc in range(4):
                            P.op("pe", (lambda oc=oc, kc=kc, bk_=bk_: PE.matmul(bk_[:, (oc % 4) * 128:(oc % 4) * 128 + T], Wglu[:, kc, oc * 128:(oc + 1) * 128], yg[:, kc, 0:T], start=(kc == 0), stop=(kc == 3))), reads=[Wglu, yg], writes=[bk_])
                        if oc % 4 == 3:
                            yield
                    P.op("act", lambda: A.activation(out=v3(sg, T, 4), in_=pv3(bga, T), func=AF.Tanh, scale=0.5), reads=[bga], writes=[sg])
                    P.op("dve", lambda: V.scalar_tensor_tensor(v3(sg, T, 4), v3(sg, T, 4), 1.0, v3(sgsp, T, 4), ALU.add, ALU.mult), reads=[sg, sgsp], writes=[sg])
                    P.op("dve", lambda: V.tensor_tensor(v3(mixT, T, 4), pv3(bva, T), v3(sg, T, 4), ALU.mult), reads=[bva, sg], writes=mixb[0:4])
                    yield
                    bwa, bwb = wring.next(), wring.next()
                    for oc in range(NCH):
                        bk_ = bwa if oc < 4 else bwb
                        for kc in range(NCH):
                            P.op("pe", (lambda oc=oc, kc=kc, bk_=bk_: PE.matmul(bk_[:, (oc % 4) * 128:(oc % 4) * 128 + T], Wout[:, kc, oc * 128:(oc + 1) * 128], mixT[:, kc, 0:T], start=(kc == 0), stop=(kc == NCH - 1))), reads=[Wout, mixb[kc]], writes=[bk_])
                        if oc % 2 == 1:
                            yield
                    P.op("dve", lambda: V.tensor_tensor(v3(xb, T, 4), v3(xb, T, 4), pv3(bwa, T), ALU.add), reads=[xb, bwa], writes=[xb])
                    P.op("dve", lambda: V.tensor_tensor(v3(xb, T, 4, c0=4), v3(xb, T, 4, c0=4), pv3(bwb, T), ALU.add), reads=[xb, bwb], writes=[xb])
                    P.dma("y%d" % bi, lambda: nc.sync.dma_start(out=dst_ap, in_=v3(xb, T, NCH)), reads=[xb])

                def attn_stream(T, ti, par, gk):
                    qTp, sgap = qT[par], sga[par]
                    jl = [j for j in range(5) if ti - 4 + j >= 0]
                    items = [(h, j) for h in range(8) for j in jl]

                    def qk(h, j):
                        hp, hh = h // 2, h % 2
                        sl = (gk - 4 + j) % 6
                        nk = T if j == 4 else 128
                        bk_ = scr.next()
                        P.op("pe", lambda: PE.matmul(bk_[0:nk, 0:T], kwin.ap(64 * hh, 64, (hp * 6 + sl) * 128, [[1, nk]]), qTp.ap(64 * hh, 64, hp * 128, [[1, T]]), start=True, stop=True), reads=[kslot[sl], qTp], writes=[bk_])
                        return bk_, nk, sl

                    def soft(h, j, bk_, nk):
                        pT = pTr.next()
                        if j in (1, 2):
                            P.op("act", lambda: A.activation(out=pT[0:nk, 0:T], in_=bk_[0:nk, 0:T], func=AF.Exp, bias=col(8 + h, nk), scale=1.0), reads=[bk_, COLB], writes=[pT])
                        else:
                            stmp = stmpr.next()
                            if j == 0:
                                badd, bias_, rd = cM0(nk, T), col(8 + h, nk), [CST]
                            elif j == 3:
                                badd, bias_, rd = BT[0:nk, 8 + h, 0:T], col(6, nk), [BT]
                            else:
                                badd, bias_, rd = BT[0:nk, h, 0:T], col(6, nk), [BT]
                            P.op("dve", lambda: V.tensor_tensor(stmp[0:nk, 0:T], bk_[0:nk, 0:T], badd, ALU.add), reads=[bk_] + rd, writes=[stmp])
                            P.op("act", lambda: A.activation(out=pT[0:nk, 0:T], in_=stmp[0:nk, 0:T], func=AF.Exp, bias=bias_, scale=1.0), reads=[stmp, COLB], writes=[pT])
                        return pT

                    def pvmm_pair(hp, j, pts, cur):
                        first, last = (j == jl[0]), (j == jl[-1])
                        osb = OS[hp % 2]
                        for hh in range(2):
                            h = 2 * hp + hh
                            pT, (bk_, nk, sl) = pts[hh], cur[hh]
                            o_ = osb.ap(64 * hh, 64, 0, [[1, T]])
                            P.op("pe", (lambda h=h, hh=hh, pT=pT, nk=nk, sl=sl, o_=o_: PE.matmul(o_, vwin[0:nk, sl, h * 64:(h + 1) * 64], pT[0:nk, 0:T], start=first, stop=last, tile_position=(0, 64 * hh), skip_group_check=True)), reads=[vslot[sl], pT], writes=[osb])
                        for hh in range(2):
                            pT, (bk_, nk, sl) = pts[hh], cur[hh]
                            s_ = osb.ap(64 * hh, 64, 128, [[1, T]])
                            P.op("pe", (lambda hh=hh, pT=pT, nk=nk, s_=s_: PE.matmul(s_, ONE64B[0:nk, :], pT[0:nk, 0:T], start=False, stop=last, tile_position=(0, 64 * hh), skip_group_check=True)), reads=[ONE64B, pT], writes=[osb])
                        if last:
                            rs = rsr.next()
                            P.op("dve", lambda: V.reciprocal(rs[:, 0:T], osb[:, 128:128 + T]), reads=[osb], writes=[rs])
                            P.op("pool", lambda: Pl.tensor_tensor(rs[:, 0:T], rs[:, 0:T], sgap[:, hp, 0:T], ALU.mult), reads=[rs, sgap], writes=[rs])
                            P.op("dve", lambda: V.tensor_tensor(mixT[:, 4 + hp, 0:T], osb[:, 0:T], rs[:, 0:T], ALU.mult), reads=[osb, rs], writes=[mixb[4 + hp]])

                    pitems = [(hp, j) for hp in range(4) for j in jl]

                    def qk_pair(hp, j):
                        return [qk(2 * hp + hh, j) for hh in range(2)]

                    q = [qk_pair(*pitems[0])]
                    for idx in range(len(pitems)):
                        if idx + 1 < len(pitems):
                            q.append(qk_pair(*pitems[idx + 1]))
                        hp, j = pitems[idx]
                        cur = q.pop(0)
                        pts = [soft(2 * hp + hh, j, cur[hh][0], cur[hh][1]) for hh in range(2)]
                        pvmm_pair(hp, j, pts, cur)
                        yield

                def run_streams(streams):
                    streams = list(streams)
                    while streams:
                        for s_ in list(streams):
                            try:
                                next(s_)
                            except StopIteration:
                                streams.remove(s_)

                def body(xb, bi, T, ti, par, dst_ap, outs, nxt=None, gk=4):
                    run_streams([s5_stream(T, par, outs), attn_stream(T, ti, par, gk)])
                    streams = [tail_stream(xb, bi, T, par, dst_ap, outs)]
                    if nxt is not None:
                        streams.insert(0, nxt)
                    run_streams(streams)

                P.op("dve", lambda: V.memset(col(7), EPS), writes=[COLB])
                srcp, dstp = (xp, y1p) if l == 0 else (y1p, yp)
                srcs, dsts = (xs, y1s) if l == 0 else (y1s, ys)

                def p_outs(s, i):
                    outs = {}
                    if i >= NT - 4:
                        outs["nk"] = nkp[l, s, i - (NT - 4)]
                        outs["nv"] = nvp[l, s, (i - (NT - 4)) * 128:(i - (NT - 4) + 1) * 128, :]
                    if i == NT - 1:
                        outs["st"] = stp[l, s]
                        outs["st_cs"] = (I_CLP, I_SLP)
                    return outs

                tiles = [(s, i) for s in range(NSP) for i in range(NT)]
                load_x(srcp[0, 0], xT[0], 128, 0)
                run_streams([head_stream(xT[0], 128, 0, 0, p_outs(0, 0), 0)])
                for k, (s, i) in enumerate(tiles):
                    bi = k % 2
                    if i == 0:
                        P.op("dve", lambda: V.memset(sm(I_INIT), 0.0), writes=smb(I_INIT))
                    nxt = None
                    if k + 1 < len(tiles):
                        s2_, i2_ = tiles[k + 1]
                        load_x(srcp[s2_, i2_], xT[1 - bi], 128, 1 - bi)
                        nxt = head_stream(xT[1 - bi], 128, i2_, 1 - bi, p_outs(s2_, i2_), k + 1)
                    body(xT[bi], bi, 128, i, bi, dstp[s, i], p_outs(s, i), nxt, k)
                for s in range(NSS):
                    bi = s % 2
                    cstg = xT[1 - bi]
                    load_x(srcs[s], xT[bi], TS, bi)
                    for hf in range(2):
                        P.dma("cstg", (lambda s=s, hf=hf, cstg=cstg: nc.sync.dma_start(out=cstg.ap(0, 128, 0, [[512, 2], [1, 512]]), in_=ck[l, s, :, 2 * hf:2 * hf + 2, :])), writes=[cstg])
                        P.op("pool", (lambda hf=hf, cstg=cstg: Pl.tensor_copy(kwin.ap(0, 128, 2 * hf * 768, [[768, 2], [128, 4], [1, 128]]), cstg.ap(0, 128, 0, [[512, 2], [128, 4], [1, 128]]))), reads=[cstg], writes=kslot[0:4])
                    for hf in range(2):
                        P.dma("cstg", (lambda s=s, hf=hf, cstg=cstg: nc.sync.dma_start(out=cstg.ap(0, 128, 0, [[512, 2], [1, 512]]), in_=cv[l, s, hf * 256:(hf + 1) * 256, :].rearrange("(j k) e -> k j e", k=128))), writes=[cstg])
                        P.op("act", (lambda hf=hf, cstg=cstg: A.activation(func=AF.Copy, out=vwin[:, 2 * hf:2 * hf + 2, :], in_=cstg.ap(0, 128, 0, [[512, 2], [1, 512]]))), reads=[cstg], writes=vslot[0:4])
                    P.dma("st0", (lambda s=s: nc.sync.dma_start(out=sm(I_ST0), in_=st0[l, s])), writes=smb(I_ST0))
                    rotate(I_INIT, sm(I_ST0), smb(I_ST0), I_C1, I_S1)
                    outs = {"nk": nks[l, s], "nv": nvs[l, s], "st": sts[l, s], "st_cs": (I_CLS, I_SLS)}
                    run_streams([head_stream(xT[bi], TS, 4, bi, outs, 4)])
                    body(xT[bi], bi, TS, 4, bi, dsts[s], outs, None, 4)
                P.phase_end()
        print("[kernel] instructions:", P.n_inst, "semaphores:", P.nsem)
    return nc


def _consts():
    c = np.zeros((128, NCST), np.float32)
    idx = np.arange(128)
    c[idx, C_J + 127 - idx] = 1.0
    p = np.arange(64)
    c[64 + p, C_SWN + p] = -1.0
    c[p, C_SWN + 64 + p] = 1.0
    c[:, C_ONES:C_ONES + 128] = 1.0 / 1024.0
    c[0:64, C_BLK:C_BLK + 64] = 1.0 / 64.0
    c[64:128, C_BLK + 64:C_BLK + 128] = 1.0 / 64.0
    c[:, C_ONE64:C_ONE64 + 64] = 1.0
    c[64:128, C_M4:C_M4 + 64] = NEG
    c[0:64, C_M0 + 64:C_M0 + 128] = NEG
    c[:, C_JROW:C_JROW + 128] = np.arange(128, dtype=np.float32)[None]
    c[0:64, C_SH] = 1.0
    c[64:128, C_SH] = -1.0
    return c


_PROG_CACHE = {}


def kernel(x_prompt, x_sample, cache_k, cache_v, state_ssm_re, state_ssm_im, norm_gain, w_in,
           ssm_a_re, ssm_a_im, ssm_b_re, ssm_b_im, ssm_c_re, ssm_c_im, ssm_d, ssm_log_dt,
           w_glu, q_norm_gain, k_norm_gain, rel_bias, w_out, n_cores=8):
    f = np.float32
    x_prompt = np.asarray(x_prompt, f)
    x_sample = np.asarray(x_sample, f)
    B, SEQ, _ = x_prompt.shape
    BS = x_sample.shape[0]
    NSP, NSS = B // n_cores, BS // n_cores
    NT = SEQ // 128
    R = cache_k.shape[2]
    assert R == 512 and SEQ >= 512 and SEQ % 128 == 0 and x_sample.shape[1] == TS

    xpb = x_prompt.reshape(B, NT, 128, NCH, 128).transpose(0, 1, 4, 3, 2)
    xsb = x_sample.reshape(BS, TS, NCH, 128).transpose(0, 3, 2, 1)
    ckb = np.asarray(cache_k, f).reshape(L, BS, R, 4, 2, 64).transpose(0, 1, 4, 5, 3, 2).reshape(L, BS, 128, 4, R)
    cvb = np.asarray(cache_v, f).reshape(L, BS, R, 512)
    st = np.concatenate([np.asarray(state_ssm_re, f).transpose(0, 1, 3, 2), np.asarray(state_ssm_im, f).transpose(0, 1, 3, 2)], axis=2)
    ngh = np.asarray(norm_gain, f).reshape(L, NCH, 128).transpose(2, 0, 1)
    are = np.asarray(ssm_a_re, f).transpose(2, 0, 1)
    aim = np.asarray(ssm_a_im, f).transpose(2, 0, 1)
    ldt = np.broadcast_to(np.asarray(ssm_log_dt, f)[None], (64, L, G))
    s5 = np.stack([are, aim, ldt], axis=2)
    s5 = np.concatenate([s5, s5], axis=0)
    bre = np.asarray(ssm_b_re, f)
    bim = np.asarray(ssm_b_im, f)
    b1 = np.zeros((L, 128, 8, 128), f)
    b2 = np.zeros((L, 128, 8, 128), f)
    cre = np.asarray(ssm_c_re, f)
    cim = np.asarray(ssm_c_im, f)
    l1 = np.zeros((L, 128, G, 32), f)
    l2 = np.zeros((L, 128, G, 32), f)
    for g in range(G):
        c, band, e = g // 8, (g % 8) // 2, g % 2
        r0 = 32 * band + 16 * e
        b1[:, r0:r0 + 16, c * 2 + e, 0:64] = bre[:, g].transpose(0, 2, 1)
        b1[:, r0:r0 + 16, c * 2 + e, 64:128] = bim[:, g].transpose(0, 2, 1)
        b2[:, r0:r0 + 16, c * 2 + e, 0:64] = bim[:, g].transpose(0, 2, 1)
        b2[:, r0:r0 + 16, c * 2 + e, 64:128] = bre[:, g].transpose(0, 2, 1)
        l1[:, 0:64, g, 16 * e:16 * e + 16] = cre[:, g].transpose(0, 2, 1)
        l1[:, 64:128, g, 16 * e:16 * e + 16] = cim[:, g].transpose(0, 2, 1)
        l2[:, 0:64, g, 16 * e:16 * e + 16] = cim[:, g].transpose(0, 2, 1)
        l2[:, 64:128, g, 16 * e:16 * e + 16] = cre[:, g].transpose(0, 2, 1)
    dch = np.asarray(ssm_d, f).reshape(L, 4, 128).transpose(2, 0, 1)
    qg = np.asarray(q_norm_gain, f)
    kg = np.asarray(k_norm_gain, f)
    qkgh = np.stack([np.concatenate([qg, qg], 1), np.concatenate([kg, kg], 1)], axis=2).transpose(1, 0, 2)
    qkr = np.broadcast_to(np.concatenate([qg, kg], 1)[None], (128, L, 128))
    rbh = np.asarray(rel_bias, f)
    rbr = np.broadcast_to(rbh.reshape(1, L, 8 * 257), (128, L, 8 * 257))
    cst = _consts()

    key = (NT, NSP, NSS)
    nc = build_program(NT, NSP, NSS)
    shared = dict(w_in=np.ascontiguousarray(w_in, f), w_glu=np.ascontiguousarray(w_glu, f), w_out=np.ascontiguousarray(w_out, f),
                  ng=np.ascontiguousarray(ngh), s5a=np.ascontiguousarray(s5), b1h=b1.reshape(L, 128, 1024), b2h=b2.reshape(L, 128, 1024),
                  l1h=l1.reshape(L, 128, 1024), l2h=l2.reshape(L, 128, 1024), dcol=np.ascontiguousarray(dch),
                  qkg=np.ascontiguousarray(qkgh), qkrow=np.ascontiguousarray(qkr), rb=np.ascontiguousarray(rbh),
                  rbrep=np.ascontiguousarray(rbr), cst=cst)
    in_maps = []
    for c in range(n_cores):
        m = dict(shared)
        m["xp"] = np.ascontiguousarray(xpb[c * NSP:(c + 1) * NSP])
        m["xs"] = np.ascontiguousarray(xsb[c * NSS:(c + 1) * NSS])
        m["ck"] = np.ascontiguousarray(ckb[:, c * NSS:(c + 1) * NSS])
        m["cv"] = np.ascontiguousarray(cvb[:, c * NSS:(c + 1) * NSS])
        m["st0"] = np.ascontiguousarray(st[:, c * NSS:(c + 1) * NSS])
        in_maps.append(m)
    res = run_bass_kernel_spmd(nc, in_maps, core_ids=list(range(n_cores)))
    rs = res.results

    def cat(name, axis):
        return np.concatenate([np.asarray(r[name]) for r in rs], axis=axis)

    ypo = cat("yp", 0).transpose(0, 1, 4, 3, 2).reshape(B, SEQ, D)
    yso = cat("ys", 0).transpose(0, 3, 2, 1).reshape(BS, TS, D)
    nkpo = cat("nkp", 1)
    nkpo = nkpo.reshape(L, B, 4, 2, 64, 4, 128).transpose(0, 1, 2, 6, 5, 3, 4).reshape(L, B, 512, 8, 64)
    nvpo = cat("nvp", 1).reshape(L, B, 512, 8, 64)
    stpo = cat("stp", 1)
    pr = stpo[:, :, 0:64].transpose(0, 1, 3, 2)
    pi = stpo[:, :, 64:128].transpose(0, 1, 3, 2)
    nkso = cat("nks", 1).reshape(L, BS, 2, 64, 4, TS).transpose(0, 1, 5, 4, 2, 3).reshape(L, BS, TS, 8, 64)
    nvso = cat("nvs", 1).reshape(L, BS, TS, 8, 64)
    stso = cat("sts", 1)
    sr = stso[:, :, 0:64].transpose(0, 1, 3, 2)
    si = stso[:, :, 64:128].transpose(0, 1, 3, 2)
    c_ = lambda a: np.ascontiguousarray(a, dtype=np.float32)
    return (c_(ypo), c_(yso), c_(nkpo), c_(nvpo), c_(pr), c_(pi), c_(nkso), c_(nvso), c_(sr), c_(si))
```

```python
import math
from contextlib import ExitStack
import numpy as np
import concourse.bass as bass
import concourse.mybir as mybir
from concourse.bass_utils import run_bass_kernel_spmd

F32 = mybir.dt.float32
BF16 = mybir.dt.bfloat16
I32 = mybir.dt.int32
ALU = mybir.AluOpType
AF = mybir.ActivationFunctionType
AX = mybir.AxisListType

L = 2
D = 1024
NCH = 8
DIN = 3072
G = 32
TS = 16
EPS = 1e-6
NEG = -30000.0
TWO_PI = 2.0 * math.pi
STRICT = True

C_J, C_SWN, C_ONES, C_BLK, C_ONE64, C_M4, C_M0, C_JROW, C_SH = 0, 128, 256, 384, 512, 576, 704, 832, 960
NCST = 961


class Buf:
    def __init__(self, name):
        self.name = name
        self.w = None
        self.r = {}
        self.excl = False


class TT:
    def __init__(self, h, shape, dt, name, nbuf=None):
        self.h = h
        self.shape = shape
        self.row = int(np.prod(shape[1:]))
        self.dt = dt
        self.b = Buf(name)

    def __getitem__(self, idx):
        return self.h[idx]

    def ap(self, p0, npart, off, dims):
        return bass.AP(self.h, p0 * self.row + off, [[self.row, npart]] + [list(d) for d in dims])


class Op:
    __slots__ = ("eng", "fn", "deps", "needs_inc", "sem", "target", "is_dma", "epoch", "gen", "vc")


class Chan:
    def __init__(self, sem):
        self.sem = sem
        self.count = 0


class Prog:
    ENG = ("pe", "act", "dve", "pool", "sp")

    def __init__(self, nc, stack):
        self.nc = nc
        self.stack = stack
        self.e = {"pe": nc.tensor, "act": nc.scalar, "dve": nc.vector, "pool": nc.gpsimd, "sp": nc.sync}
        self.ops = []
        self.epoch = 0
        self.gen = 0
        self.sems = {}
        self.rank = {}
        self.seen = {}
        self.known = {}
        self.last = {}
        self.chans = {}
        self.nsem = 0
        self.n_inst = 0

    def _sem(self, name):
        self.nsem += 1
        return self.stack.enter_context(self.nc.semaphore(name))

    def chan(self, name):
        if name not in self.chans:
            self.chans[name] = Chan(self._sem("c_" + name))
        return self.chans[name]

    def _deps(self, eng, reads, writes):
        deps = []
        for b in reads:
            if b.w is not None:
                deps.append(b.w)
            if b.excl:
                for en_, o_ in b.r.items():
                    if en_ != eng:
                        deps.append(o_)
        for b in writes:
            if b.w is not None:
                deps.append(b.w)
            deps.extend(b.r.values())
        out = []
        for d in deps:
            if d.gen != self.gen:
                continue
            if (not d.is_dma) and d.eng == eng and (eng == "pe" or not STRICT):
                continue
            d.needs_inc = True
            out.append(d)
        return out

    def op(self, eng, fn, reads=(), writes=()):
        reads = [x.b if hasattr(x, 'b') else x for x in reads]
        writes = [x.b if hasattr(x, 'b') else x for x in writes]
        o = Op()
        o.eng = eng
        o.fn = fn
        o.is_dma = False
        o.needs_inc = False
        o.sem = None
        o.target = 0
        o.epoch = self.epoch
        o.gen = self.gen
        o.deps = self._deps(eng, reads, writes)
        for b in writes:
            b.w = o
            b.r = {}
        for b in reads:
            b.r[eng] = o
        self.ops.append(o)
        return o

    def dma(self, chan, fn, reads=(), writes=()):
        reads = [x.b if hasattr(x, 'b') else x for x in reads]
        writes = [x.b if hasattr(x, 'b') else x for x in writes]
        ch = self.chan(chan)
        o = Op()
        o.eng = "sp"
        o.fn = fn
        o.is_dma = True
        o.needs_inc = True
        ch.count += 1
        o.sem = ch.sem
        o.target = 16 * ch.count
        o.epoch = self.epoch
        o.gen = self.gen
        o.deps = self._deps("sp", reads, writes)
        for b in writes:
            b.w = o
            b.r = {}
        for b in reads:
            b.r["dma_" + chan] = o
        self.ops.append(o)
        return o

    def emit(self):
        known = self.known
        for o in self.ops:
            e = self.e[o.eng]
            kn = known.setdefault(o.eng, {})
            need = {}
            for d in o.deps:
                k = id(d.sem)
                if k not in need or need[k][1] < d.target:
                    need[k] = (d.sem, d.target, d)
            for k, (sem, tgt, d) in need.items():
                if kn.get(k, 0) >= tgt:
                    continue
                e.wait_ge(sem, tgt)
                self.n_inst += 1
                kn[k] = tgt
                for k2, v2 in d.vc.items():
                    if kn.get(k2, 0) < v2:
                        kn[k2] = v2
            inst = o.fn()
            self.n_inst += 1
            if o.is_dma:
                inst.then_inc(o.sem, 16)
                o.vc = dict(kn)
            else:
                self.last[o.eng] = o
                if o.needs_inc:
                    key = (o.eng, o.epoch)
                    if key not in self.sems:
                        self.sems[key] = self._sem("e_%s_%d" % key)
                        self.rank[key] = 0
                    self.rank[key] += 1
                    o.sem = self.sems[key]
                    o.target = self.rank[key]
                    inst.then_inc(o.sem, 1)
                    o.vc = dict(kn)
        self.ops = []

    def phase_end(self):
        lastops = {}
        for o in self.ops:
            if not o.is_dma:
                lastops[o.eng] = o
        for o in lastops.values():
            o.needs_inc = True
        self.emit()
        for en in self.ENG:
            e = self.e[en]
            kn = self.known.setdefault(en, {})
            for fn_, o in self.last.items():
                if fn_ == en or o.sem is None:
                    continue
                if kn.get(id(o.sem), 0) >= o.target:
                    continue
                kn[id(o.sem)] = o.target
                e.wait_ge(o.sem, o.target)
            for ch in self.chans.values():
                if ch.count == 0:
                    continue
                if kn.get(id(ch.sem), 0) >= 16 * ch.count:
                    continue
                kn[id(ch.sem)] = 16 * ch.count
                e.wait_ge(ch.sem, 16 * ch.count)
        self.gen += 1
        self.epoch += 1


class PS:
    def __init__(self, bank, c0, name):
        self.bank = bank
        self.c0 = c0
        self.b = Buf(name)

    def ap(self, p0, n, off, dims):
        return self.bank.ap(p0, n, self.c0 + off, dims)

    def v(self, n, T, off=0, p0=0):
        return self.bank.ap(p0, n, self.c0 + off, [[1, T]])


class Ring:
    def __init__(self, items):
        self.items = items
        self.i = 0

    def next(self):
        x = self.items[self.i % len(self.items)]
        self.i += 1
        return x


def build_program(NT, NSP, NSS):
    nc = bass.Bass("TRN2", target_bir_lowering=False)

    def din(name, shape):
        return nc.dram_tensor(name, list(shape), F32, kind="ExternalInput").ap()

    def dout(name, shape):
        return nc.dram_tensor(name, list(shape), F32, kind="ExternalOutput").ap()

    xp = din("xp", [NSP, NT, 128, NCH, 128])
    xs = din("xs", [NSS, 128, NCH, TS])
    ck = din("ck", [L, NSS, 128, 4, 512])
    cv = din("cv", [L, NSS, 512, 512])
    st0 = din("st0", [L, NSS, 128, G])
    w_in = din("w_in", [L, D, DIN])
    w_glu = din("w_glu", [L, 512, 1024])
    w_out = din("w_out", [L, D, D])
    ng = din("ng", [128, L, NCH])
    s5a = din("s5a", [128, L, 3, G])
    b1h = din("b1h", [L, 128, 1024])
    b2h = din("b2h", [L, 128, 1024])
    l1h = din("l1h", [L, 128, 1024])
    l2h = din("l2h", [L, 128, 1024])
    dcol = din("dcol", [128, L, 4])
    qkg = din("qkg", [128, L, 2])
    qkrow = din("qkrow", [128, L, 128])
    rb = din("rb", [L, 8, 257])
    rbrep = din("rbrep", [128, L, 8 * 257])
    cst = din("cst", [128, NCST])

    yp = dout("yp", [NSP, NT, 128, NCH, 128])
    ys = dout("ys", [NSS, 128, NCH, TS])
    nkp = dout("nkp", [L, NSP, 4, 128, 4, 128])
    nvp = dout("nvp", [L, NSP, 512, 512])
    stp = dout("stp", [L, NSP, 128, G])
    nks = dout("nks", [L, NSS, 128, 4, TS])
    nvs = dout("nvs", [L, NSS, TS, 512])
    sts = dout("sts", [L, NSS, 128, G])
    y1p = nc.dram_tensor("y1p", [NSP, NT, 128, NCH, 128], F32, kind="Internal").ap()
    y1s = nc.dram_tensor("y1s", [NSS, 128, NCH, TS], F32, kind="Internal").ap()
    ext = nc.dram_tensor("ext", [L, 8, 384], F32, kind="Internal").ap()

    with ExitStack() as glob:
        P = Prog(nc, glob)

        uniq = [0]

        def sb(stack, name, shape, dt=F32):
            uniq[0] += 1
            name = "%s_%d" % (name, uniq[0])
            h = stack.enter_context(nc.sbuf_tensor(name, list(shape), dt))
            return TT(h, list(shape), dt, name)

        def ps(stack, name):
            h = stack.enter_context(nc.psum_tensor(name, [128, 512], F32))
            t_ = TT(h, [128, 512], F32, name)
            t_.b.excl = True
            return t_

        Win = sb(glob, "Win", [128, NCH, DIN], BF16)
        Wglu = sb(glob, "Wglu", [128, 4, 1024], BF16)
        Wout = sb(glob, "Wout", [128, NCH, D], BF16)
        PRE1 = sb(glob, "PRE1", [128, G, 128], BF16)
        PRE2 = sb(glob, "PRE2", [128, G, 128], BF16)
        POST1 = sb(glob, "POST1", [128, G, 128], BF16)
        POST2 = sb(glob, "POST2", [128, G, 128], BF16)
        B1 = sb(glob, "B1", [128, 8, 128], BF16)
        B2 = sb(glob, "B2", [128, 8, 128], BF16)
        L1 = sb(glob, "L1", [128, G, 32], BF16)
        L2 = sb(glob, "L2", [128, G, 32], BF16)
        CST = sb(glob, "CST", [128, NCST], F32)
        ONESB = sb(glob, "ONESB", [128, 128], BF16)
        BLKB = sb(glob, "BLKB", [128, 128], BF16)
        ONE64B = sb(glob, "ONE64B", [128, 64], BF16)
        BT = sb(glob, "BT", [128, 16, 128], F32)
        kwin = sb(glob, "kwin", [128, 4, 6, 128], BF16)
        vwin = sb(glob, "vwin", [128, 6, 512], BF16)
        kslot = [Buf("ks%d" % i) for i in range(6)]
        vslot = [Buf("vs%d" % i) for i in range(6)]
        SM = sb(glob, "SM", [128, 40, G], F32)
        SMB = [Buf("sm%d" % i) for i in range(40)]
        COL = sb(glob, "COL", [128, 64], F32)
        COLB = Buf("col")
        NGs = sb(glob, "NGs", [128, L, NCH], F32)
        S5A = sb(glob, "S5A", [128, L, 3, G], F32)
        DCOL = sb(glob, "DCOL", [128, L, 4], F32)
        QKG = sb(glob, "QKG", [128, L, 2], F32)
        banks = [ps(glob, "pb%d" % i) for i in range(8)]
        ybank = banks[0]
        pqr = Ring(banks[1:3])
        scr = Ring(banks[3:7])
        OS = [banks[7], banks[7]]
        GEN = banks[1]
        GEN2 = banks[2]
        prot = Ring(banks[1:8])

        (I_ARE, I_AIM, I_LDT, I_DT, I_R, I_TH, I_S1, I_C1, I_ABRE, I_ABIM, I_NR, I_DEN, I_FRE, I_FIM,
         I_CA, I_SA, I_CLP, I_SLP, I_CLS, I_SLS, I_INIT, I_GLAST, I_T0, I_T1, I_T2, I_T3, I_T4, I_T5, I_T6,
         I_T7, I_HL, I_ST0, I_NFIM) = range(33)

        def sm(i, n=128):
            return SM.ap(0, n, i * G, [[1, G]])

        def col(i, n=128, p0=0):
            return COL.ap(p0, n, i, [[1, 1]])

        cJ = CST.ap(0, 128, C_J, [[1, 128]])
        cSWN = CST.ap(0, 128, C_SWN, [[1, 128]])
        cM4 = lambda nk, T: CST.ap(0, nk, C_M4, [[1, T]])
        cM0 = lambda nk, T: CST.ap(0, nk, C_M0, [[1, T]])
        cSH = CST.ap(0, 128, C_SH, [[1, 1]])

        V, A, Pl, PE = nc.vector, nc.scalar, nc.gpsimd, nc.tensor

        P.dma("cst", lambda: nc.sync.dma_start(out=CST[:], in_=cst[:]), writes=[CST])
        P.dma("cst2", lambda: nc.sync.dma_start(out=NGs[:], in_=ng[:]), writes=[NGs])
        P.dma("cst3", lambda: nc.sync.dma_start(out=S5A[:], in_=s5a[:]), writes=[S5A])
        P.dma("cst4", lambda: nc.sync.dma_start(out=DCOL[:], in_=dcol[:]), writes=[DCOL])
        P.dma("cst5", lambda: nc.sync.dma_start(out=QKG[:], in_=qkg[:]), writes=[QKG])
        P.op("dve", lambda: V.tensor_copy(ONESB[:], CST.ap(0, 128, C_ONES, [[1, 128]])), reads=[CST], writes=[ONESB])
        P.op("dve", lambda: V.tensor_copy(BLKB[:], CST.ap(0, 128, C_BLK, [[1, 128]])), reads=[CST], writes=[BLKB])
        P.op("dve", lambda: V.tensor_copy(ONE64B[:], CST.ap(0, 128, C_ONE64, [[1, 64]])), reads=[CST], writes=[ONE64B])
        P.phase_end()

        for l in range(L):
            with ExitStack() as s1:
                stg = [sb(s1, "stg%d" % i, [128, 1536], F32) for i in range(2)]
                tmpN = 256
                tb = [sb(s1, "tb%d" % i, [128, tmpN], F32) for i in range(8)]
                tbi = sb(s1, "tbi", [128, tmpN], I32)
                HK = sb(s1, "HK", [128, 16, 128], F32)
                RBR = sb(s1, "RBR", [128, 8 * 257], F32)
                QKR = sb(s1, "QKR", [128, 128], F32)
                EXS = sb(s1, "EXS", [8, 384], F32)

                def sa(i):
                    return S5A.ap(0, 128, (l * 3 + i) * G, [[1, G]])

                def sincos(ang, n, out_sin, out_cos, rd, wr):
                    t, kf, d, m = (tb[4].ap(0, 128, 0, [[1, n]]), tb[5].ap(0, 128, 0, [[1, n]]),
                                   tb[6].ap(0, 128, 0, [[1, n]]), tb[7].ap(0, 128, 0, [[1, n]]))
                    ki = tbi.ap(0, 128, 0, [[1, n]])
                    for off, outp in ((0.0, out_sin), (0.25, out_cos)):
                        P.op("dve", (lambda off=off: V.tensor_scalar(t, ang, 1.0 / TWO_PI, off, ALU.mult, ALU.add)), reads=rd + [tb[4]], writes=[tb[4]])
                        P.op("dve", lambda: V.tensor_copy(ki, t), reads=[tb[4]], writes=[tbi])
                        P.op("dve", lambda: V.tensor_copy(kf, ki), reads=[tbi], writes=[tb[5]])
                        P.op("dve", lambda: V.tensor_tensor(d, t, kf, ALU.subtract), reads=[tb[4], tb[5]], writes=[tb[6]])
                        P.op("dve", lambda: V.tensor_scalar(m, d, 0.5, None, ALU.is_gt), reads=[tb[6]], writes=[tb[7]])
                        P.op("dve", lambda: V.tensor_tensor(d, d, m, ALU.subtract), reads=[tb[6], tb[7]], writes=[tb[6]])
                        P.op("dve", lambda: V.tensor_scalar(m, d, -0.5, None, ALU.is_lt), reads=[tb[6]], writes=[tb[7]])
                        P.op("dve", lambda: V.tensor_tensor(d, d, m, ALU.add), reads=[tb[6], tb[7]], writes=[tb[6]])
                        P.op("act", (lambda outp=outp: A.activation(out=outp, in_=d, func=AF.Sin, scale=TWO_PI * (1.0 - 2e-6))), reads=[tb[6]], writes=wr)

                smb = lambda *ids: [SMB[i] for i in ids]
                P.op("act", lambda: A.activation(out=sm(I_DT), in_=sa(2), func=AF.Exp), reads=[S5A], writes=smb(I_DT))
                P.op("dve", lambda: V.tensor_tensor(sm(I_T0), sm(I_DT), sa(0), ALU.mult), reads=[S5A] + smb(I_DT), writes=smb(I_T0))
                P.op("act", lambda: A.activation(out=sm(I_R), in_=sm(I_T0), func=AF.Exp), reads=smb(I_T0), writes=smb(I_R))
                P.op("dve", lambda: V.tensor_tensor(sm(I_TH), sm(I_DT), sa(1), ALU.mult), reads=[S5A] + smb(I_DT), writes=smb(I_TH))
                sincos(sm(I_TH), G, sm(I_S1), sm(I_C1), smb(I_TH), smb(I_S1, I_C1))
                for mult_, (ci, si) in ((128.0, (I_CA, I_SA)), (127.0, (I_CLP, I_SLP)), (float(TS - 1), (I_CLS, I_SLS))):
                    P.op("dve", (lambda mult_=mult_: V.tensor_scalar(sm(I_T1), sm(I_TH), mult_, None, ALU.mult)), reads=smb(I_TH), writes=smb(I_T1))
                    sincos(sm(I_T1), G, sm(si), sm(ci), smb(I_T1), smb(si, ci))
                P.op("dve", lambda: V.tensor_tensor(sm(I_ABRE), sm(I_R), sm(I_C1), ALU.mult), reads=smb(I_R, I_C1), writes=smb(I_ABRE))
                P.op("dve", lambda: V.tensor_tensor(sm(I_ABIM), sm(I_R), sm(I_S1), ALU.mult), reads=smb(I_R, I_S1), writes=smb(I_ABIM))
                P.op("dve", lambda: V.tensor_scalar(sm(I_NR), sm(I_ABRE), -1.0, None, ALU.add), reads=smb(I_ABRE), writes=smb(I_NR))
                P.op("dve", lambda: V.tensor_tensor(sm(I_T0), sa(0), sa(0), ALU.mult), reads=[S5A], writes=smb(I_T0))
                P.op("dve", lambda: V.tensor_tensor(sm(I_T1), sa(1), sa(1), ALU.mult), reads=[S5A], writes=smb(I_T1))
                P.op("dve", lambda: V.tensor_tensor(sm(I_DEN), sm(I_T0), sm(I_T1), ALU.add), reads=smb(I_T0, I_T1), writes=smb(I_DEN))
                P.op("dve", lambda: V.reciprocal(sm(I_DEN), sm(I_DEN)), reads=smb(I_DEN), writes=smb(I_DEN))
                P.op("dve", lambda: V.tensor_tensor(sm(I_T0), sm(I_NR), sa(0), ALU.mult), reads=[S5A] + smb(I_NR), writes=smb(I_T0))
                P.op("dve", lambda: V.tensor_tensor(sm(I_T1), sm(I_ABIM), sa(1), ALU.mult), reads=[S5A] + smb(I_ABIM), writes=smb(I_T1))
                P.op("dve", lambda: V.tensor_tensor(sm(I_T0), sm(I_T0), sm(I_T1), ALU.add), reads=smb(I_T0, I_T1), writes=smb(I_T0))
                P.op("dve", lambda: V.tensor_tensor(sm(I_FRE), sm(I_T0), sm(I_DEN), ALU.mult), reads=smb(I_T0, I_DEN), writes=smb(I_FRE))
                P.op("dve", lambda: V.tensor_tensor(sm(I_T2), sm(I_ABIM), sa(0), ALU.mult), reads=[S5A] + smb(I_ABIM), writes=smb(I_T2))
                P.op("dve", lambda: V.tensor_tensor(sm(I_T3), sm(I_NR), sa(1), ALU.mult), reads=[S5A] + smb(I_NR), writes=smb(I_T3))
                P.op("dve", lambda: V.tensor_tensor(sm(I_T2), sm(I_T2), sm(I_T3), ALU.subtract), reads=smb(I_T2, I_T3), writes=smb(I_T2))
                P.op("dve", lambda: V.tensor_tensor(sm(I_FIM), sm(I_T2), sm(I_DEN), ALU.mult), reads=smb(I_T2, I_DEN), writes=smb(I_FIM))
                P.op("dve", lambda: V.tensor_scalar(sm(I_NFIM), sm(I_FIM), -1.0, None, ALU.mult), reads=smb(I_FIM), writes=smb(I_NFIM))

                def wload():
                    wi = 0
                    for kc2 in range(2 * NCH):
                        kc, hf = kc2 // 2, kc2 % 2
                        st = stg[wi % 2]
                        P.dma("w%d" % (wi % 2), (lambda st=st, kc=kc, hf=hf: nc.sync.dma_start(out=st[:], in_=w_in[l, kc * 128:(kc + 1) * 128, hf * 1536:(hf + 1) * 1536])), writes=[st])
                        gcol = NGs.ap(0, 128, l * NCH + kc, [[1, 1]])
                        P.op("act", (lambda st=st, kc=kc, hf=hf, gcol=gcol: A.activation(out=Win[:, kc, hf * 1536:(hf + 1) * 1536], in_=st[:], func=AF.Identity, scale=gcol)), reads=[st, NGs], writes=[Win])
                        wi += 1
                        yield
                    for kc in range(4):
                        st = stg[wi % 2]
                        P.dma("w%d" % (wi % 2), (lambda st=st, kc=kc: nc.sync.dma_start(out=st[:, 0:1024], in_=w_glu[l, kc * 128:(kc + 1) * 128, :])), writes=[st])
                        P.op("act", (lambda st=st, kc=kc: A.activation(out=Wglu[:, kc, :], in_=st[:, 0:1024], func=AF.Copy, scale=0.5)), reads=[st], writes=[Wglu])
                        wi += 1
                        yield
                    for kc in range(NCH):
                        st = stg[wi % 2]
                        P.dma("w%d" % (wi % 2), (lambda st=st, kc=kc: nc.sync.dma_start(out=st[:, 0:1024], in_=w_out[l, kc * 128:(kc + 1) * 128, :])), writes=[st])
                        wsc = 0.25 if kc < 4 else 0.5
                        P.op("act", (lambda st=st, kc=kc, wsc=wsc: A.activation(out=Wout[:, kc, :], in_=st[:, 0:1024], func=AF.Copy, scale=wsc)), reads=[st], writes=[Wout])
                        wi += 1
                        yield
                    for src, dst in ((b1h, B1), (b2h, B2), (l1h, L1), (l2h, L2)):
                        st = stg[wi % 2]
                        P.dma("w%d" % (wi % 2), (lambda st=st, src=src: nc.sync.dma_start(out=st[:, 0:1024], in_=src[l])), writes=[st])
                        P.op("pool", (lambda st=st, dst=dst: Pl.tensor_copy(dst.ap(0, 128, 0, [[1, 1024]]), st[:, 0:1024])), reads=[st], writes=[dst])
                        wi += 1
                        yield


                def tabgen():
                    NB = 2
                    for gb in range(G // NB):
                        ang = tb[0].ap(0, 128, 0, [[1, NB * 128]])
                        sinb = tb[1].ap(0, 128, 0, [[1, NB * 128]])
                        cosb = tb[2].ap(0, 128, 0, [[1, NB * 128]])
                        a3 = tb[3].ap(0, 128, 0, [[128, NB], [1, 128]])
                        ang3 = tb[0].ap(0, 128, 0, [[128, NB], [1, 128]])
                        sin3 = tb[1].ap(0, 128, 0, [[128, NB], [1, 128]])
                        cos3 = tb[2].ap(0, 128, 0, [[128, NB], [1, 128]])
                        thb = SM.ap(0, 128, I_TH * G + gb * NB, [[1, NB], [0, 128]])
                        freb = SM.ap(0, 128, I_FRE * G + gb * NB, [[1, NB], [0, 128]])
                        fimb = SM.ap(0, 128, I_FIM * G + gb * NB, [[1, NB], [0, 128]])
                        nfimb = SM.ap(0, 128, I_NFIM * G + gb * NB, [[1, NB], [0, 128]])
                        jb = CST.ap(0, 128, C_JROW, [[0, NB], [1, 128]])
                        dst = lambda Tn: Tn.ap(0, 128, gb * NB * 128, [[128, NB], [1, 128]])
                        P.op("dve", (lambda ang3=ang3, thb=thb, jb=jb: V.tensor_tensor(ang3, thb, jb, ALU.mult)), reads=[CST] + smb(I_TH), writes=[tb[0]])
                        sincos(ang, NB * 128, sinb, cosb, [tb[0]], [tb[1], tb[2]])
                        P.op("dve", (lambda a3=a3, cos3=cos3, freb=freb: V.tensor_tensor(a3, cos3, freb, ALU.mult)), reads=[tb[2]] + smb(I_FRE), writes=[tb[3]])
                        P.op("dve", (lambda ang3=ang3, sin3=sin3, fimb=fimb: V.tensor_tensor(ang3, sin3, fimb, ALU.mult)), reads=[tb[1]] + smb(I_FIM), writes=[tb[0]])
                        P.op("dve", (lambda a3=a3, ang3=ang3, d_=dst(PRE1): V.tensor_tensor(d_, a3, ang3, ALU.add)), reads=[tb[3], tb[0]], writes=[PRE1])
                        P.op("dve", (lambda a3=a3, sin3=sin3, freb=freb: V.tensor_tensor(a3, sin3, freb, ALU.mult)), reads=[tb[1]] + smb(I_FRE), writes=[tb[3]])
                        P.op("dve", (lambda ang3=ang3, cos3=cos3, nfimb=nfimb: V.tensor_tensor(ang3, cos3, nfimb, ALU.mult)), reads=[tb[2]] + smb(I_NFIM), writes=[tb[0]])
                        P.op("dve", (lambda a3=a3, ang3=ang3: V.tensor_tensor(a3, a3, ang3, ALU.add)), reads=[tb[3], tb[0]], writes=[tb[3]])
                        P.op("dve", (lambda a3=a3, d_=dst(PRE2): V.tensor_scalar(d_, a3, cSH, None, ALU.mult)), reads=[tb[3], CST], writes=[PRE2])
                        P.op("dve", (lambda cos3=cos3, d_=dst(POST1): V.tensor_scalar(d_, cos3, cSH, None, ALU.mult)), reads=[tb[2], CST], writes=[POST1])
                        P.op("dve", (lambda sin3=sin3, d_=dst(POST2): V.tensor_scalar(d_, sin3, -1.0, None, ALU.mult)), reads=[tb[1]], writes=[POST2])
                        yield


                wg_, tg_ = wload(), tabgen()
                alive = [wg_, tg_]
                while alive:
                    for g_, n_ in ((tg_, 1), (wg_, 3)):
                        if g_ in alive:
                            for _ in range(n_):
                                try:
                                    next(g_)
                                except StopIteration:
                                    alive.remove(g_)
                                    break
                P.dma("rbr", lambda: nc.sync.dma_start(out=RBR[:], in_=rbrep[:, l, :]), writes=[RBR])
                P.dma("qkr", lambda: nc.sync.dma_start(out=QKR[:], in_=qkrow[:, l, :]), writes=[QKR])
                P.dma("exs", lambda: nc.sync.dma_start(out=EXS[0:8, 0:257], in_=rb[l]), writes=[EXS])
                P.op("dve", lambda: V.tensor_copy(EXS[0:8, 257:384], EXS[0:8, 256:257].to_broadcast([8, 127])), reads=[EXS], writes=[EXS])
                P.dma("exd", lambda: nc.sync.dma_start(out=ext[l], in_=EXS[0:8, :]), reads=[EXS], writes=[HK])
                for i in range(16):
                    h = i % 8
                    off = 1 if i < 8 else 129
                    src = bass.AP(ext.tensor, (l * 8 + h) * 384 + off, [[1, 128], [1, 128]])
                    P.dma("hk", (lambda i=i, src=src: nc.sync.dma_start(out=HK[:, i, :], in_=src)), reads=[HK], writes=[HK])
                P.op("act", lambda: A.activation(out=RBR[:], in_=RBR[:], func=AF.Abs), reads=[RBR], writes=[RBR])
                P.op("dve", lambda: V.reduce_max(col(4), RBR[:], AX.X), reads=[RBR], writes=[COLB])
                P.op("act", lambda: A.activation(out=QKR[:], in_=QKR[:], func=AF.Abs), reads=[QKR], writes=[QKR])
                P.op("dve", lambda: V.reduce_max(col(2), QKR[:, 0:64], AX.X), reads=[QKR], writes=[COLB])
                P.op("dve", lambda: V.reduce_max(col(3), QKR[:, 64:128], AX.X), reads=[QKR], writes=[COLB])
                P.op("dve", lambda: V.tensor_tensor(col(5), col(2), col(3), ALU.mult), reads=[COLB], writes=[COLB])
                P.op("dve", lambda: V.tensor_scalar(col(5), col(5), 8.0, col(4), ALU.mult, ALU.add), reads=[COLB], writes=[COLB])
                P.op("dve", lambda: V.tensor_scalar(col(6), col(5), -1.0, None, ALU.mult), reads=[COLB], writes=[COLB])
                P.dma("rbr", lambda: nc.sync.dma_start(out=RBR[:], in_=rbrep[:, l, :]), reads=[RBR], writes=[RBR])
                P.op("dve", lambda: V.tensor_scalar(COL.ap(0, 128, 8, [[1, 8]]), RBR.ap(0, 128, 256, [[257, 8]]), col(5), None, ALU.subtract), reads=[RBR, COLB], writes=[COLB])
                P.op("dve", lambda: V.tensor_scalar(COL.ap(0, 128, 24, [[1, 8]]), COL.ap(0, 128, 8, [[1, 8]]), CST.ap(0, 128, C_M0 + 64, [[1, 1]]), None, ALU.add), reads=[COLB, CST], writes=[COLB])
                P.op("dve", lambda: V.tensor_scalar(col(0), QKG.ap(0, 128, l * 2, [[1, 1]]), 0.125, None, ALU.mult), reads=[QKG], writes=[COLB])
                P.op("dve", lambda: V.tensor_copy(col(1), QKG.ap(0, 128, l * 2 + 1, [[1, 1]])), reads=[QKG], writes=[COLB])
                for i in range(16):
                    bk = prot.next()
                    P.op("pe", (lambda i=i, bk=bk: PE.matmul(bk[:, 0:128], cJ, HK[:, i, :], start=True, stop=True)), reads=[CST, HK], writes=[bk])
                    if i < 8:
                        P.op("dve", (lambda i=i, bk=bk: V.tensor_tensor(BT[:, i, :], bk[:, 0:128], cM4(128, 128), ALU.add)), reads=[bk, CST], writes=[BT])
                    else:
                        P.op("dve", (lambda i=i, bk=bk: V.tensor_copy(BT[:, i, :], bk[:, 0:128])), reads=[bk], writes=[BT])
                P.phase_end()

            with ExitStack() as s2:
                xT = [sb(s2, "xT%d" % i, [128, NCH, 128], F32) for i in range(2)]
                sq = sb(s2, "sq", [128, NCH, 128], BF16)
                hT = sb(s2, "hT", [128, NCH, 128], BF16)
                uT = [sb(s2, "uT%d" % i, [128, 4, 128], BF16) for i in range(2)]
                sgs = [sb(s2, "sgs%d" % i, [128, 4, 128], F32) for i in range(2)]
                sga = [sb(s2, "sga%d" % i, [128, 4, 128], F32) for i in range(2)]
                qT = [sb(s2, "qT%d" % i, [128, 4, 128], BF16) for i in range(2)]
                qsq = sb(s2, "qsq", [128, 2, 128], BF16)
                rstd = sb(s2, "rstd", [128, 128], F32)
                rq = sb(s2, "rq", [128, 2, 128], F32)
                kn32 = sb(s2, "kn32", [128, 4, 128], F32)
                v32 = sb(s2, "v32", [128, 512], F32)
                t1r = Ring([sb(s2, "t1_%d" % i, [128, 2, 128], F32) for i in range(2)])
                t2r = Ring([sb(s2, "t2_%d" % i, [128, 2, 128], F32) for i in range(2)])
                btr = Ring([sb(s2, "bt_%d" % i, [128, 2, 128], F32) for i in range(2)])
                Gr = Ring([sb(s2, "G_%d" % i, [128, 2, 128], F32) for i in range(2)])
                W1r = Ring([sb(s2, "W1_%d" % i, [128, 2, 128], BF16) for i in range(2)])
                W2r = Ring([sb(s2, "W2_%d" % i, [128, 2, 128], BF16) for i in range(2)])
                ypre = sb(s2, "ypre", [128, 4, 128], F32)
                tt = sb(s2, "tt", [128, 4, 128], F32)
                yg = sb(s2, "yg", [128, 4, 128], BF16)
                sg = sb(s2, "sg", [128, 4, 128], F32)
                mixT = sb(s2, "mixT", [128, NCH, 128], BF16)
                mixb = [Buf("mix%d" % i) for i in range(NCH)]
                stmpr = Ring([sb(s2, "stmp%d" % i, [128, 128], F32) for i in range(4)])
                pTr = Ring([sb(s2, "pT%d" % i, [128, 128], BF16) for i in range(6)])
                rsr = Ring([sb(s2, "rs%d" % i, [128, 128], F32) for i in range(2)])
                hring = Ring([banks[5], banks[6], banks[7]])
                wring = Ring(banks[3:5])
                C_GELU = math.sqrt(2.0 / math.pi)

                def v3(t, T, nchunk, c0=0, n=128, p0=0):
                    return t.ap(p0, n, c0 * 128, [[128, nchunk], [1, T]])

                def pv3(bk, T, nchunk=4, c0=0):
                    return bk.ap(0, 128, c0 * 128, [[128, nchunk], [1, T]])

                def rotate(dst_i, src_ap, src_bufs, ci, si, bk=None):
                    bk = bk if bk is not None else prot.next()
                    P.op("pe", lambda: PE.matmul(bk[:, 0:G], cSWN, src_ap, start=True, stop=True), reads=[CST] + src_bufs, writes=[bk])
                    P.op("dve", lambda: V.tensor_tensor(sm(I_T6), sm(ci), src_ap, ALU.mult), reads=smb(ci) + src_bufs, writes=smb(I_T6))
                    P.op("dve", lambda: V.tensor_tensor(sm(I_T7), sm(si), bk[:, 0:G], ALU.mult), reads=smb(si) + [bk], writes=smb(I_T7))
                    P.op("dve", lambda: V.tensor_tensor(sm(dst_i), sm(I_T6), sm(I_T7), ALU.add), reads=smb(I_T6, I_T7), writes=smb(dst_i))

                def load_x(src_ap, xb, T, bi):
                    P.dma("x%d" % bi, lambda: nc.sync.dma_start(out=v3(xb, T, NCH), in_=src_ap), writes=[xb])

                def rsqrt_act(dst_ap, src_ap, rd, wr):
                    P.op("act", lambda: A.activation(out=dst_ap, in_=src_ap, func=AF.Ln, bias=col(7), scale=1.0), reads=rd + [COLB], writes=wr)
                    P.op("act", lambda: A.activation(out=dst_ap, in_=dst_ap, func=AF.Exp, scale=-0.5), reads=wr, writes=wr)

                def head_stream(xb, T, ti, par, outs, gk):
                    slot = gk % 6
                    uTp, qTp, sgsp, sgap = uT[par], qT[par], sgs[par], sga[par]
                    P.op("act", lambda: A.activation(out=v3(sq, T, NCH), in_=v3(xb, T, NCH), func=AF.Square), reads=[xb], writes=[sq])
                    HB = hring.next()
                    for c in range(NCH):
                        P.op("pe", (lambda c=c, HB=HB: PE.matmul(HB[:, 0:T], ONESB[:], sq[:, c, 0:T], start=(c == 0), stop=(c == NCH - 1))), reads=[ONESB, sq], writes=[HB])
                    rsqrt_act(rstd[:, 0:T], HB[:, 0:T], [HB], [rstd])
                    P.op("dve", lambda: V.tensor_tensor(v3(hT, T, NCH), v3(xb, T, NCH), rstd.ap(0, 128, 0, [[0, NCH], [1, T]]), ALU.mult), reads=[xb, rstd], writes=[hT])
                    yield

                    def win_group(HB, cb, nch=4, c0=0):
                        for c in range(nch):
                            for kc in range(NCH):
                                P.op("pe", (lambda c=c, kc=kc, HB=HB: PE.matmul(HB[:, (c0 + c) * 128:(c0 + c) * 128 + T], Win[:, kc, cb + c * 128:cb + (c + 1) * 128], hT[:, kc, 0:T], start=(kc == 0), stop=(kc == NCH - 1))), reads=[Win, hT], writes=[HB])
                            if c % 2 == 1:
                                yield

                    for hf in range(2):
                        HB = hring.next()
                        yield from win_group(HB, 1024 + hf * 256, nch=2)
                        P.op("act", lambda HB=HB: A.activation(out=v3(qsq, T, 2), in_=pv3(HB, T, 2), func=AF.Square), reads=[HB], writes=[qsq])
                        for c in range(2):
                            P.op("pe", (lambda c=c, HB=HB: PE.matmul(HB[:, (2 + c) * 128:(2 + c) * 128 + T], BLKB[:], qsq[:, c, 0:T], start=True, stop=True)), reads=[BLKB, qsq], writes=[HB])
                        rsqrt_act(v3(rq, T, 2), pv3(HB, T, 2, c0=2), [HB], [rq])
                        P.op("dve", (lambda hf=hf, HB=HB: V.scalar_tensor_tensor(v3(qTp, T, 2, c0=2 * hf), pv3(HB, T, 2), col(0), v3(rq, T, 2), ALU.mult, ALU.mult)), reads=[HB, rq, COLB], writes=[qTp])
                        yield
                    for hf in range(2):
                        HB = hring.next()
                        yield from win_group(HB, 1536 + hf * 256, nch=2)
                        P.op("act", lambda HB=HB: A.activation(out=v3(qsq, T, 2), in_=pv3(HB, T, 2), func=AF.Square), reads=[HB], writes=[qsq])
                        for c in range(2):
                            P.op("pe", (lambda c=c, HB=HB: PE.matmul(HB[:, (2 + c) * 128:(2 + c) * 128 + T], BLKB[:], qsq[:, c, 0:T], start=True, stop=True)), reads=[BLKB, qsq], writes=[HB])
                        rsqrt_act(v3(rq, T, 2), pv3(HB, T, 2, c0=2), [HB], [rq])
                        P.op("dve", (lambda hf=hf, HB=HB: V.scalar_tensor_tensor(v3(kn32, T, 2, c0=2 * hf), pv3(HB, T, 2), col(1), v3(rq, T, 2), ALU.mult, ALU.mult)), reads=[HB, rq, COLB], writes=[kn32])
                        yield
                    P.op("pool", lambda: Pl.tensor_copy(kwin.ap(0, 128, slot * 128, [[768, 4], [1, T]]), v3(kn32, T, 4)), reads=[kn32], writes=[kslot[slot]])
                    if outs.get("nk") is not None:
                        P.dma("nk", lambda: nc.sync.dma_start(out=outs["nk"], in_=v3(kn32, T, 4)), reads=[kn32])
                    HB = hring.next()
                    yield from win_group(HB, 0)
                    P.op("act", lambda HB=HB: A.activation(func=AF.Copy, out=v3(uTp, T, 4), in_=pv3(HB, T)), reads=[HB], writes=[uTp])
                    HB = hring.next()
                    for kc in range(NCH):
                        P.op("pe", (lambda kc=kc, HB=HB: PE.matmul(HB[0:T, :], hT[:, kc, 0:T], Win[:, kc, 2048:2560], start=(kc == 0), stop=(kc == NCH - 1))), reads=[Win, hT], writes=[HB])
                    P.op("act", lambda HB=HB: A.activation(func=AF.Copy, out=vwin[0:T, slot, :], in_=HB[0:T, :]), reads=[HB], writes=[vslot[slot]])
                    if outs.get("nv") is not None:
                        P.op("dve", lambda HB=HB: V.tensor_copy(v32[0:T, :], HB[0:T, :]), reads=[HB], writes=[v32])
                        P.dma("nv", lambda: nc.sync.dma_start(out=outs["nv"], in_=v32[0:T, :]), reads=[v32])
                    yield
                    for cb, dstp in ((512, sgsp), (2560, sgap)):
                        HB = hring.next()
                        yield from win_group(HB, cb)
                        P.op("act", (lambda dstp=dstp, HB=HB: A.activation(out=v3(dstp, T, 4), in_=pv3(HB, T), func=AF.Tanh, scale=0.5)), reads=[HB], writes=[dstp])
                        P.op("dve", (lambda dstp=dstp, HB=HB: V.scalar_tensor_tensor(v3(dstp, T, 4), v3(dstp, T, 4), 1.0, pv3(HB, T), ALU.add, ALU.mult)), reads=[HB, dstp], writes=[dstp])
                        yield

                def s5_stream(T, par, outs):
                    uTp, sgsp = uT[par], sgs[par]

                    def p2(t, off=0):
                        return t.ap(0, 128, off, [[128, 2], [1, T]])

                    def front(pi):
                        g0 = 2 * pi
                        c, band = g0 // 8, (g0 % 8) // 2
                        r0 = 32 * band
                        bkp = pqr.next()
                        for q_, Bm in ((0, B1), (1, B2)):
                            for e in range(2):
                                P.op("pe", (lambda q_=q_, Bm=Bm, e=e: PE.matmul(bkp[:, (2 * q_ + e) * 128:(2 * q_ + e) * 128 + T], Bm.ap(r0, 32, (c * 2 + e) * 128, [[1, 128]]), uTp.ap(r0, 32, c * 128, [[1, T]]), start=True, stop=True, tile_position=(r0, 0))), reads=[Bm, uTp], writes=[bkp])
                        return bkp

                    def stageB(pi, bkp):
                        g0 = 2 * pi
                        t1, t2, bt_ = t1r.next(), t2r.next(), btr.next()
                        P.op("dve", lambda: V.tensor_tensor(p2(t1), p2(bkp), p2(PRE1, g0 * 128), ALU.mult), reads=[bkp, PRE1], writes=[t1])
                        P.op("dve", lambda: V.tensor_tensor(p2(t2), p2(bkp, 256), p2(PRE2, g0 * 128), ALU.mult), reads=[bkp, PRE2], writes=[t2])
                        P.op("pool", lambda: Pl.tensor_tensor(p2(bt_), p2(t1), p2(t2), ALU.add), reads=[t1, t2], writes=[bt_])
                        return bt_

                    def stageC(pi, bt_):
                        g0 = 2 * pi
                        Gt, W1, W2 = Gr.next(), W1r.next(), W2r.next()
                        for e in range(2):
                            g = g0 + e
                            P.op("dve", (lambda e=e, g=g: V.tensor_tensor_scan(Gt[:, e, 0:T], SM.ap(0, 128, I_R * G + g, [[0, T]]), bt_[:, e, 0:T], SM.ap(0, 128, I_INIT * G + g, [[1, 1]]), ALU.mult, ALU.add)), reads=[bt_] + smb(I_R, I_INIT), writes=[Gt])
                        P.op("pool", lambda: Pl.tensor_tensor(p2(W1), p2(Gt), p2(POST1, g0 * 128), ALU.mult), reads=[Gt, POST1], writes=[W1])
                        P.op("pool", lambda: Pl.tensor_tensor(p2(W2), p2(Gt), p2(POST2, g0 * 128), ALU.mult), reads=[Gt, POST2], writes=[W2])
                        P.op("act", lambda: A.activation(func=AF.Copy, out=SM.ap(0, 128, I_GLAST * G + g0, [[1, 2]]), in_=Gt.ap(0, 128, T - 1, [[128, 2]])), reads=[Gt], writes=smb(I_GLAST))
                        return W1, W2

                    def back(pi, W1, W2):
                        g0 = 2 * pi
                        c, band = g0 // 8, (g0 % 8) // 2
                        r0 = 32 * band
                        o_ = ybank.ap(r0, 32, c * 128, [[1, T]])
                        seq = [(L1, W1, 0), (L2, W2, 0), (L1, W1, 1), (L2, W2, 1)]
                        for n_, (Lm, Wm, e) in enumerate(seq):
                            P.op("pe", (lambda n_=n_, Lm=Lm, Wm=Wm, e=e: PE.matmul(o_, Lm[:, g0 + e, :], Wm[:, e, 0:T], start=(n_ == 0), stop=(n_ == 3), tile_position=(0, r0))), reads=[Lm, Wm], writes=[ybank])

                    NPAIR = G // 2
                    sA, sB, sC = {}, {}, {}
                    for i in range(NPAIR + 3):
                        if i < NPAIR:
                            sA[i] = front(i)
                        if 0 <= i - 1 < NPAIR:
                            sB[i - 1] = stageB(i - 1, sA.pop(i - 1))
                        if 0 <= i - 2 < NPAIR:
                            sC[i - 2] = stageC(i - 2, sB.pop(i - 2))
                        if 0 <= i - 3 < NPAIR:
                            back(i - 3, *sC.pop(i - 3))
                        yield

                def tail_stream(xb, bi, T, par, dst_ap, outs):
                    uTp, sgsp = uT[par], sgs[par]
                    for c in range(4):
                        P.op("dve", (lambda c=c: V.scalar_tensor_tensor(ypre[:, c, 0:T], uTp[:, c, 0:T], DCOL.ap(0, 128, l * 4 + c, [[1, 1]]), ybank[:, c * 128:c * 128 + T], ALU.mult, ALU.add)), reads=[uTp, DCOL, ybank], writes=[ypre])
                    P.op("act", lambda: A.activation(out=v3(tt, T, 4), in_=v3(ypre, T, 4), func=AF.Square), reads=[ypre], writes=[tt])
                    P.op("dve", lambda: V.tensor_scalar(v3(tt, T, 4), v3(tt, T, 4), 0.044715, 1.0, ALU.mult, ALU.add), reads=[tt], writes=[tt])
                    P.op("dve", lambda: V.tensor_tensor(v3(tt, T, 4), v3(tt, T, 4), v3(ypre, T, 4), ALU.mult), reads=[tt, ypre], writes=[tt])
                    P.op("act", lambda: A.activation(out=v3(tt, T, 4), in_=v3(tt, T, 4), func=AF.Tanh, scale=C_GELU), reads=[tt], writes=[tt])
                    P.op("dve", lambda: V.scalar_tensor_tensor(v3(yg, T, 4), v3(tt, T, 4), 1.0, v3(ypre, T, 4), ALU.add, ALU.mult), reads=[tt, ypre], writes=[yg])
                    yield
                    if outs.get("st") is not None:
                        ci, si = outs["st_cs"]
                        rotate(I_HL, sm(I_GLAST), smb(I_GLAST), ci, si, GEN)
                        P.dma("st", lambda: nc.sync.dma_start(out=outs["st"], in_=sm(I_HL)), reads=smb(I_HL))
                    else:
                        rotate(I_INIT, sm(I_GLAST), smb(I_GLAST), I_CA, I_SA, GEN)
                    bva, bga = GEN2, ybank
                    for oc in (4, 5, 6, 7, 0, 1, 2, 3):
                        bk_ = bva if oc < 4 else bga
                        for kc in range(4):
                            P.op("pe", (lambda oc=oc, kc=kc, bk_=bk_: PE.matmul(bk_[:, (oc % 4) * 128:(oc % 4) * 128 + T], Wglu[:, kc, oc * 128:(oc + 1) * 128], yg[:, kc, 0:T], start=(kc == 0), stop=(kc == 3))), reads=[Wglu, yg], writes=[bk_])
                        if oc % 4 == 3:
                            yield
                    P.op("act", lambda: A.activation(out=v3(sg, T, 4), in_=pv3(bga, T), func=AF.Tanh, scale=0.5), reads=[bga], writes=[sg])
                    P.op("dve", lambda: V.scalar_tensor_tensor(v3(sg, T, 4), v3(sg, T, 4), 1.0, v3(sgsp, T, 4), ALU.add, ALU.mult), reads=[sg, sgsp], writes=[sg])
                    P.op("dve", lambda: V.tensor_tensor(v3(mixT, T, 4), pv3(bva, T), v3(sg, T, 4), ALU.mult), reads=[bva, sg], writes=mixb[0:4])
                    yield
                    bwa, bwb = wring.next(), wring.next()
                    for oc in range(NCH):
                        bk_ = bwa if oc < 4 else bwb
                        for kc in range(NCH):
                            P.op("pe", (lambda oc=oc, kc=kc, bk_=bk_: PE.matmul(bk_[:, (oc % 4) * 128:(oc % 4) * 128 + T], Wout[:, kc, oc * 128:(oc + 1) * 128], mixT[:, kc, 0:T], start=(kc == 0), stop=(kc == NCH - 1))), reads=[Wout, mixb[kc]], writes=[bk_])
                        if oc % 2 == 1:
                            yield
                    P.op("dve", lambda: V.tensor_tensor(v3(xb, T, 4), v3(xb, T, 4), pv3(bwa, T), ALU.add), reads=[xb, bwa], writes=[xb])
                    P.op("dve", lambda: V.tensor_tensor(v3(xb, T, 4, c0=4), v3(xb, T, 4, c0=4), pv3(bwb, T), ALU.add), reads=[xb, bwb], writes=[xb])
                    P.dma("y%d" % bi, lambda: nc.sync.dma_start(out=dst_ap, in_=v3(xb, T, NCH)), reads=[xb])

                def attn_stream(T, ti, par, gk):
                    qTp, sgap = qT[par], sga[par]
                    jl = [j for j in range(5) if ti - 4 + j >= 0]
                    items = [(h, j) for h in range(8) for j in jl]

                    def qk(h, j):
                        hp, hh = h // 2, h % 2
                        sl = (gk - 4 + j) % 6
                        nk = T if j == 4 else 128
                        bk_ = scr.next()
                        P.op("pe", lambda: PE.matmul(bk_[0:nk, 0:T], kwin.ap(64 * hh, 64, (hp * 6 + sl) * 128, [[1, nk]]), qTp.ap(64 * hh, 64, hp * 128, [[1, T]]), start=True, stop=True), reads=[kslot[sl], qTp], writes=[bk_])
                        return bk_, nk, sl

                    def soft(h, j, bk_, nk):
                        pT = pTr.next()
                        if j in (1, 2) or (j == 0 and T <= 64):
                            P.op("act", lambda: A.activation(out=pT[0:nk, 0:T], in_=bk_[0:nk, 0:T], func=AF.Exp, bias=col(8 + h, nk), scale=1.0), reads=[bk_, COLB], writes=[pT])
                        elif j == 0:
                            P.op("act", lambda: A.activation(out=pT[0:nk, 0:64], in_=bk_[0:nk, 0:64], func=AF.Exp, bias=col(8 + h, nk), scale=1.0), reads=[bk_, COLB], writes=[pT])
                            P.op("act", lambda: A.activation(out=pT[0:nk, 64:T], in_=bk_[0:nk, 64:T], func=AF.Exp, bias=col(24 + h, nk), scale=1.0), reads=[bk_, COLB], writes=[pT])
                        else:
                            stmp = stmpr.next()
                            if j == 0:
                                badd, bias_, rd = cM0(nk, T), col(8 + h, nk), [CST]
                            elif j == 3:
                                badd, bias_, rd = BT[0:nk, 8 + h, 0:T], col(6, nk), [BT]
                            else:
                                badd, bias_, rd = BT[0:nk, h, 0:T], col(6, nk), [BT]
                            P.op("dve", lambda: V.tensor_tensor(stmp[0:nk, 0:T], bk_[0:nk, 0:T], badd, ALU.add), reads=[bk_] + rd, writes=[stmp])
                            P.op("act", lambda: A.activation(out=pT[0:nk, 0:T], in_=stmp[0:nk, 0:T], func=AF.Exp, bias=bias_, scale=1.0), reads=[stmp, COLB], writes=[pT])
                        return pT

                    def pvmm_pair(hp, j, pts, cur):
                        first, last = (j == jl[0]), (j == jl[-1])
                        osb = OS[hp % 2]
                        for hh in range(2):
                            h = 2 * hp + hh
                            pT, (bk_, nk, sl) = pts[hh], cur[hh]
                            o_ = osb.ap(64 * hh, 64, 0, [[1, T]])
                            P.op("pe", (lambda h=h, hh=hh, pT=pT, nk=nk, sl=sl, o_=o_: PE.matmul(o_, vwin[0:nk, sl, h * 64:(h + 1) * 64], pT[0:nk, 0:T], start=first, stop=last, tile_position=(0, 64 * hh), skip_group_check=True)), reads=[vslot[sl], pT], writes=[osb])
                        for hh in range(2):
                            pT, (bk_, nk, sl) = pts[hh], cur[hh]
                            s_ = osb.ap(64 * hh, 64, 128, [[1, T]])
                            P.op("pe", (lambda hh=hh, pT=pT, nk=nk, s_=s_: PE.matmul(s_, ONE64B[0:nk, :], pT[0:nk, 0:T], start=False, stop=last, tile_position=(0, 64 * hh), skip_group_check=True)), reads=[ONE64B, pT], writes=[osb])
                        if last:
                            rs = rsr.next()
                            P.op("dve", lambda: V.reciprocal(rs[:, 0:T], osb[:, 128:128 + T]), reads=[osb], writes=[rs])
                            P.op("pool", lambda: Pl.tensor_tensor(rs[:, 0:T], rs[:, 0:T], sgap[:, hp, 0:T], ALU.mult), reads=[rs, sgap], writes=[rs])
                            P.op("dve", lambda: V.tensor_tensor(mixT[:, 4 + hp, 0:T], osb[:, 0:T], rs[:, 0:T], ALU.mult), reads=[osb, rs], writes=[mixb[4 + hp]])

                    pitems = [(hp, j) for hp in range(4) for j in jl]

                    def qk_pair(hp, j):
                        return [qk(2 * hp + hh, j) for hh in range(2)]

                    q = [qk_pair(*pitems[0])]
                    for idx in range(len(pitems)):
                        if idx + 1 < len(pitems):
                            q.append(qk_pair(*pitems[idx + 1]))
                        hp, j = pitems[idx]
                        cur = q.pop(0)
                        pts = [soft(2 * hp + hh, j, cur[hh][0], cur[hh][1]) for hh in range(2)]
                        pvmm_pair(hp, j, pts, cur)
                        yield

                def run_streams(streams):
                    streams = list(streams)
                    while streams:
                        for s_ in list(streams):
                            try:
                                next(s_)
                            except StopIteration:
                                streams.remove(s_)

                def body(xb, bi, T, ti, par, dst_ap, outs, nxt=None, gk=4):
                    run_streams([s5_stream(T, par, outs), attn_stream(T, ti, par, gk)])
                    streams = [tail_stream(xb, bi, T, par, dst_ap, outs)]
                    if nxt is not None:
                        streams.insert(0, nxt)
                    run_streams(streams)

                P.op("dve", lambda: V.memset(col(7), EPS), writes=[COLB])
                srcp, dstp = (xp, y1p) if l == 0 else (y1p, yp)
                srcs, dsts = (xs, y1s) if l == 0 else (y1s, ys)

                def p_outs(s, i):
                    outs = {}
                    if i >= NT - 4:
                        outs["nk"] = nkp[l, s, i - (NT - 4)]
                        outs["nv"] = nvp[l, s, (i - (NT - 4)) * 128:(i - (NT - 4) + 1) * 128, :]
                    if i == NT - 1:
                        outs["st"] = stp[l, s]
                        outs["st_cs"] = (I_CLP, I_SLP)
                    return outs

                tiles = [(s, i) for s in range(NSP) for i in range(NT)]
                load_x(srcp[0, 0], xT[0], 128, 0)
                run_streams([head_stream(xT[0], 128, 0, 0, p_outs(0, 0), 0)])
                for k, (s, i) in enumerate(tiles):
                    bi = k % 2
                    if i == 0:
                        P.op("dve", lambda: V.memset(sm(I_INIT), 0.0), writes=smb(I_INIT))
                    nxt = None
                    if k + 1 < len(tiles):
                        s2_, i2_ = tiles[k + 1]
                        load_x(srcp[s2_, i2_], xT[1 - bi], 128, 1 - bi)
                        nxt = head_stream(xT[1 - bi], 128, i2_, 1 - bi, p_outs(s2_, i2_), k + 1)
                    body(xT[bi], bi, 128, i, bi, dstp[s, i], p_outs(s, i), nxt, k)
                for s in range(NSS):
                    bi = s % 2
                    cstg = xT[1 - bi]
                    load_x(srcs[s], xT[bi], TS, bi)
                    for hf in range(2):
                        P.dma("cstg", (lambda s=s, hf=hf, cstg=cstg: nc.sync.dma_start(out=cstg.ap(0, 128, 0, [[512, 2], [1, 512]]), in_=ck[l, s, :, 2 * hf:2 * hf + 2, :])), writes=[cstg])
                        P.op("pool", (lambda hf=hf, cstg=cstg: Pl.tensor_copy(kwin.ap(0, 128, 2 * hf * 768, [[768, 2], [128, 4], [1, 128]]), cstg.ap(0, 128, 0, [[512, 2], [128, 4], [1, 128]]))), reads=[cstg], writes=kslot[0:4])
                    for hf in range(2):
                        P.dma("cstg", (lambda s=s, hf=hf, cstg=cstg: nc.sync.dma_start(out=cstg.ap(0, 128, 0, [[512, 2], [1, 512]]), in_=cv[l, s, hf * 256:(hf + 1) * 256, :].rearrange("(j k) e -> k j e", k=128))), writes=[cstg])
                        P.op("act", (lambda hf=hf, cstg=cstg: A.activation(func=AF.Copy, out=vwin[:, 2 * hf:2 * hf + 2, :], in_=cstg.ap(0, 128, 0, [[512, 2], [1, 512]]))), reads=[cstg], writes=vslot[0:4])
                    P.dma("st0", (lambda s=s: nc.sync.dma_start(out=sm(I_ST0), in_=st0[l, s])), writes=smb(I_ST0))
                    rotate(I_INIT, sm(I_ST0), smb(I_ST0), I_C1, I_S1)
                    outs = {"nk": nks[l, s], "nv": nvs[l, s], "st": sts[l, s], "st_cs": (I_CLS, I_SLS)}
                    run_streams([head_stream(xT[bi], TS, 4, bi, outs, 4)])
                    body(xT[bi], bi, TS, 4, bi, dsts[s], outs, None, 4)
                P.phase_end()
        print("[kernel] instructions:", P.n_inst, "semaphores:", P.nsem)
    return nc


def _consts():
    c = np.zeros((128, NCST), np.float32)
    idx = np.arange(128)
    c[idx, C_J + 127 - idx] = 1.0
    p = np.arange(64)
    c[64 + p, C_SWN + p] = -1.0
    c[p, C_SWN + 64 + p] = 1.0
    c[:, C_ONES:C_ONES + 128] = 1.0 / 1024.0
    c[0:64, C_BLK:C_BLK + 64] = 1.0 / 64.0
    c[64:128, C_BLK + 64:C_BLK + 128] = 1.0 / 64.0
    c[:, C_ONE64:C_ONE64 + 64] = 1.0
    c[64:128, C_M4:C_M4 + 64] = NEG
    c[0:64, C_M0 + 64:C_M0 + 128] = NEG
    c[:, C_JROW:C_JROW + 128] = np.arange(128, dtype=np.float32)[None]
    c[0:64, C_SH] = 1.0
    c[64:128, C_SH] = -1.0
    return c


_PROG_CACHE = {}


def kernel(x_prompt, x_sample, cache_k, cache_v, state_ssm_re, state_ssm_im, norm_gain, w_in,
           ssm_a_re, ssm_a_im, ssm_b_re, ssm_b_im, ssm_c_re, ssm_c_im, ssm_d, ssm_log_dt,
           w_glu, q_norm_gain, k_norm_gain, rel_bias, w_out, n_cores=8):
    f = np.float32
    x_prompt = np.asarray(x_prompt, f)
    x_sample = np.asarray(x_sample, f)
    B, SEQ, _ = x_prompt.shape
    BS = x_sample.shape[0]
    NSP, NSS = B // n_cores, BS // n_cores
    NT = SEQ // 128
    R = cache_k.shape[2]
    assert R == 512 and SEQ >= 512 and SEQ % 128 == 0 and x_sample.shape[1] == TS

    xpb = x_prompt.reshape(B, NT, 128, NCH, 128).transpose(0, 1, 4, 3, 2)
    xsb = x_sample.reshape(BS, TS, NCH, 128).transpose(0, 3, 2, 1)
    ckb = np.asarray(cache_k, f).reshape(L, BS, R, 4, 2, 64).transpose(0, 1, 4, 5, 3, 2).reshape(L, BS, 128, 4, R)
    cvb = np.asarray(cache_v, f).reshape(L, BS, R, 512)
    st = np.concatenate([np.asarray(state_ssm_re, f).transpose(0, 1, 3, 2), np.asarray(state_ssm_im, f).transpose(0, 1, 3, 2)], axis=2)
    ngh = np.asarray(norm_gain, f).reshape(L, NCH, 128).transpose(2, 0, 1)
    are = np.asarray(ssm_a_re, f).transpose(2, 0, 1)
    aim = np.asarray(ssm_a_im, f).transpose(2, 0, 1)
    ldt = np.broadcast_to(np.asarray(ssm_log_dt, f)[None], (64, L, G))
    s5 = np.stack([are, aim, ldt], axis=2)
    s5 = np.concatenate([s5, s5], axis=0)
    bre = np.asarray(ssm_b_re, f)
    bim = np.asarray(ssm_b_im, f)
    b1 = np.zeros((L, 128, 8, 128), f)
    b2 = np.zeros((L, 128, 8, 128), f)
    cre = np.asarray(ssm_c_re, f)
    cim = np.asarray(ssm_c_im, f)
    l1 = np.zeros((L, 128, G, 32), f)
    l2 = np.zeros((L, 128, G, 32), f)
    for g in range(G):
        c, band, e = g // 8, (g % 8) // 2, g % 2
        r0 = 32 * band + 16 * e
        b1[:, r0:r0 + 16, c * 2 + e, 0:64] = bre[:, g].transpose(0, 2, 1)
        b1[:, r0:r0 + 16, c * 2 + e, 64:128] = bim[:, g].transpose(0, 2, 1)
        b2[:, r0:r0 + 16, c * 2 + e, 0:64] = bim[:, g].transpose(0, 2, 1)
        b2[:, r0:r0 + 16, c * 2 + e, 64:128] = bre[:, g].transpose(0, 2, 1)
        l1[:, 0:64, g, 16 * e:16 * e + 16] = cre[:, g].transpose(0, 2, 1)
        l1[:, 64:128, g, 16 * e:16 * e + 16] = cim[:, g].transpose(0, 2, 1)
        l2[:, 0:64, g, 16 * e:16 * e + 16] = cim[:, g].transpose(0, 2, 1)
        l2[:, 64:128, g, 16 * e:16 * e + 16] = cre[:, g].transpose(0, 2, 1)
    dch = np.asarray(ssm_d, f).reshape(L, 4, 128).transpose(2, 0, 1)
    qg = np.asarray(q_norm_gain, f)
    kg = np.asarray(k_norm_gain, f)
    qkgh = np.stack([np.concatenate([qg, qg], 1), np.concatenate([kg, kg], 1)], axis=2).transpose(1, 0, 2)
    qkr = np.broadcast_to(np.concatenate([qg, kg], 1)[None], (128, L, 128))
    rbh = np.asarray(rel_bias, f)
    rbr = np.broadcast_to(rbh.reshape(1, L, 8 * 257), (128, L, 8 * 257))
    cst = _consts()

    key = (NT, NSP, NSS)
    nc = build_program(NT, NSP, NSS)
    shared = dict(w_in=np.ascontiguousarray(w_in, f), w_glu=np.ascontiguousarray(w_glu, f), w_out=np.ascontiguousarray(w_out, f),
                  ng=np.ascontiguousarray(ngh), s5a=np.ascontiguousarray(s5), b1h=b1.reshape(L, 128, 1024), b2h=b2.reshape(L, 128, 1024),
                  l1h=l1.reshape(L, 128, 1024), l2h=l2.reshape(L, 128, 1024), dcol=np.ascontiguousarray(dch),
                  qkg=np.ascontiguousarray(qkgh), qkrow=np.ascontiguousarray(qkr), rb=np.ascontiguousarray(rbh),
                  rbrep=np.ascontiguousarray(rbr), cst=cst)
    in_maps = []
    for c in range(n_cores):
        m = dict(shared)
        m["xp"] = np.ascontiguousarray(xpb[c * NSP:(c + 1) * NSP])
        m["xs"] = np.ascontiguousarray(xsb[c * NSS:(c + 1) * NSS])
        m["ck"] = np.ascontiguousarray(ckb[:, c * NSS:(c + 1) * NSS])
        m["cv"] = np.ascontiguousarray(cvb[:, c * NSS:(c + 1) * NSS])
        m["st0"] = np.ascontiguousarray(st[:, c * NSS:(c + 1) * NSS])
        in_maps.append(m)
    res = run_bass_kernel_spmd(nc, in_maps, core_ids=list(range(n_cores)))
    rs = res.results

    def cat(name, axis):
        return np.concatenate([np.asarray(r[name]) for r in rs], axis=axis)

    ypo = cat("yp", 0).transpose(0, 1, 4, 3, 2).reshape(B, SEQ, D)
    yso = cat("ys", 0).transpose(0, 3, 2, 1).reshape(BS, TS, D)
    nkpo = cat("nkp", 1)
    nkpo = nkpo.reshape(L, B, 4, 2, 64, 4, 128).transpose(0, 1, 2, 6, 5, 3, 4).reshape(L, B, 512, 8, 64)
    nvpo = cat("nvp", 1).reshape(L, B, 512, 8, 64)
    stpo = cat("stp", 1)
    pr = stpo[:, :, 0:64].transpose(0, 1, 3, 2)
    pi = stpo[:, :, 64:128].transpose(0, 1, 3, 2)
    nkso = cat("nks", 1).reshape(L, BS, 2, 64, 4, TS).transpose(0, 1, 5, 4, 2, 3).reshape(L, BS, TS, 8, 64)
    nvso = cat("nvs", 1).reshape(L, BS, TS, 8, 64)
    stso = cat("sts", 1)
    sr = stso[:, :, 0:64].transpose(0, 1, 3, 2)
    si = stso[:, :, 64:128].transpose(0, 1, 3, 2)
    c_ = lambda a: np.ascontiguousarray(a, dtype=np.float32)
    return (c_(ypo), c_(yso), c_(nkpo), c_(nvpo), c_(pr), c_(pi), c_(nkso), c_(nvso), c_(sr), c_(si))
```
